# Optimizing a Trainium2 kernel written in Bass

```python
import math
import jax, jax.numpy as jnp
from jax import lax
import numpy as np

D_MODEL = 1024
BATCH = 16
SEQ = 2048
DEPTH = 1

D_MIX = D_MODEL
ATT_WIDTH = D_MIX // 2
SSM_WIDTH = D_MIX - ATT_WIDTH
ATT_HEAD_DIM = 64
ATT_V_DIM = 2 * ATT_HEAD_DIM
ATT_HEADS = ATT_WIDTH // ATT_V_DIM
Q_BLOCK = 128
SSM_GROUP = 16
SSM_GROUPS = SSM_WIDTH // SSM_GROUP
SSM_STATE = 64
DT_MIN = 1e-3
DT_MAX = 1e-1
D_FF = 2816
D_IN = 3 * ATT_WIDTH + SSM_WIDTH
EPS = 1e-6

kernel_name = "hybrid_diffattn_s5_macaron_layer"


def rms_norm(x, g):
    xf = x.astype(jnp.float32)
    y = xf * lax.rsqrt(jnp.mean(xf * xf, axis=-1, keepdims=True) + EPS)
    return (y * g.astype(jnp.float32)).astype(x.dtype)


def swiglu(x, w_gate, w_up, w_down):
    return (jax.nn.silu(x @ w_gate) * (x @ w_up)) @ w_down


def lambda_init_fn(layer):
    return 0.8 - 0.6 * math.exp(-0.3 * layer)


def diff_attention(q, k, v, lam, sub_g, lam_init):
    bsz, s = q.shape[0], q.shape[1]
    nblk = s // Q_BLOCK
    scale = ATT_HEAD_DIM ** -0.5
    qb = q.reshape(bsz, nblk, Q_BLOCK, ATT_HEADS, 2, ATT_HEAD_DIM).swapaxes(0, 1)
    k_pos = jnp.arange(s)

    def block(args):
        qi, i = args
        sc = jnp.einsum('bqhcd,bkhcd->bhcqk', qi, k).astype(jnp.float32) * scale
        q_pos = i * Q_BLOCK + jnp.arange(Q_BLOCK)
        causal = k_pos[None, :] <= q_pos[:, None]
        sc = jnp.where(causal, sc, -jnp.inf)
        p = jax.nn.softmax(sc, axis=-1)
        a = p[:, :, 0] - lam * p[:, :, 1]
        return jnp.einsum('bhqk,bkhv->bqhv', a.astype(v.dtype), v)

    o = lax.map(block, (qb, jnp.arange(nblk)))
    o = o.swapaxes(0, 1).reshape(bsz, s, ATT_HEADS, ATT_V_DIM)
    o = rms_norm(o, sub_g) * (1.0 - lam_init)
    return o.reshape(bsz, s, ATT_WIDTH)


def s5_mixer(u, a_re, a_im, log_dt, b_re, b_im, c_re, c_im, d_skip, w_glu, b_glu, norm_g):
    bsz, s = u.shape[0], u.shape[1]
    uf = u.astype(jnp.float32).reshape(bsz, s, SSM_GROUPS, SSM_GROUP)
    lam = lax.complex(a_re.astype(jnp.float32), a_im.astype(jnp.float32))
    dt = jnp.exp(log_dt.astype(jnp.float32))[:, None]
    a_bar = jnp.exp(lam * dt)
    b_c = lax.complex(b_re.astype(jnp.float32), b_im.astype(jnp.float32))
    b_bar = ((a_bar - 1.0) / lam)[..., None] * b_c
    bu = jnp.einsum('gph,bsgh->bsgp', b_bar, uf.astype(jnp.complex64))
    a_seq = jnp.broadcast_to(a_bar, (s, SSM_GROUPS, SSM_STATE))

    def combine(e1, e2):
        a1, b1 = e1
        a2, b2 = e2
        return a1 * a2, a2 * b1 + b2

    def scan_one(b_one):
        return lax.associative_scan(combine, (a_seq, b_one), axis=0)[1]

    states = jax.vmap(scan_one)(bu)
    y = (jnp.einsum('ghp,bsgp->bsgh', c_re.astype(jnp.float32), states.real)
         - jnp.einsum('ghp,bsgp->bsgh', c_im.astype(jnp.float32), states.imag))
    y = y + d_skip.astype(jnp.float32).reshape(SSM_GROUPS, SSM_GROUP) * uf
    y = y.reshape(bsz, s, SSM_WIDTH).astype(u.dtype)
    g = jax.nn.gelu(y)
    out = g * jax.nn.sigmoid(g @ w_glu + b_glu)
    return rms_norm(out, norm_g)


def setup_inputs(seed: int = 0) -> dict:
    key = jax.random.key(seed)
    ks = jax.random.split(key, 32)
    f32 = jnp.float32

    def nrm(k, shape, scale):
        return jax.random.normal(k, shape, f32) * scale

    def gain(k, n):
        return 1.0 + 0.05 * jax.random.normal(k, (DEPTH, n), f32)

    L, G, P, H = DEPTH, SSM_GROUPS, SSM_STATE, SSM_GROUP
    return {
        "x": jax.random.normal(ks[0], (BATCH, SEQ, D_MODEL), f32),
        "ff1_pre_g": gain(ks[1], D_MODEL),
        "ff1_w_gate": nrm(ks[2], (L, D_MODEL, D_FF), D_MODEL ** -0.5),
        "ff1_w_up": nrm(ks[3], (L, D_MODEL, D_FF), D_MODEL ** -0.5),
        "ff1_w_down": nrm(ks[4], (L, D_FF, D_MODEL), D_FF ** -0.5),
        "ff1_post_g": gain(ks[5], D_MODEL),
        "mix_pre_g": gain(ks[6], D_MODEL),
        "w_in": nrm(ks[7], (L, D_MODEL, D_IN), D_MODEL ** -0.5),
        "lambda_q1": nrm(ks[8], (L, ATT_HEAD_DIM), 0.1),
        "lambda_k1": nrm(ks[9], (L, ATT_HEAD_DIM), 0.1),
        "lambda_q2": nrm(ks[10], (L, ATT_HEAD_DIM), 0.1),
        "lambda_k2": nrm(ks[11], (L, ATT_HEAD_DIM), 0.1),
        "attn_subln_g": gain(ks[12], ATT_V_DIM),
        "ssm_a_re": -0.5 + 0.01 * jax.random.normal(ks[13], (L, G, P), f32),
        "ssm_a_im": jnp.pi * jnp.arange(P, dtype=f32)[None, None, :] + 0.01 * jax.random.normal(ks[14], (L, G, P), f32),
        "ssm_log_dt": jax.random.uniform(ks[15], (L, G), f32, math.log(DT_MIN), math.log(DT_MAX)),
        "ssm_b_re": nrm(ks[16], (L, G, P, H), (2.0 * H) ** -0.5),
        "ssm_b_im": nrm(ks[17], (L, G, P, H), (2.0 * H) ** -0.5),
        "ssm_c_re": nrm(ks[18], (L, G, H, P), P ** -0.5),
        "ssm_c_im": nrm(ks[19], (L, G, H, P), P ** -0.5),
        "ssm_d": nrm(ks[20], (L, SSM_WIDTH), 1.0),
        "ssm_w_glu": nrm(ks[21], (L, SSM_WIDTH, SSM_WIDTH), SSM_WIDTH ** -0.5),
        "ssm_b_glu": nrm(ks[22], (L, SSM_WIDTH), 0.01),
        "ssm_norm_g": gain(ks[23], SSM_WIDTH),
        "w_out": nrm(ks[24], (L, D_MIX, D_MODEL), D_MIX ** -0.5),
        "mix_post_g": gain(ks[25], D_MODEL),
        "ff2_pre_g": gain(ks[26], D_MODEL),
        "ff2_w_gate": nrm(ks[27], (L, D_MODEL, D_FF), D_MODEL ** -0.5),
        "ff2_w_up": nrm(ks[28], (L, D_MODEL, D_FF), D_MODEL ** -0.5),
        "ff2_w_down": nrm(ks[29], (L, D_FF, D_MODEL), D_FF ** -0.5),
        "ff2_post_g": gain(ks[30], D_MODEL),
    }


def reference(x, ff1_pre_g, ff1_w_gate, ff1_w_up, ff1_w_down, ff1_post_g,
              mix_pre_g, w_in, lambda_q1, lambda_k1, lambda_q2, lambda_k2, attn_subln_g,
              ssm_a_re, ssm_a_im, ssm_log_dt, ssm_b_re, ssm_b_im, ssm_c_re, ssm_c_im,
              ssm_d, ssm_w_glu, ssm_b_glu, ssm_norm_g, w_out, mix_post_g,
              ff2_pre_g, ff2_w_gate, ff2_w_up, ff2_w_down, ff2_post_g):
    bsz, s = x.shape[0], x.shape[1]
    for l in range(DEPTH):
        h = swiglu(rms_norm(x, ff1_pre_g[l]), ff1_w_gate[l], ff1_w_up[l], ff1_w_down[l])
        x = x + 0.5 * rms_norm(h, ff1_post_g[l])

        hn = rms_norm(x, mix_pre_g[l])
        proj = hn @ w_in[l]
        q = proj[..., :ATT_WIDTH].reshape(bsz, s, ATT_HEADS, 2, ATT_HEAD_DIM)
        k = proj[..., ATT_WIDTH:2 * ATT_WIDTH].reshape(bsz, s, ATT_HEADS, 2, ATT_HEAD_DIM)
        v = proj[..., 2 * ATT_WIDTH:3 * ATT_WIDTH].reshape(bsz, s, ATT_HEADS, ATT_V_DIM)
        u = proj[..., 3 * ATT_WIDTH:]

        lam_init = lambda_init_fn(l)
        lam = (jnp.exp(jnp.sum(lambda_q1[l].astype(jnp.float32) * lambda_k1[l].astype(jnp.float32)))
               - jnp.exp(jnp.sum(lambda_q2[l].astype(jnp.float32) * lambda_k2[l].astype(jnp.float32)))
               + lam_init)
        o_att = diff_attention(q, k, v, lam, attn_subln_g[l], lam_init)
        o_ssm = s5_mixer(u, ssm_a_re[l], ssm_a_im[l], ssm_log_dt[l], ssm_b_re[l], ssm_b_im[l],
                         ssm_c_re[l], ssm_c_im[l], ssm_d[l], ssm_w_glu[l], ssm_b_glu[l],
                         ssm_norm_g[l])
        mixed = jnp.concatenate([o_att, o_ssm], axis=-1) @ w_out[l]
        x = x + rms_norm(mixed, mix_post_g[l])

        h = swiglu(rms_norm(x, ff2_pre_g[l]), ff2_w_gate[l], ff2_w_up[l], ff2_w_down[l])
        x = x + 0.5 * rms_norm(h, ff2_post_g[l])
    return x
```

```python
import math
import os
from contextlib import ExitStack

import numpy as np
import concourse.bass as bass
import concourse.mybir as mybir
from concourse.bass_utils import run_bass_kernel_spmd

F32 = mybir.dt.float32
BF16 = mybir.dt.bfloat16
AF = mybir.ActivationFunctionType
ALU = mybir.AluOpType

D = 1024
DFF = 2816
NF = DFF // 128
T = 2048
NSEQ = 2
TT = 512
NTT = T // TT
EPS = 1e-6
NCORES = 8

ENGS = ("pe", "act", "dve", "pool", "sp")


class Op:
    __slots__ = ("eng", "fn", "deps", "is_dma", "sig", "sem", "val", "idx", "pre")

    def __init__(self, eng, fn, is_dma):
        self.eng = eng
        self.fn = fn
        self.is_dma = is_dma
        self.deps = set()
        self.sig = False
        self.sem = None
        self.val = 0
        self.pre = None


class Prog:
    def __init__(self, G):
        self.G = G
        self.ops = []
        self.last_w = {}
        self.readers = {}

    def add(self, eng, fn, reads=(), writes=(), dma=False):
        op = Op(eng, fn, dma)
        op.idx = len(self.ops)
        for r in reads:
            w = self.last_w.get(r)
            if w is not None:
                op.deps.add(w)
        for r in writes:
            w = self.last_w.get(r)
            if w is not None:
                op.deps.add(w)
            for q in self.readers.get(r, ()):
                op.deps.add(q)
        for r in reads:
            self.readers.setdefault(r, []).append(op.idx)
        for r in writes:
            self.last_w[r] = op.idx
            self.readers[r] = []
        op.deps.discard(op.idx)
        self.ops.append(op)
        return op.idx

    def emit(self, block):
        ops = self.ops
        G = self.G
        sems, dmasems = G.sems, G.dmasems

        def skip(dop, op):
            return (dop.eng == "pe" and op.eng == "pe" and not dop.is_dma and not op.is_dma)

        for op in ops:
            best = {}
            keep = set()
            for d in op.deps:
                dop = ops[d]
                if skip(dop, op):
                    continue
                if dop.is_dma:
                    keep.add(d)
                elif best.get(dop.eng, -1) < d:
                    best[dop.eng] = d
            keep.update(best.values())
            op.deps = keep
            for d in keep:
                ops[d].sig = True
        last_compute = {}
        for op in ops:
            if not op.is_dma and op.eng in ("pe", "act", "dve", "pool"):
                last_compute[op.eng] = op
        for op in last_compute.values():
            op.sig = True
        cnt, dcnt = G.cnt, G.dcnt
        for op in ops:
            if op.is_dma:
                i = dcnt[op.eng]
                dcnt[op.eng] += 1
                ring = dmasems[op.eng]
                op.sem = ring[i % len(ring)]
                op.val = 16 * (i // len(ring) + 1)
                op.pre = (op.sem, op.val - 16)
            elif op.sig:
                cnt[op.eng] += 1
                op.sem = sems[op.eng]
                op.val = cnt[op.eng]
        per_eng = {e: [o for o in ops if o.eng == e] for e in ENGS}
        if os.environ.get("KVERB"):
            print("sem counts", cnt, "dma counts", dcnt, "nops", len(ops))

        def run(engname, eng):
            waited = G.waited[engname]
            for op in per_eng[engname]:
                need = {}
                for d in op.deps:
                    dop = ops[d]
                    if dop.sem is None:
                        continue
                    k = id(dop.sem)
                    if need.get(k, (None, 0))[1] < dop.val:
                        need[k] = (dop.sem, dop.val)
                if op.pre is not None and op.pre[1] > 0:
                    k = id(op.pre[0])
                    if need.get(k, (None, 0))[1] < op.pre[1]:
                        need[k] = op.pre
                for k, (s, v) in need.items():
                    if waited.get(k, 0) < v:
                        eng.wait_ge(s, v)
                        waited[k] = v
                inst = op.fn(eng)
                if op.is_dma:
                    inst.then_inc(op.sem, 16)
                elif op.sig:
                    inst.then_inc(op.sem, 1)
            last = {}
            for op in per_eng[engname]:
                if op.is_dma:
                    last[id(op.sem)] = (op.sem, op.val)
            for k, (s, v) in last.items():
                if waited.get(k, 0) < v:
                    eng.wait_ge(s, v)
                    waited[k] = v
            lc = last_compute.get(engname)
            if lc is not None and waited.get(id(lc.sem), 0) < lc.val:
                eng.wait_ge(lc.sem, lc.val)
                waited[id(lc.sem)] = lc.val

        @block.tensor
        def _(e):
            run("pe", e)

        @block.scalar
        def _(e):
            run("act", e)

        @block.vector
        def _(e):
            run("dve", e)

        @block.gpsimd
        def _(e):
            run("pool", e)

        @block.sync
        def _(e):
            run("sp", e)


class Globals:
    def __init__(self, nc, es):
        self.sems = {e: es.enter_context(nc.semaphore(f"s_{e}")) for e in ENGS}
        ring = {"sp": 24, "pool": 8, "act": 4, "pe": 1, "dve": 1}
        self.dmasems = {e: [es.enter_context(nc.semaphore(f"d_{e}{i}")) for i in range(ring[e])]
                        for e in ENGS}
        self.cnt = {e: 0 for e in ENGS}
        self.dcnt = {e: 0 for e in ENGS}
        self.waited = {e: {} for e in ENGS}
        self.uid = 0
        self.base = (nc.sbuf_base + 31) // 32 * 32
        self.arena = es.enter_context(nc.sbuf_tensor("arena", [128, SB_LIMIT // 4], F32))


SB_LIMIT = 207 * 1024
DT_SIZE = {F32: 4, BF16: 2}


class Phase:
    def __init__(self, nc, G, name):
        self.nc = nc
        self.G = G
        self.name = name
        self.es = ExitStack()
        self.P = Prog(G)
        self.cur = 0

    def __enter__(self):
        self.es.__enter__()
        return self

    def sb(self, name, shape, dt, at=None):
        n = 1
        for d in shape[1:]:
            n *= d
        size = (n * DT_SIZE[dt] + 31) // 32 * 32
        if at is None:
            at = self.cur
        self.cur = max(self.cur, at + size)
        assert at + size <= SB_LIMIT, (self.name, name, at + size)
        self.G.uid += 1
        return self.nc.alloc_sbuf_tensor_at(f"{self.name}_{name}_{self.G.uid}", shape, dt,
                                            offset=self.G.base + at)

    def ps(self, name, shape, dt=F32):
        return self.es.enter_context(self.nc.psum_tensor(f"{self.name}_{name}", shape, dt))

    def finish(self):
        with self.nc.Block(no_gpsimd_drain=True) as block:
            self.P.emit(block)

    def __exit__(self, *a):
        return self.es.__exit__(*a)


class Ctx:
    pass


def declare_dram(nc, debug=False):
    c = Ctx()
    di = lambda n, s, d=F32: nc.dram_tensor(n, s, d, kind="ExternalInput").ap()
    sc = lambda n, s, d=BF16: nc.dram_tensor(n, s, d).ap()
    c.x = di("x", [NSEQ, T, D])
    c.out = nc.dram_tensor("out", [NSEQ, T, D], F32, kind="ExternalOutput").ap()
    if debug:
        c.dbg = nc.dram_tensor("dbg", [NSEQ, 128, 4 * T], BF16, kind="ExternalOutput").ap()
    c.ident = di("ident", [128, 128])
    c.maskneg = di("maskneg", [128, 128])
    c.gains = di("gains", [128, 8, 8])
    c.lamv = di("lamv", [128, 4, 64])
    c.gsub = di("gsub", [128, 128])
    c.vecs = di("vecs", [128, 8])
    c.wgu = [di(f"wgu{i}", [NF, 128, 2 * 8 * 128]) for i in range(2)]
    c.wd = [di(f"wd{i}", [8, 128, NF * 128]) for i in range(2)]
    c.wgub = [sc(f"wgub{i}", [NF, 128, 2 * 8 * 128]) for i in range(2)]
    c.wdb = [sc(f"wdb{i}", [8, 128, NF * 128]) for i in range(2)]
    c.winf = di("winf", [12, 128, 8 * 128])
    c.winv = di("winv", [128, 8 * 512])
    c.wout = di("wout", [8, 128, 8 * 128])
    c.wglu = di("wglu", [4, 128, 4 * 128])
    c.winfb = sc("winfb", [12, 128, 8 * 128])
    c.winvb = sc("winvb", [128, 8 * 512])
    c.woutb = sc("woutb", [8, 128, 8 * 128])
    c.wglub = sc("wglub", [4, 128, 4 * 128])
    c.s5p = di("s5p", [128, S5P_COLS])
    c.cmask = di("cmask", [128, 128])
    c.esel = di("esel", [128, 64 * 128])
    c.eb = sc("eb", [128, 64 * 128])
    c.xt7b = sc("xt7b", [128, 2 * 32 * 64])
    c.ttb = sc("ttb", [128, 32 * 128])
    c.ypb = sc("ypb", [2, 128, 2 * 16 * 128])
    return c


S5P_COLS = 1128
G_FF1_PRE, G_FF1_POST, G_MIX_PRE, G_MIX_POST, G_FF2_PRE, G_FF2_POST = range(6)


def emit_cast_weights(ph, c, which):
    P = ph.P
    for f in range(NF):
        P.add("pool", lambda e, f=f: e.dma_start(out=c.wgub[which][f], in_=c.wgu[which][f]),
              writes=[("wgub", which, f)], dma=True)
    for m in range(8):
        src = c.wd[which][m].rearrange("p (a b) -> p a b", a=2)
        dst = c.wdb[which][m].rearrange("p (a b) -> p a b", a=2)
        P.add("pool", lambda e, src=src, dst=dst: e.dma_start(out=dst, in_=src),
              writes=[("wdb", which, m)], dma=True)


def emit_cast_mixer_weights(ph, c):
    P = ph.P
    for j in range(12):
        P.add("pool", lambda e, j=j: e.dma_start(out=c.winfb[j], in_=c.winf[j]),
              writes=[("winfb", j)], dma=True)
    src = c.winv.rearrange("p (a b) -> p a b", a=2)
    dst = c.winvb.rearrange("p (a b) -> p a b", a=2)
    P.add("pool", lambda e: e.dma_start(out=dst, in_=src), writes=["winvb"], dma=True)
    for m in range(8):
        P.add("pool", lambda e, m=m: e.dma_start(out=c.woutb[m], in_=c.wout[m]),
              writes=[("woutb", m)], dma=True)
    for m in range(4):
        P.add("pool", lambda e, m=m: e.dma_start(out=c.wglub[m], in_=c.wglu[m]),
              writes=[("wglub", m)], dma=True)


def load_x_blocks(ph, c, K, seq, act_only=False):
    P = ph.P
    for tb in range(T // 128):
        slot = tb % 2
        xin = K.xio[slot]
        P.add("sp", lambda e, tb=tb, xin=xin: e.dma_start(out=xin[:], in_=c.x[seq, tb * 128:(tb + 1) * 128, :]),
              writes=[("xio", slot)], dma=True)
        for half in range(2):
            bank = K.psA[(2 * tb + half) % 2]
            bname = ("psA", (2 * tb + half) % 2)
            for j in range(4):
                cc = half * 4 + j
                P.add("pe", lambda e, bank=bank, j=j, cc=cc, xin=xin:
                      e.transpose(bank[:, j * 128:(j + 1) * 128], xin[:, cc * 128:(cc + 1) * 128], K.ident[:]),
                      reads=[("xio", slot), "ident"], writes=[bname])
            eng = "act" if (half == 0 or act_only) else "dve"
            dst = K.XT[:, half * 4:(half + 1) * 4, tb * 128:(tb + 1) * 128]
            srcv = bank[:].rearrange("p (j t) -> p j t", j=4)
            wr = [("XT", tb // 4, q) for q in range(half * 4, half * 4 + 4)]
            if eng == "act":
                P.add("act", lambda e, dst=dst, srcv=srcv: e.activation(dst, srcv, AF.Copy),
                      reads=[bname], writes=wr)
            else:
                P.add("dve", lambda e, dst=dst, srcv=srcv: e.tensor_copy(dst, srcv),
                      reads=[bname], writes=wr)
        yield tb


def emit_load_x(ph, c, K, seq):
    for _ in load_x_blocks(ph, c, K, seq):
        pass


def emit_store_x(ph, c, K, seq):
    P = ph.P
    for tb in range(T // 128):
        slot = tb % 2
        xo = K.xio[slot]
        for half in range(2):
            bank = K.psA[(2 * tb + half) % 2]
            bname = ("psA", (2 * tb + half) % 2)
            for j in range(4):
                cc = half * 4 + j
                P.add("pe", lambda e, bank=bank, j=j, cc=cc, tb=tb:
                      e.transpose(bank[:, j * 128:(j + 1) * 128], K.XT[:, cc, tb * 128:(tb + 1) * 128], K.ident[:]),
                      reads=[("XT", tb // 4, cc), "ident"], writes=[bname])
            dst = xo[:, half * 512:(half + 1) * 512]
            if half == 0:
                P.add("act", lambda e, dst=dst, bank=bank: e.activation(dst, bank[:], AF.Copy),
                      reads=[bname], writes=[("xio", slot)])
            else:
                P.add("dve", lambda e, dst=dst, bank=bank: e.tensor_copy(dst, bank[:]),
                      reads=[bname], writes=[("xio", slot)])
        P.add("sp", lambda e, tb=tb, xo=xo: e.dma_start(out=c.out[seq, tb * 128:(tb + 1) * 128, :], in_=xo[:]),
              reads=[("xio", slot)], writes=[("out", seq, tb)], dma=True)


def emit_rstd(P, K, ps_stat, sq_res, dim, out_rstd, out_name, nch=8):
    for cc in range(nch):
        P.add("pe", lambda e, cc=cc: e.matmul(ps_stat[:], K.ones[:], K.sq[:, cc, :], start=(cc == 0), stop=(cc == nch - 1)),
              reads=[(sq_res, cc), "ones"], writes=["ps_stat"])
    P.add("act", lambda e: e.activation(K.rt[:], ps_stat[:], AF.Sqrt, bias=K.epsb[:], scale=1.0 / dim),
          reads=["ps_stat", "epsb"], writes=["rt"])
    P.add("dve", lambda e: e.reciprocal(out_rstd[:], K.rt[:]), reads=["rt"], writes=[out_name])


import os
DBG = int(os.environ.get("KDBG", "9"))


def emit_ffn(ph, c, K, which, gpre, gpost, direct_cast=False, x_gen=None):
    P = ph.P
    st = {"wl": 0, "dl": 0}

    def prenorm(tt):
        ts = slice(tt * TT, (tt + 1) * TT)
        xn = K.xnT2[tt % 2]
        for cc in range(8):
            P.add("act", lambda e, cc=cc: e.activation(K.sq[:, cc, :], K.XT[:, cc, ts], AF.Square),
                  reads=[("XT", tt, cc)], writes=[("sq", cc)])
        emit_rstd(P, K, K.ps_stat, "sq", D, K.rstd, "rstd")
        for cc in range(8):
            P.add("dve", lambda e, cc=cc: e.scalar_tensor_tensor(
                out=xn[:, cc, :], in0=K.XT[:, cc, ts], scalar=K.gains[:, gpre, cc:cc + 1],
                in1=K.rstd[:], op0=ALU.mult, op1=ALU.mult),
                reads=[("XT", tt, cc), "rstd", "gains"], writes=[("xnT", tt % 2, cc)])

    def gateup(tt, f):
        xn = K.xnT2[tt % 2]
        slot = st["wl"] % K.NWGU
        st["wl"] += 1
        wt = K.wgu[slot]
        if direct_cast and tt == 0:
            P.add("pool", lambda e: e.dma_start(out=wt[:], in_=c.wgu[which][f]),
                  writes=[("wgu", slot)], dma=True)
            P.add("sp", lambda e: e.dma_start(out=c.wgub[which][f], in_=wt[:]),
                  reads=[("wgu", slot)], writes=[("wgub", which, f)], dma=True)
        else:
            P.add("sp", lambda e: e.dma_start(out=wt[:], in_=c.wgub[which][f]),
                  reads=[("wgub", which, f)], writes=[("wgu", slot)], dma=True)
        pg = K.psA[f % 2]
        pu = K.psB[f % 2]
        for cc in range(8):
            P.add("pe", lambda e, cc=cc: e.matmul(
                pg[:], wt[:, cc * 128:(cc + 1) * 128], xn[:, cc, :], start=(cc == 0), stop=(cc == 7)),
                reads=[("wgu", slot), ("xnT", tt % 2, cc)], writes=[("psA", f % 2)])
        for cc in range(8):
            P.add("pe", lambda e, cc=cc: e.matmul(
                pu[:], wt[:, (8 + cc) * 128:(9 + cc) * 128], xn[:, cc, :], start=(cc == 0), stop=(cc == 7)),
                reads=[("wgu", slot), ("xnT", tt % 2, cc)], writes=[("psB", f % 2)])
        sg = K.sg[f % 2]
        P.add("act", lambda e: e.activation(sg[:], pg[:], AF.Silu),
              reads=[("psA", f % 2)], writes=[("sg", f % 2)])
        P.add("dve", lambda e: e.tensor_tensor(K.hT[:, f, :], sg[:], pu[:], ALU.mult),
              reads=[("sg", f % 2), ("psB", f % 2)], writes=[("hT", f)])

    def down(tt, m):
        slot = st["dl"] % K.NWD
        st["dl"] += 1
        wt = K.wd[slot]
        if direct_cast and tt == 0:
            P.add("pool", lambda e: e.dma_start(out=wt[:].rearrange("p (a b) -> p a b", a=2),
                                                in_=c.wd[which][m].rearrange("p (a b) -> p a b", a=2)),
                  writes=[("wd", slot)], dma=True)
            P.add("sp", lambda e: e.dma_start(out=c.wdb[which][m], in_=wt[:]),
                  reads=[("wd", slot)], writes=[("wdb", which, m)], dma=True)
        else:
            P.add("sp", lambda e: e.dma_start(out=wt[:], in_=c.wdb[which][m]),
                  reads=[("wdb", which, m)], writes=[("wd", slot)], dma=True)
        py = K.psC[m % 2]
        for f in range(NF):
            P.add("pe", lambda e, f=f: e.matmul(
                py[:], wt[:, f * 128:(f + 1) * 128], K.hT[:, f, :], start=(f == 0), stop=(f == NF - 1)),
                reads=[("wd", slot), ("hT", f)], writes=[("psC", m % 2)])
        P.add("dve", lambda e: e.tensor_copy(K.yT[:, m, :], py[:]),
              reads=[("psC", m % 2)], writes=[("yT", m)])
        P.add("act", lambda e: e.activation(K.sq[:, m, :], K.yT[:, m, :], AF.Square),
              reads=[("yT", m)], writes=[("sq", m)])

    def post_piece(tt, k):
        ts = slice(tt * TT, (tt + 1) * TT)
        if k == 0:
            emit_rstd(P, K, K.ps_stat, "sq", D, K.rstd2, "rstd2")
            return
        for m in (k - 1,):
            tmp = K.tmp[m % 4]
            P.add("dve", lambda e, m=m, tmp=tmp: e.scalar_tensor_tensor(
                out=tmp[:], in0=K.yT[:, m, :], scalar=K.gains[:, gpost, m:m + 1],
                in1=K.rstd2[:], op0=ALU.mult, op1=ALU.mult),
                reads=[("yT", m), "rstd2", "gains"], writes=[("tmp", m % 4)])
            P.add("pool", lambda e, m=m, tmp=tmp: e.tensor_tensor(
                K.XT[:, m, ts], K.XT[:, m, ts], tmp[:], ALU.add),
                reads=[("tmp", m % 4), ("XT", tt, m)], writes=[("XT", tt, m)])

    prenorm(0)
    for tt in range(NTT):
        for f in range(NF):
            gateup(tt, f)
            if x_gen is not None and tt == 0:
                next(x_gen, None)
            if tt > 0 and f < 9:
                post_piece(tt - 1, f)
            if f == 9 and tt + 1 < NTT:
                prenorm(tt + 1)
        for m in range(8):
            down(tt, m)
        if direct_cast and tt == 0:
            emit_cast_mixer_weights(ph, c)
            emit_cast_weights(ph, c, 1)
    for k in range(9):
        post_piece(NTT - 1, k)


LAM_INIT = 0.8 - 0.6 * math.exp(-0.3 * 0)
COMMON_END = 68 * 1024


def alloc_common(ph, c, K):
    K.XT = ph.sb("XT", [128, 8, T], F32)
    K.ident = ph.sb("ident", [128, 128], F32)
    K.identb = ph.sb("identb", [128, 128], BF16)
    K.maskneg = ph.sb("maskneg", [128, 128], BF16)
    K.gains = ph.sb("gains", [128, 8, 8], F32)
    K.ones = ph.sb("ones", [128, 128], BF16)
    K.epsb = ph.sb("epsb", [128, 1], F32)
    K.nlam = ph.sb("nlam", [128, 1], F32)
    K.gsubb = ph.sb("gsubb", [128, 128], F32)
    K.vecs = ph.sb("vecs", [128, 8], F32)
    K.a8 = ph.sb("a8", [128, 2, 16], F32)
    K.a8b = ph.sb("a8b", [128, 2, 16], F32)
    K.mhalf = ph.sb("mhalf", [128, 1], F32)
    assert ph.cur <= COMMON_END
    ph.cur = COMMON_END


def emit_consts(ph, c, K):
    P = ph.P
    P.add("sp", lambda e: e.dma_start(out=K.ident[:], in_=c.ident), writes=["ident"], dma=True)
    P.add("sp", lambda e: e.dma_start(out=K.gains[:], in_=c.gains), writes=["gains"], dma=True)
    P.add("sp", lambda e: e.dma_start(out=K.gsubb[:], in_=c.gsub), writes=["gsubb"], dma=True)
    P.add("sp", lambda e: e.dma_start(out=K.vecs[:], in_=c.vecs), writes=["vecs"], dma=True)
    P.add("pool", lambda e: e.dma_start(out=K.maskneg[:], in_=c.maskneg), writes=["maskneg"], dma=True)
    P.add("pool", lambda e: e.memset(K.ones[:], 1.0), writes=["ones"])
    P.add("pool", lambda e: e.memset(K.epsb[:], EPS), writes=["epsb"])
    P.add("pool", lambda e: e.memset(K.mhalf[:], -0.5), writes=["mhalf"])
    P.add("dve", lambda e: e.tensor_copy(K.identb[:], K.ident[:]), reads=["ident"], writes=["identb"])
    for g in (G_FF1_POST, G_FF2_POST):
        P.add("dve", lambda e, g=g: e.tensor_scalar(K.gains[:, g, :], K.gains[:, g, :], 0.5, None, ALU.mult),
              reads=["gains"], writes=["gains"])
    P.add("dve", lambda e: e.tensor_scalar(K.gsubb[:], K.gsubb[:], 1.0 - LAM_INIT, None, ALU.mult),
          reads=["gsubb"], writes=["gsubb"])
    lamv = ph.sb("lamv", [128, 4, 64], F32)
    junk = ph.sb("lamjunk", [128, 64], F32)
    ss = ph.sb("lamss", [128, 4], F32)
    P.add("sp", lambda e: e.dma_start(out=lamv[:], in_=c.lamv), writes=["lamv"], dma=True)
    for i in range(2):
        P.add("dve", lambda e, i=i: e.tensor_tensor(junk[:], lamv[:, 2 * i, :], lamv[:, 2 * i + 1, :], ALU.mult),
              reads=["lamv"], writes=["lamjunk"])
        P.add("dve", lambda e, i=i: e.tensor_reduce(ss[:, i:i + 1], junk[:], mybir.AxisListType.X, ALU.add),
              reads=["lamjunk"], writes=[("lamss", i)])
        P.add("act", lambda e, i=i: e.activation(ss[:, 2 + i:3 + i], ss[:, i:i + 1], AF.Exp),
              reads=[("lamss", i)], writes=[("lamss", 2 + i)])
    P.add("dve", lambda e: e.tensor_tensor(K.nlam[:], ss[:, 3:4], ss[:, 2:3], ALU.subtract),
          reads=[("lamss", 2), ("lamss", 3)], writes=["nlam"])
    P.add("dve", lambda e: e.tensor_scalar(K.nlam[:], K.nlam[:], -LAM_INIT, None, ALU.add),
          reads=["nlam"], writes=["nlam"])


def alloc_ffn(ph, K):
    K.xio = [ph.sb(f"xio{i}", [128, D], F32) for i in range(2)]
    K.sq = ph.sb("sq", [128, 8, TT], BF16)
    K.xnT2 = [ph.sb(f"xnT{i}", [128, 8, TT], BF16) for i in range(2)]
    K.hT = ph.sb("hT", [128, NF, TT], BF16)
    K.yT = ph.sb("yT", [128, 8, TT], F32)
    K.rt = ph.sb("rt", [128, TT], F32)
    K.rstd = ph.sb("rstd", [128, TT], F32)
    K.rstd2 = ph.sb("rstd2", [128, TT], F32)
    K.sg = [ph.sb(f"sg{i}", [128, TT], F32) for i in range(2)]
    K.tmp = [ph.sb(f"tmp{i}", [128, TT], F32) for i in range(4)]
    K.NWGU = 6
    K.NWD = 4
    K.wgu = [ph.sb(f"wgu{i}", [128, 2 * 8 * 128], BF16) for i in range(K.NWGU)]
    K.wd = [ph.sb(f"wd{i}", [128, NF * 128], BF16) for i in range(K.NWD)]
    K.psA = [ph.ps(f"psA{i}", [128, 512]) for i in range(2)]
    K.psB = [ph.ps(f"psB{i}", [128, 512]) for i in range(2)]
    K.psC = [ph.ps(f"psC{i}", [128, 512]) for i in range(2)]
    K.ps_stat = ph.ps("ps_stat", [128, 512])


M_UT = COMMON_END + 0
M_QT = COMMON_END + 16384
M_KT = COMMON_END + 32768
M_VA = COMMON_END + 49152
M_OTA = COMMON_END + 65792
M_TMP = COMMON_END + 82176
M_UBAR = COMMON_END + 16384
M_SBF = COMMON_END + 32768
M_GT = COMMON_END + 0


def emit_prenorm(P, K, tt, gidx):
    ts = slice(tt * TT, (tt + 1) * TT)
    XTr = lambda q: ("XT", tt, q)
    xn = K.xnT2[tt % 2]
    for cc in range(8):
        P.add("act", lambda e, cc=cc: e.activation(K.sq[:, cc, :], K.XT[:, cc, ts], AF.Square),
              reads=[XTr(cc)], writes=[("sq", cc)])
    emit_rstd(P, K, K.ps_stat, "sq", D, K.rstd, "rstd")
    for cc in range(8):
        P.add("dve", lambda e, cc=cc: e.scalar_tensor_tensor(
            out=xn[:, cc, :], in0=K.XT[:, cc, ts], scalar=K.gains[:, gidx, cc:cc + 1],
            in1=K.rstd[:], op0=ALU.mult, op1=ALU.mult),
            reads=[XTr(cc), "rstd", "gains"], writes=[("xnT", tt % 2, cc)])


def emit_proj(ph, c, K):
    P = ph.P
    K.uT = ph.sb("uT", [128, 4, T], BF16, at=M_UT)
    K.qT = ph.sb("qT", [128, 4, T], BF16, at=M_QT)
    K.kT = ph.sb("kT", [128, 4, T], BF16, at=M_KT)
    K.vA = ph.sb("vA", [128, 16, 4, 130], BF16, at=M_VA)
    ph.cur = M_TMP
    K.sq = ph.sb("sq", [128, 8, TT], BF16)
    K.xnT2 = [ph.sb(f"xnT{i}", [128, 8, TT], BF16) for i in range(2)]
    K.rt = ph.sb("rt", [128, TT], F32)
    K.rstd = ph.sb("rstd", [128, TT], F32)
    NW = 4
    wr = [ph.sb(f"wr{i}", [128, 8 * 128], BF16) for i in range(NW)]
    winv = ph.sb("winv", [128, 8 * 512], BF16)
    K.ps_stat = ph.ps("ps_stat", [128, 512])
    NPA = 4
    psA = [ph.ps(f"psA{i}", [128, 512]) for i in range(NPA)]
    psV = [ph.ps(f"psV{i}", [128, 512]) for i in range(2)]
    P.add("sp", lambda e: e.dma_start(out=winv[:], in_=c.winvb), reads=["winvb"], writes=["winv"], dma=True)
    P.add("pool", lambda e: e.memset(K.vA[:, :, :, 128:129], 1.0), writes=["vAones"])
    wl = 0
    ev = 0
    emit_prenorm(P, K, 0, G_MIX_PRE)
    for tt in range(NTT):
        ts = slice(tt * TT, (tt + 1) * TT)
        xn = K.xnT2[tt % 2]
        for j in range(12):
            if j == 6 and tt + 1 < NTT:
                emit_prenorm(P, K, tt + 1, G_MIX_PRE)
            slot = wl % NW
            wl += 1
            wt = wr[slot]
            P.add("sp", lambda e, wt=wt, j=j: e.dma_start(out=wt[:], in_=c.winfb[j]),
                  reads=[("winfb", j)], writes=[("wr", slot)], dma=True)
            pa = psA[j % NPA]
            for cc in range(8):
                P.add("pe", lambda e, wt=wt, cc=cc, pa=pa, xn=xn: e.matmul(
                    pa[:], wt[:, cc * 128:(cc + 1) * 128], xn[:, cc, :], start=(cc == 0), stop=(cc == 7)),
                    reads=[("wr", slot), ("xnT", tt % 2, cc)], writes=[("psA", j % NPA)])
            if j < 4:
                dst, res = K.qT[:, j, ts], ("qT", j, tt)
            elif j < 8:
                dst, res = K.kT[:, j - 4, ts], ("kT", j - 4, tt)
            else:
                uTp = K.uT[:].rearrange("p j (s n) -> p j s n", s=8)
                dst, res = uTp[:, j - 8, :, tt * 64:(tt + 1) * 64], ("uT", j - 8, tt)
            srcp = pa[:] if j < 8 else pa[:].rearrange("p (n s) -> p s n", s=8)
            if ev % 2 == 0:
                P.add("act", lambda e, dst=dst, srcp=srcp: e.activation(dst, srcp, AF.Copy),
                      reads=[("psA", j % NPA)], writes=[res])
            else:
                P.add("dve", lambda e, dst=dst, srcp=srcp: e.tensor_copy(dst, srcp),
                      reads=[("psA", j % NPA)], writes=[res])
            ev += 1
        for b4 in range(4):
            tb = tt * 4 + b4
            pv = psV[b4 % 2]
            for cc in range(8):
                P.add("pe", lambda e, cc=cc, pv=pv, b4=b4, xn=xn: e.matmul(
                    pv[:], xn[:, cc, b4 * 128:(b4 + 1) * 128], winv[:, cc * 512:(cc + 1) * 512],
                    start=(cc == 0), stop=(cc == 7)),
                    reads=["winv", ("xnT", tt % 2, cc)], writes=[("psV", b4 % 2)])
            dst = K.vA[:, tb, :, 0:128]
            srcv = pv[:].rearrange("p (h v) -> p h v", h=4)
            if ev % 2 == 0:
                P.add("act", lambda e, dst=dst, srcv=srcv: e.activation(dst, srcv, AF.Copy),
                      reads=[("psV", b4 % 2)], writes=[("vA", tb)])
            else:
                P.add("dve", lambda e, dst=dst, srcv=srcv: e.tensor_copy(dst, srcv),
                      reads=[("psV", b4 % 2)], writes=[("vA", tb)])
            ev += 1


def emit_attn(ph, c, K):
    P = ph.P
    K.qT = ph.sb("qT", [128, 4, T], BF16, at=M_QT)
    K.kT = ph.sb("kT", [128, 4, T], BF16, at=M_KT)
    K.vA = ph.sb("vA", [128, 16, 4, 130], BF16, at=M_VA)
    K.oTa = ph.sb("oTa", [128, 4, T], BF16, at=M_OTA)
    ph.cur = M_TMP
    NPT = 6
    pT = [ph.sb(f"pT{i}", [128, 512], BF16) for i in range(NPT)]
    o1 = [ph.sb(f"o1_{i}", [128, 4, 128], F32) for i in range(2)]
    od = [ph.sb(f"od{i}", [128, 128], F32) for i in range(8)]
    junk = [ph.sb(f"junk{i}", [128, 128], F32) for i in range(4)]
    sm = [ph.sb(f"sm{i}", [128, 8], F32) for i in range(8)]
    NPS = 4
    psS = [ph.ps(f"psS{i}", [128, 512]) for i in range(NPS)]
    acc = [[ph.ps(f"acc{r}{b}", [128, 512]) for b in range(2)] for r in range(2)]
    ob_tok = ph.sb("ob_tok", [128, 16, 4, 128], BF16)
    items = []
    si = 0
    pi = 0
    rnd = 0
    fin = 0
    for h in range(4):
        for qt in range(NTT):
            for cmap in range(2):
                r = rnd % 2
                rnd += 1
                rows = slice(cmap * 64, (cmap + 1) * 64)
                started = [False, False]
                for kb in range(4 * qt + 4):
                    j = kb - 4 * qt
                    qlo = max(j, 0) * 128
                    sb_ = psS[si % NPS]
                    sres = ("psS", si % NPS)
                    si += 1
                    pt = pT[pi % NPT]
                    pres = ("pT", pi % NPT)
                    pi += 1

                    def s_part(sb_=sb_, sres=sres, qlo=qlo, kb=kb, rows=rows, h=h, qt=qt, j=j):
                        P.add("pe", lambda e: e.matmul(
                            sb_[:, qlo:512], K.kT[rows, h, kb * 128:(kb + 1) * 128],
                            K.qT[rows, h, qt * 512 + qlo:(qt + 1) * 512], start=True, stop=(j < 0)),
                            reads=[("kT", h, kb // 4), ("qT", h, qt)], writes=[sres])
                        if j >= 0:
                            P.add("pe", lambda e: e.matmul(
                                sb_[:, qlo:qlo + 128], K.identb[:], K.maskneg[:], start=False, stop=True,
                                skip_group_check=True),
                                reads=["identb", "maskneg"], writes=[sres])

                    sts = []
                    for qb in range(max(j, 0), 4):
                        sts.append(not started[qb // 2])
                        started[qb // 2] = True

                    def r_part(sb_=sb_, sres=sres, pt=pt, pres=pres, qlo=qlo, kb=kb, h=h, qt=qt, j=j, r=r, sts=sts):
                        P.add("act", lambda e: e.activation(
                            pt[:, qlo:512], sb_[:, qlo:512], AF.Exp, scale=0.125),
                            reads=[sres], writes=[pres])
                        for ii, qb in enumerate(range(max(j, 0), 4)):
                            bank = acc[r][qb // 2]
                            col = (qb % 2) * 256
                            st = sts[ii]
                            P.add("pe", lambda e, bank=bank, col=col, qb=qb, st=st: e.matmul(
                                bank[:, col:col + 129], pt[:, qb * 128:(qb + 1) * 128], K.vA[:, kb, h, 0:129],
                                start=st, stop=(kb == 4 * qt + qb), skip_group_check=True),
                                reads=[pres, ("vA", kb), "vAones"], writes=[("acc", r, qb // 2)])

                    items.append((s_part, r_part))
                items.append((None, (lambda h=h, qt=qt, cmap=cmap, r=r, rnd=rnd: finalize(h, qt, cmap, r, rnd))))
    fin = [0]

    def finalize(h, qt, cmap, r, rnd):
        def acc_of(qb):
            return acc[r][qb // 2], (qb % 2) * 256, ("acc", r, qb // 2)
        par = fin[0] % 2
        fin[0] += 1
        smq = [sm[par * 4 + qb] for qb in range(4)]
        if cmap == 0:
            o1t = o1[(rnd // 2) % 2]
            for qb in range(4):
                bank, col, ares = acc_of(qb)
                P.add("dve", lambda e, qb=qb, bank=bank, col=col: e.reciprocal(
                    smq[qb][:, 0:1], bank[:, col + 128:col + 129]),
                    reads=[ares], writes=[("sm", par, qb, 0)])
            for qb in range(4):
                bank, col, ares = acc_of(qb)
                P.add("dve", lambda e, qb=qb, bank=bank, col=col: e.tensor_scalar(
                    o1t[:, qb, :], bank[:, col:col + 128], smq[qb][:, 0:1], None, ALU.mult),
                    reads=[ares, ("sm", par, qb, 0)], writes=[("o1", (rnd // 2) % 2, qb)])
            return
        o1t = o1[((rnd - 1) // 2) % 2]
        odq = [od[par * 4 + qb] for qb in range(4)]
        for qb in range(4):
            bank, col, ares = acc_of(qb)
            P.add("dve", lambda e, qb=qb, bank=bank, col=col: e.reciprocal(
                smq[qb][:, 1:2], bank[:, col + 128:col + 129]),
                reads=[ares], writes=[("sm", par, qb, 1)])
        for qb in range(4):
            P.add("dve", lambda e, qb=qb: e.tensor_tensor(smq[qb][:, 2:3], smq[qb][:, 1:2], K.nlam[:], ALU.mult),
                  reads=[("sm", par, qb, 1), "nlam"], writes=[("sm", par, qb, 2)])
        for qb in range(4):
            bank, col, ares = acc_of(qb)
            P.add("dve", lambda e, qb=qb, bank=bank, col=col: e.scalar_tensor_tensor(
                out=odq[qb][:], in0=bank[:, col:col + 128], scalar=smq[qb][:, 2:3], in1=o1t[:, qb, :],
                op0=ALU.mult, op1=ALU.add),
                reads=[ares, ("sm", par, qb, 2), ("o1", ((rnd - 1) // 2) % 2, qb)], writes=[("od", par, qb)])
        for qb in range(4):
            P.add("dve", lambda e, qb=qb: e.tensor_tensor(junk[qb][:], odq[qb][:], odq[qb][:], ALU.mult),
                  reads=[("od", par, qb)], writes=[("junk", qb)])
        for qb in range(4):
            P.add("dve", lambda e, qb=qb: e.tensor_reduce(
                smq[qb][:, 3:4], junk[qb][:], mybir.AxisListType.X, ALU.add),
                reads=[("junk", qb)], writes=[("sm", par, qb, 3)])
        for qb in range(4):
            P.add("dve", lambda e, qb=qb: e.tensor_scalar(
                smq[qb][:, 4:5], smq[qb][:, 3:4], 1.0 / 128, EPS, ALU.mult, ALU.add),
                reads=[("sm", par, qb, 3)], writes=[("sm", par, qb, 4)])
        for qb in range(4):
            P.add("pool", lambda e, qb=qb: e.tensor_tensor(smq[qb][:, 5:6], smq[qb][:, 4:5], K.mhalf[:], ALU.pow),
                  reads=[("sm", par, qb, 4), "mhalf"], writes=[("sm", par, qb, 5)])
        for qb in range(4):
            tb = qt * 4 + qb
            P.add("dve", lambda e, qb=qb, tb=tb: e.scalar_tensor_tensor(
                out=ob_tok[:, tb, h, :], in0=odq[qb][:], scalar=smq[qb][:, 5:6], in1=K.gsubb[:],
                op0=ALU.mult, op1=ALU.mult),
                reads=[("od", par, qb), ("sm", par, qb, 5), "gsubb"], writes=[("ob", tb, h)])

    LOOK = NPS - 1
    seq_s = [it for it in items if it[0] is not None]
    ns = 0
    nr = 0
    for it in items:
        if it[0] is None:
            it[1]()
            continue
        while ns < len(seq_s) and ns <= nr + LOOK:
            seq_s[ns][0]()
            ns += 1
        it[1]()
        nr += 1
    for tb in range(16):
        bank = psS[tb % NPS]
        bres = ("psS", tb % NPS)
        for h in range(4):
            P.add("pe", lambda e, bank=bank, tb=tb, h=h: e.matmul(
                bank[:, h * 128:(h + 1) * 128], ob_tok[:, tb, h, :], K.identb[:], start=True, stop=True,
                skip_group_check=True),
                reads=[("ob", tb, h), "identb"], writes=[bres])
        dst = K.oTa[:, :, tb * 128:(tb + 1) * 128]
        srcv = bank[:].rearrange("p (h t) -> p h t", h=4)
        if tb % 2 == 0:
            P.add("act", lambda e, dst=dst, srcv=srcv: e.activation(dst, srcv, AF.Copy),
                  writes=[bres, ("oTa", tb)])
        else:
            P.add("dve", lambda e, dst=dst, srcv=srcv: e.tensor_copy(dst, srcv),
                  writes=[bres, ("oTa", tb)])


TWO_PI = 2.0 * math.pi
MAGIC = 12582912.0


def emit_s5_setup(ph, c, K, hook=None, hook_every=8):
    class _PW:
        def __init__(self, P):
            self.P = P
            self.n = 0
            self.lim = int(os.environ.get("KSTEP", "100000"))

        def add(self, *a, **k):
            self.n += 1
            if self.n > self.lim:
                return None
            if os.environ.get("KSTEPV") and self.n == self.lim:
                import traceback
                traceback.print_stack(limit=4)
            r = self.P.add(*a, **k)
            if hook is not None and self.n % hook_every == 0:
                hook()
            return r
    P = _PW(ph.P)
    sp_ = ph.sb("s5p", [128, S5P_COLS], F32)
    cmask = ph.sb("cmask", [128, 128], F32)
    P.add("sp", lambda e: e.dma_start(out=sp_[:], in_=c.s5p), writes=["s5p"], dma=True)
    P.add("sp", lambda e: e.dma_start(out=cmask[:], in_=c.cmask), writes=["cmask"], dma=True)
    for q in range(4):
        src = c.esel[:, q * 2048:(q + 1) * 2048]
        dst = c.eb[:, q * 2048:(q + 1) * 2048]
        P.add("pool", lambda e, src=src, dst=dst: e.dma_start(out=dst, in_=src), writes=[("eb", q)], dma=True)
    are, aim, ldt = sp_[:, 0:16], sp_[:, 16:32], sp_[:, 32:48]
    bre = sp_[:, 48:304].rearrange("p (g h) -> p g h", g=16)
    bim = sp_[:, 304:560].rearrange("p (g h) -> p g h", g=16)
    cre = sp_[:, 560:816].rearrange("p (g h) -> p g h", g=16)
    cim = sp_[:, 816:1072].rearrange("p (g h) -> p g h", g=16)
    kk = sp_[:, 1072:1096].rearrange("p (w k) -> p w k", w=3)
    dD = sp_[:, 1096:1128]
    cnt = [0]

    def T_(shape):
        cnt[0] += 1
        return ph.sb(f"t{cnt[0]}", shape, F32), f"t{cnt[0]}"

    def tt(out, a, b, op, r, w, eng="dve"):
        P.add(eng, lambda e: e.tensor_tensor(out, a, b, op), reads=r, writes=w)

    def ts(out, a, s1, s2, op0, op1, r, w):
        P.add("dve", lambda e: e.tensor_scalar(out, a, s1, s2, op0, op1), reads=r, writes=w)

    def sin_of(x, xr_, n):
        t, tn = T_([128, n])
        r, rn = T_([128, n])
        ts(t[:], x, 1.0 / TWO_PI, MAGIC, ALU.mult, ALU.add, [xr_], [tn])
        ts(t[:], t[:], -MAGIC, None, ALU.add, ALU.bypass, [tn], [tn])
        P.add("dve", lambda e: e.scalar_tensor_tensor(out=r[:], in0=t[:], scalar=-TWO_PI, in1=x,
                                                       op0=ALU.mult, op1=ALU.add), reads=[tn, xr_], writes=[rn])
        ts(r[:], r[:], 3.141592, -3.141592, ALU.min, ALU.max, [rn], [rn])
        P.add("act", lambda e: e.activation(r[:], r[:], AF.Sin), reads=[rn], writes=[rn])
        return r, rn

    def cexp(xr_t, xr_n, xi_t, xi_n, n, unit=False):
        s_, sn = sin_of(xi_t, xi_n, n)
        x2, x2n = T_([128, n])
        ts(x2[:], xi_t, math.pi / 2, None, ALU.add, ALU.bypass, [xi_n], [x2n])
        c_, cn = sin_of(x2[:], x2n, n)
        if unit:
            return c_, cn, s_, sn
        e_, en = T_([128, n])
        P.add("act", lambda e: e.activation(e_[:], xr_t, AF.Exp), reads=[xr_n], writes=[en])
        tt(c_[:], c_[:], e_[:], ALU.mult, [cn, en], [cn])
        tt(s_[:], s_[:], e_[:], ALU.mult, [sn, en], [sn])
        return c_, cn, s_, sn

    dt, dtn = T_([128, 16])
    P.add("act", lambda e: e.activation(dt[:], ldt, AF.Exp), reads=["s5p"], writes=[dtn])
    xr, xrn = T_([128, 16])
    xi, xin = T_([128, 16])
    tt(xr[:], are, dt[:], ALU.mult, ["s5p", dtn], [xrn])
    tt(xi[:], aim, dt[:], ALU.mult, ["s5p", dtn], [xin])
    if int(os.environ.get('KSET', '9')) < 1:
        return
    c1, c1n, s1, s1n = cexp(xr[:], xrn, xi[:], xin, 16)
    ts(c1[:], c1[:], -1.0, None, ALU.add, ALU.bypass, [c1n], [c1n])
    t1, t1n = T_([128, 16])
    t2, t2n = T_([128, 16])
    rden, rdn = T_([128, 16])
    tt(t1[:], are, are, ALU.mult, ["s5p"], [t1n])
    tt(t2[:], aim, aim, ALU.mult, ["s5p"], [t2n])
    tt(t1[:], t1[:], t2[:], ALU.add, [t1n, t2n], [t1n])
    P.add("dve", lambda e: e.reciprocal(rden[:], t1[:]), reads=[t1n], writes=[rdn])
    cr, crn = T_([128, 16])
    ci, cin = T_([128, 16])
    tt(t1[:], c1[:], are, ALU.mult, [c1n, "s5p"], [t1n])
    tt(t2[:], s1[:], aim, ALU.mult, [s1n, "s5p"], [t2n])
    tt(t1[:], t1[:], t2[:], ALU.add, [t1n, t2n], [t1n])
    tt(cr[:], t1[:], rden[:], ALU.mult, [t1n, rdn], [crn])
    tt(t1[:], s1[:], are, ALU.mult, [s1n, "s5p"], [t1n])
    tt(t2[:], c1[:], aim, ALU.mult, [c1n, "s5p"], [t2n])
    tt(t1[:], t1[:], t2[:], ALU.subtract, [t1n, t2n], [t1n])
    tt(ci[:], t1[:], rden[:], ALU.mult, [t1n, rdn], [cin])
    if int(os.environ.get('KSET', '9')) < 2:
        return
    Br, Brn = T_([128, 16, 16])
    Bi, Bin = T_([128, 16, 16])
    u1, u1n = T_([128, 16, 16])
    crb = cr[:].rearrange("p (g o) -> p g o", o=1).broadcast_to([128, 16, 16])
    cib = ci[:].rearrange("p (g o) -> p g o", o=1).broadcast_to([128, 16, 16])
    tt(Br[:], crb, bre, ALU.mult, [crn, "s5p"], [Brn])
    tt(u1[:], cib, bim, ALU.mult, [cin, "s5p"], [u1n])
    tt(Br[:], Br[:], u1[:], ALU.subtract, [Brn, u1n], [Brn])
    tt(Bi[:], crb, bim, ALU.mult, [crn, "s5p"], [Bin])
    tt(u1[:], cib, bre, ALU.mult, [cin, "s5p"], [u1n])
    tt(Bi[:], Bi[:], u1[:], ALU.add, [Bin, u1n], [Bin])
    if int(os.environ.get('KSET', '9')) < 3:
        return
    k8r, k8rn = T_([128, 16])
    k8i, k8in = T_([128, 16])
    ts(k8r[:], xr[:], 8.0, None, ALU.mult, ALU.bypass, [xrn], [k8rn])
    ts(k8i[:], xi[:], 8.0, None, ALU.mult, ALU.bypass, [xin], [k8in])
    a8c, a8cn, a8s, a8sn = cexp(k8r[:], k8rn, k8i[:], k8in, 16)
    P.add("dve", lambda e: e.tensor_copy(K.a8[:, 0, :], a8c[:]), reads=[a8cn], writes=["a8"])
    P.add("dve", lambda e: e.tensor_copy(K.a8[:, 1, :], a8s[:]), reads=[a8sn], writes=["a8"])
    P.add("dve", lambda e: e.tensor_copy(K.a8b[:, 0, :], a8s[:]), reads=[a8sn], writes=["a8b"])
    ts(K.a8b[:, 1, :], a8s[:], -1.0, None, ALU.mult, ALU.bypass, [a8sn], ["a8b"])
    if os.environ.get('KVERB'):
        print('setup ops before powers', P.n)
    if int(os.environ.get('KSET', '9')) < 4:
        return
    outs = []
    xrb = xr[:].rearrange("p (g o) -> p g o", o=1).broadcast_to([128, 16, 8])
    xib = xi[:].rearrange("p (g o) -> p g o", o=1).broadcast_to([128, 16, 8])
    W4 = [128, 16, 8, 16]
    f1, f1n = T_(W4)
    f2, f2n = T_(W4)
    X7 = [ph.sb(f"X7{i}", W4, BF16) for i in range(2)]
    Zt = [ph.sb(f"Zt{i}", W4, BF16) for i in range(2)]
    Yp = ph.sb("Yp", [128, 2, 16, 128], BF16)
    for w in range(3):
        kb = kk[:, w:w + 1, :].broadcast_to([128, 16, 8])
        kr, krn = T_([128, 16, 8])
        ki, kin = T_([128, 16, 8])
        tt(kr[:], xrb, kb, ALU.mult, [xrn, "s5p"], [krn])
        tt(ki[:], xib, kb, ALU.mult, [xin, "s5p"], [kin])
        pr, prn, pi_, pin = cexp(kr[:].rearrange("p g k -> p (g k)"), krn,
                                 ki[:].rearrange("p g k -> p (g k)"), kin, 128)
        prb = pr[:].rearrange("p (g k o) -> p g k o", g=16, o=1).broadcast_to(W4)
        pib = pi_[:].rearrange("p (g k o) -> p g k o", g=16, o=1).broadcast_to(W4)
        if w == 0:
            mre = Br[:].rearrange("p g (o h) -> p g o h", o=1).broadcast_to(W4)
            mim = Bi[:].rearrange("p g (o h) -> p g o h", o=1).broadcast_to(W4)
            mrn, min_ = Brn, Bin
            ore, oim = X7[0][:], X7[1][:]
            orn, oin = "X7re", "X7im"
        else:
            mre = cre.rearrange("p g (o h) -> p g o h", o=1).broadcast_to(W4)
            mim = cim.rearrange("p g (o h) -> p g o h", o=1).broadcast_to(W4)
            mrn, min_ = "s5p", "s5p"
            if w == 1:
                ore, oim = Zt[0][:], Zt[1][:]
                orn, oin = "Zre", "Zim"
            else:
                ore = Yp[:, 0, :, :].rearrange("p g (k h) -> p g k h", k=8)
                oim = Yp[:, 1, :, :].rearrange("p g (k h) -> p g k h", k=8)
                orn, oin = "Ypre", "Ypim"
        tt(f1[:], prb, mre, ALU.mult, [prn, mrn], [f1n])
        tt(f2[:], pib, mim, ALU.mult, [pin, min_], [f2n])
        tt(ore, f1[:], f2[:], ALU.subtract, [f1n, f2n], [orn])
        tt(f1[:], prb, mim, ALU.mult, [prn, min_], [f1n])
        tt(f2[:], pib, mre, ALU.mult, [pin, mrn], [f2n])
        if w == 0:
            tt(oim, f1[:], f2[:], ALU.add, [f1n, f2n], [oin])
        else:
            P.add("dve", lambda e, oim=oim: e.scalar_tensor_tensor(
                out=oim, in0=f1[:], scalar=-1.0, in1=f2[:], op0=ALU.mult, op1=ALU.subtract),
                reads=[f1n, f2n], writes=[oin])
    Ypm = [ph.sb(f"Ypm{a}", [128, 2, 16, 128], BF16) for a in range(2)]
    for a in range(2):
        keep = slice(a * 64, (a + 1) * 64)
        P.add("pool", lambda e, a=a: e.memset(Ypm[a][:], 0.0), writes=[("Ypmz", a)])
        P.add("dve", lambda e, a=a, keep=keep: e.tensor_copy(Ypm[a][keep], Yp[keep]),
              reads=["Ypre", "Ypim", ("Ypmz", a)], writes=[("Ypm", a)])
        P.add("sp", lambda e, a=a: e.dma_start(out=c.ypb[a], in_=Ypm[a][:].rearrange("p r g k -> p (r g k)")),
              reads=[("Ypm", a)], writes=[("ypb", a)], dma=True)
    if os.environ.get('KVERB'):
        print('setup ops after powers', P.n)
    if int(os.environ.get('KSET', '9')) < 5:
        return
    xt7 = ph.sb("xt7", [128, 2, 32, 64], BF16)
    psX = [ph.ps(f"psX{i}", [128, 512]) for i in range(2)]
    bi = 0
    for ri in range(2):
        for gq in range(4):
            bank = psX[bi % 2]
            bres = ("psX", bi % 2)
            bi += 1
            for gi in range(4):
                gp = gq * 4 + gi
                src = X7[ri][:, gp, :, :].rearrange("p k h -> p (k h)")
                P.add("pe", lambda e, bank=bank, gi=gi, src=src: e.matmul(
                    bank[:, gi * 128:(gi + 1) * 128], src, K.identb[:], start=True, stop=True,
                    skip_group_check=True),
                    reads=["X7re" if ri == 0 else "X7im", "identb"], writes=[bres])
            dst = xt7[:, ri, gq * 8:(gq + 1) * 8, :]
            srcv = bank[:, 0:512].rearrange("p (g q) -> p g q", g=8)
            P.add("act" if gq % 2 == 0 else "dve",
                  (lambda e, dst=dst, srcv=srcv: e.activation(dst, srcv, AF.Copy)) if gq % 2 == 0 else
                  (lambda e, dst=dst, srcv=srcv: e.tensor_copy(dst, srcv)),
                  reads=[bres], writes=[bres, "xt7"])
    P.add("sp", lambda e: e.dma_start(out=c.xt7b, in_=xt7[:].rearrange("p r g q -> p (r g q)")),
          reads=["xt7"], writes=["xt7b"], dma=True)
    if int(os.environ.get('KSET', '9')) < 6:
        return
    ttl = ph.sb("ttl", [128, 32, 128], BF16)
    Zm = [[ph.sb(f"Zm{a}{b}", W4, BF16) for b in range(2)] for a in range(2)]
    for a in range(2):
        keep = slice(a * 64, (a + 1) * 64)
        for b in range(2):
            P.add("pool", lambda e, a=a, b=b: e.memset(Zm[a][b][:], 0.0), writes=[("Zmz", a, b)])
            P.add("dve", lambda e, a=a, b=b, keep=keep: e.tensor_copy(Zm[a][b][keep], Zt[b][keep]),
                  reads=["Zre", "Zim", ("Zmz", a, b)], writes=["Zm"])
    tm = [ph.sb(f"tm{i}", [128, 4, 128], F32) for i in range(2)]
    psT = [ph.ps(f"psTT{i}", [128, 512]) for i in range(2)]
    cmb = cmask[:].rearrange("p (o q) -> p o q", o=1).broadcast_to([128, 4, 128])
    for gq in range(8):
        bank = psT[gq % 2]
        bres = ("psTT", gq % 2)
        for gi in range(4):
            g = gq * 4 + gi
            gp, g2 = g // 2, g % 2
            rows = slice(g2 * 64, (g2 + 1) * 64)
            for ri in range(2):
                lh = X7[ri][:, gp, :, :].rearrange("p k h -> p (k h)")
                rh = Zm[g2][ri][:, gp, :, :].rearrange("p k h -> p (k h)")
                P.add("pe", lambda e, bank=bank, gi=gi, lh=lh, rh=rh, ri=ri, first=(gi == 0 and ri == 0): e.matmul(
                    bank[:, gi * 128:(gi + 1) * 128], lh, rh, start=first, stop=(ri == 1),
                    skip_group_check=True),
                    reads=["X7re", "X7im", "Zm"], writes=[bres])
        tmt = tm[gq % 2]
        P.add("dve", lambda e, tmt=tmt, bank=bank: e.tensor_tensor(
            tmt[:], bank[:].rearrange("p (g q) -> p g q", g=4), cmb, ALU.mult),
            reads=["cmask"], writes=[bres, ("tm", gq % 2)])
        for gi in range(4):
            g = gq * 4 + gi
            P.add("dve", lambda e, tmt=tmt, gi=gi, g=g: e.scalar_tensor_tensor(
                out=ttl[:, g, :], in0=K.ident[:], scalar=dD[:, g:g + 1], in1=tmt[:, gi, :],
                op0=ALU.mult, op1=ALU.add),
                reads=[("tm", gq % 2), "ident", "s5p"], writes=["ttl"])
    P.add("sp", lambda e: e.dma_start(out=c.ttb, in_=ttl[:].rearrange("p g q -> p (g q)")),
          reads=["ttl"], writes=["ttb"], dma=True)


def emit_s5a(ph, c, K):
    P = ph.P
    K.uT = ph.sb("uT", [128, 4, T], BF16, at=M_UT)
    K.Ubar = ph.sb("Ubar", [128, 32, 256], BF16, at=M_UBAR)
    K.Sbf = ph.sb("Sbf", [128, 2, 16, 257], BF16, at=M_SBF)
    ph.cur = M_TMP
    NCH, CL = 9, 32
    GRP = [(0, 5), (5, 9)]
    ph.cur = COMMON_END + 49280
    P1 = [[ph.sb(f"P1_{g}{i}", [128, b - a, 2, 16], F32) for i in range(2)] for g, (a, b) in enumerate(GRP)]
    P2 = [[ph.sb(f"P2_{g}{i}", [128, b - a, 2, 16], F32) for i in range(2)] for g, (a, b) in enumerate(GRP)]
    XT7_OFF = ph.cur
    xt7 = ph.sb("xt7", [128, 2, 32, 64], BF16)
    assert ph.cur <= M_OTA
    ph.cur = M_TMP
    E = ph.sb("E", [128, 64 * 128], BF16)
    L = ph.sb("L", [128, NCH * CL, 2, 16], F32)
    F1 = [ph.sb("F1", [128, 32, 16], F32, at=XT7_OFF)] * 2
    F2 = [ph.sb("F2", [128, 32, 16], F32, at=XT7_OFF + 2048)] * 2
    F3 = [ph.sb("F3", [128, 32, 16], F32, at=XT7_OFF + 4096)] * 2
    F4 = [ph.sb("F4", [128, 32, 16], F32, at=XT7_OFF + 6144)] * 2
    NPU, NPL = 4, 4
    psU = [ph.ps(f"psU{i}", [128, 512]) for i in range(NPU)]
    psL = [ph.ps(f"psL{i}", [128, 512]) for i in range(NPL)]
    P.add("sp", lambda e: e.dma_start(out=E[:], in_=c.eb), reads=[("eb", q) for q in range(4)],
          writes=["E"], dma=True)
    P.add("sp", lambda e: e.dma_start(out=xt7[:].rearrange("p r g q -> p (r g q)"), in_=c.xt7b),
          reads=["xt7b"], writes=["xt7"], dma=True)
    P.add("pool", lambda e: e.memset(K.Sbf[:, :, :, 0:1], 0.0), writes=["Sbf0"])
    uTv = K.uT[:].rearrange("p j (s n) -> p j s n", s=8)
    for g in range(32):
        j, glo = g // 8, g % 8
        bank = psU[g % NPU]
        for sg in range(8):
            idx = glo * 8 + sg
            P.add("pe", lambda e, bank=bank, idx=idx, j=j, sg=sg: e.matmul(
                bank[:, 0:256], E[:, idx * 128:(idx + 1) * 128], uTv[:, j, sg, :],
                start=(sg == 0), stop=(sg == 7)),
                reads=["E"], writes=[("psU", g % NPU)])
        if g % 2 == 0:
            P.add("act", lambda e, g=g, bank=bank: e.activation(K.Ubar[:, g, :], bank[:, 0:256], AF.Copy),
                  writes=[("psU", g % NPU), ("Ubar", g)])
        else:
            P.add("dve", lambda e, g=g, bank=bank: e.tensor_copy(K.Ubar[:, g, :], bank[:, 0:256]),
                  writes=[("psU", g % NPU), ("Ubar", g)])
    for gp in range(16):
        bank = psL[gp % NPL]
        for g2 in range(2):
            g = 2 * gp + g2
            for ri in range(2):
                P.add("pe", lambda e, bank=bank, g2=g2, g=g, ri=ri: e.matmul(
                    bank[g2 * 64:(g2 + 1) * 64, ri * 256:(ri + 1) * 256], xt7[:, ri, g, :], K.Ubar[:, g, :],
                    start=True, stop=True, skip_group_check=True),
                    reads=["xt7", ("Ubar", g)], writes=[("psL", gp % NPL)])
        dst = L[:, 0:256, :, gp].rearrange("p n r -> p r n")
        srcv = bank[:].rearrange("p (r n) -> p r n", r=2)
        if gp % 2 == 0:
            P.add("act", lambda e, dst=dst, srcv=srcv: e.activation(dst, srcv, AF.Copy),
                  writes=[("psL", gp % NPL), ("Lgp", gp)])
        else:
            P.add("dve", lambda e, dst=dst, srcv=srcv: e.tensor_copy(dst, srcv),
                  writes=[("psL", gp % NPL), ("Lgp", gp)])
    Lc = L[:].rearrange("p (c j) r g -> p c j r g", c=NCH)
    allL = [("Lgp", gp) for gp in range(16)]
    P.add("dve", lambda e: e.memset(L[:, 256:288, :, :], 0.0), reads=allL, writes=[("Lg", 1), "xt7"])
    P.add("dve", lambda e: e.tensor_copy(L[:, 256, :, :], K.a8[:]), writes=[("Lg", 1)])
    for j in range(1, CL):
        for stage in range(5):
            for g, (ca, cb) in enumerate(GRP):
                nchk = cb - ca
                p1, p2 = P1[g][j % 2], P2[g][j % 2]
                a1c = K.a8[:, 0:1, :].rearrange("p (c r) g -> p c r g", c=1).broadcast_to([128, nchk, 2, 16])
                a2c = K.a8b[:].rearrange("p (c r) g -> p c r g", c=1).broadcast_to([128, nchk, 2, 16])
                lg = ("Lg", g)
                if stage == 0:
                    P.add("dve", lambda e, p1=p1, j=j, ca=ca, cb=cb, a1c=a1c: e.tensor_tensor(
                        p1[:], Lc[:, ca:cb, j - 1, :, :], a1c, ALU.mult),
                        reads=[lg], writes=[("P1", g, j % 2)])
                elif stage == 1:
                    P.add("dve", lambda e, p2=p2, j=j, ca=ca, cb=cb, a2c=a2c: e.tensor_tensor(
                        p2[:], Lc[:, ca:cb, j - 1, :, :], a2c, ALU.mult),
                        reads=[lg], writes=[("P2", g, j % 2)])
                elif stage == 2:
                    P.add("dve", lambda e, p1=p1, j=j, ca=ca, cb=cb: e.tensor_tensor(
                        Lc[:, ca:cb, j, :, :], Lc[:, ca:cb, j, :, :], p1[:], ALU.add),
                        reads=[("P1", g, j % 2)], writes=[lg])
                elif stage == 3:
                    P.add("dve", lambda e, p2=p2, j=j, ca=ca, cb=cb: e.tensor_tensor(
                        Lc[:, ca:cb, j, 0, :], Lc[:, ca:cb, j, 0, :], p2[:, :, 1, :], ALU.add),
                        reads=[("P2", g, j % 2)], writes=[lg])
                else:
                    P.add("dve", lambda e, p2=p2, j=j, ca=ca, cb=cb: e.tensor_tensor(
                        Lc[:, ca:cb, j, 1, :], Lc[:, ca:cb, j, 1, :], p2[:, :, 0, :], ALU.add),
                        reads=[("P2", g, j % 2)], writes=[lg])
    pwr = Lc[:, 8, :, 0, :]
    pwi = Lc[:, 8, :, 1, :]
    LG = [("Lg", 0), ("Lg", 1)]
    for cch in range(1, 8):
        cr = Lc[:, cch - 1, CL - 1:CL, 0, :].broadcast_to([128, CL, 16])
        ci = Lc[:, cch - 1, CL - 1:CL, 1, :].broadcast_to([128, CL, 16])
        f = 0
        P.add("dve", lambda e, cr=cr, f=f: e.tensor_tensor(F1[f][:], pwr, cr, ALU.mult),
              reads=LG, writes=[("F1", f)])
        P.add("dve", lambda e, ci=ci, f=f: e.tensor_tensor(F2[f][:], pwi, ci, ALU.mult),
              reads=LG, writes=[("F2", f)])
        P.add("dve", lambda e, ci=ci, f=f: e.tensor_tensor(F3[f][:], pwr, ci, ALU.mult),
              reads=LG, writes=[("F3", f)])
        P.add("dve", lambda e, cr=cr, f=f: e.tensor_tensor(F4[f][:], pwi, cr, ALU.mult),
              reads=LG, writes=[("F4", f)])
        P.add("dve", lambda e, f=f: e.tensor_tensor(F1[f][:], F1[f][:], F2[f][:], ALU.subtract),
              reads=[("F2", f)], writes=[("F1", f)])
        P.add("dve", lambda e, f=f: e.tensor_tensor(F3[f][:], F3[f][:], F4[f][:], ALU.add),
              reads=[("F4", f)], writes=[("F3", f)])
        P.add("dve", lambda e, cch=cch, f=f: e.tensor_tensor(Lc[:, cch, :, 0, :], Lc[:, cch, :, 0, :], F1[f][:], ALU.add),
              reads=[("F1", f)], writes=LG)
        P.add("dve", lambda e, cch=cch, f=f: e.tensor_tensor(Lc[:, cch, :, 1, :], Lc[:, cch, :, 1, :], F3[f][:], ALU.add),
              reads=[("F3", f)], writes=LG)
    for ri in range(2):
        dst = K.Sbf[:, ri, :, 1:257]
        srcv = L[:, 0:256, ri, :].rearrange("p n g -> p g n")
        P.add("act" if ri == 0 else "dve",
              (lambda e, dst=dst, srcv=srcv: e.activation(dst, srcv, AF.Copy)) if ri == 0 else
              (lambda e, dst=dst, srcv=srcv: e.tensor_copy(dst, srcv)),
              reads=[("Lg", 0), ("Lg", 1), "Sbf0"], writes=[("Sbf", ri)])


def emit_s5b(ph, c, K):
    P = ph.P
    K.gT = ph.sb("gT", [128, 4, T], BF16, at=M_GT)
    K.Ubar = ph.sb("Ubar", [128, 32, 256], BF16, at=M_UBAR)
    K.Sbf = ph.sb("Sbf", [128, 2, 16, 257], BF16, at=M_SBF)
    ph.cur = M_TMP
    E = ph.sb("E", [128, 64 * 128], BF16)
    ttl = ph.sb("ttl", [128, 32, 128], BF16)
    yp = [ph.sb(f"yp{a}", [128, 2, 16, 128], BF16) for a in range(2)]
    gst = [ph.sb(f"gst{i}", [128, 8, 256], BF16) for i in range(2)]
    psY = [ph.ps(f"psY{i}", [128, 512]) for i in range(2)]
    psG = [ph.ps(f"psG{i}", [128, 512]) for i in range(2)]
    P.add("sp", lambda e: e.dma_start(out=ttl[:].rearrange("p g q -> p (g q)"), in_=c.ttb),
          writes=["ttl"], dma=True)
    for a in range(2):
        P.add("sp", lambda e, a=a: e.dma_start(out=yp[a][:].rearrange("p r g q -> p (r g q)"), in_=c.ypb[a]),
              writes=["yp"], dma=True)
    P.add("sp", lambda e: e.dma_start(out=E[:], in_=c.eb), writes=["E"], dma=True)
    gTv = K.gT[:].rearrange("p j (n s) -> p j s n", s=8)
    for j in range(4):
        gs = gst[j % 2]
        for glo in range(8):
            g = 8 * j + glo
            gp, g2 = g // 2, g % 2
            rows = slice(g2 * 64, (g2 + 1) * 64)
            bank = psY[g % 2]
            P.add("pe", lambda e, bank=bank, g=g: e.matmul(
                bank[:, 0:256], ttl[:, g, :], K.Ubar[:, g, :], start=True, stop=False),
                reads=["ttl"], writes=[("psY", g % 2)])
            for ri in range(2):
                P.add("pe", lambda e, bank=bank, g2=g2, ri=ri, gp=gp: e.matmul(
                    bank[:, 0:256], yp[g2][:, ri, gp, :], K.Sbf[:, ri, gp, 0:256],
                    start=False, stop=(ri == 1)),
                    reads=["yp"], writes=[("psY", g % 2)])
            P.add("act", lambda e, gs=gs, glo=glo, bank=bank: e.activation(
                gs[:, glo, :], bank[:, 0:256], AF.Gelu_apprx_tanh),
                writes=[("psY", g % 2), ("gst", j % 2, glo)])
        for tau in range(8):
            bank = psG[tau % 2]
            for glo in range(8):
                idx = tau * 8 + glo
                P.add("pe", lambda e, bank=bank, idx=idx, gs=gs, glo=glo: e.matmul(
                    bank[:, 0:256], E[:, idx * 128:(idx + 1) * 128], gs[:, glo, :],
                    start=(glo == 0), stop=(glo == 7)),
                    reads=["E", ("gst", j % 2, glo)], writes=[("psG", tau % 2)])
            dst = gTv[:, j, tau, :]
            if tau % 2 == 0:
                P.add("act", lambda e, dst=dst, bank=bank: e.activation(dst, bank[:, 0:256], AF.Copy),
                      writes=[("psG", tau % 2), ("gT", j)])
            else:
                P.add("dve", lambda e, dst=dst, bank=bank: e.tensor_copy(dst, bank[:, 0:256]),
                      writes=[("psG", tau % 2), ("gT", j)])


def emit_mixout(ph, c, K):
    P = ph.P
    K.gT = ph.sb("gT", [128, 4, T], BF16, at=M_GT)
    K.oTa = ph.sb("oTa", [128, 4, T], BF16, at=M_OTA)
    ph.cur = COMMON_END + 16384
    wgl = ph.sb("wgl", [128, 4, 4 * 128], BF16)
    NW = 4
    wo = [ph.sb(f"wo{i}", [128, 8 * 128], BF16) for i in range(NW)]
    K.sq = ph.sb("sq", [128, 8, TT], BF16)
    osT = ph.sb("osT", [128, 4, TT], F32)
    onT = ph.sb("onT", [128, 4, TT], BF16)
    K.rt = ph.sb("rt", [128, TT], F32)
    K.rstd = ph.sb("rstd", [128, TT], F32)
    K.rstd2 = ph.sb("rstd2", [128, TT], F32)
    sig = [ph.sb(f"sig{i}", [128, TT], F32) for i in range(2)]
    sqA = ph.sb("sqA", [128, 4, TT], BF16)
    rtA = ph.sb("rtA", [128, TT], F32)
    assert ph.cur <= M_OTA
    ph.cur = M_TMP
    K.yT = ph.sb("yT", [128, 8, TT], F32)
    tmp = [ph.sb(f"tmp{i}", [128, TT], F32) for i in range(8)]
    psA = [ph.ps(f"psA{i}", [128, 512]) for i in range(2)]
    psC = [ph.ps(f"psC{i}", [128, 512]) for i in range(4)]
    K.ps_stat = ph.ps("ps_stat", [128, 512])
    for m in range(4):
        P.add("sp", lambda e, m=m: e.dma_start(out=wgl[:, m, :], in_=c.wglub[m]), writes=[("wgl", m)], dma=True)
    st = {"wl": 0}

    def glu_a(tt):
        ts = slice(tt * TT, (tt + 1) * TT)
        for m in range(4):
            pa = psA[m % 2]
            for cc in range(4):
                P.add("pe", lambda e, pa=pa, m=m, cc=cc: e.matmul(
                    pa[:], wgl[:, m, cc * 128:(cc + 1) * 128], K.gT[:, cc, ts], start=(cc == 0), stop=(cc == 3)),
                    reads=[("wgl", m)], writes=[("psA", m % 2)])
            sg = sig[m % 2]
            P.add("act", lambda e, sg=sg, pa=pa, m=m: e.activation(
                sg[:], pa[:], AF.Sigmoid, bias=K.vecs[:, m:m + 1]),
                writes=[("psA", m % 2), ("sig", m % 2)])
            P.add("dve", lambda e, sg=sg, m=m: e.tensor_tensor(osT[:, m, :], K.gT[:, m, ts], sg[:], ALU.mult),
                  reads=[("sig", m % 2)], writes=[("osT", m)])
            P.add("act", lambda e, m=m: e.activation(sqA[:, m, :], osT[:, m, :], AF.Square),
                  reads=[("osT", m)], writes=[("sqA", m)])

    def glu_b(tt):
        for cc in range(4):
            P.add("pe", lambda e, cc=cc: e.matmul(K.ps_stat[:], K.ones[:], sqA[:, cc, :], start=(cc == 0), stop=(cc == 3)),
                  reads=[("sqA", cc), "ones"], writes=["ps_stat"])
        P.add("act", lambda e: e.activation(rtA[:], K.ps_stat[:], AF.Sqrt, bias=K.epsb[:], scale=1.0 / 512),
              reads=["epsb"], writes=["ps_stat", "rtA"])
        P.add("dve", lambda e: e.reciprocal(K.rstd[:], rtA[:]), reads=["rtA"], writes=["rstd"])
        for m in range(4):
            P.add("dve", lambda e, m=m: e.scalar_tensor_tensor(
                out=onT[:, m, :], in0=osT[:, m, :], scalar=K.vecs[:, 4 + m:5 + m], in1=K.rstd[:],
                op0=ALU.mult, op1=ALU.mult),
                reads=[("osT", m), "rstd"], writes=[("onT", m)])

    def outproj(tt):
        ts = slice(tt * TT, (tt + 1) * TT)
        for m in range(8):
            slot = st["wl"] % NW
            st["wl"] += 1
            wt = wo[slot]
            P.add("sp", lambda e, wt=wt, m=m: e.dma_start(out=wt[:], in_=c.woutb[m]),
                  writes=[("wo", slot)], dma=True)
            py = psC[m % 4]
            for cc in range(8):
                rhs = K.oTa[:, cc, ts] if cc < 4 else onT[:, cc - 4, :]
                P.add("pe", lambda e, wt=wt, cc=cc, py=py, rhs=rhs: e.matmul(
                    py[:], wt[:, cc * 128:(cc + 1) * 128], rhs, start=(cc == 0), stop=(cc == 7)),
                    reads=[("wo", slot)] + ([("onT", cc - 4)] if cc >= 4 else []), writes=[("psC", m % 4)])
            P.add("act", lambda e, m=m, py=py: e.activation(K.yT[:, m, :], py[:], AF.Copy),
                  reads=[("psC", m % 4)], writes=[("yT", m)])
            P.add("act", lambda e, m=m: e.activation(K.sq[:, m, :], K.yT[:, m, :], AF.Square),
                  reads=[("yT", m)], writes=[("sq", m)])

    def post_stats(tt):
        emit_rstd(P, K, K.ps_stat, "sq", D, K.rstd2, "rstd2")

    def post_apply(tt):
        ts = slice(tt * TT, (tt + 1) * TT)
        for m in range(8):
            tm_ = tmp[m]
            P.add("dve", lambda e, m=m, tm_=tm_: e.scalar_tensor_tensor(
                out=tm_[:], in0=K.yT[:, m, :], scalar=K.gains[:, G_MIX_POST, m:m + 1],
                in1=K.rstd2[:], op0=ALU.mult, op1=ALU.mult),
                reads=[("yT", m), "rstd2"], writes=[("tmp", m)])
            P.add("pool", lambda e, m=m, tm_=tm_: e.tensor_tensor(
                K.XT[:, m, ts], K.XT[:, m, ts], tm_[:], ALU.add),
                reads=[("tmp", m)], writes=[("XT", tt, m)])

    glu_a(0)
    glu_b(0)
    if NTT > 1:
        glu_a(1)
    for tt in range(NTT):
        outproj(tt)
        if tt + 1 < NTT:
            glu_b(tt + 1)
        if tt + 2 < NTT:
            glu_a(tt + 2)
        post_stats(tt)
        post_apply(tt)


def alloc_ffn_phase(ph, c, K):
    alloc_common(ph, c, K)
    alloc_ffn(ph, K)


def build_nc(stage="full"):
    nc = bass.Bass("TRN2", target_bir_lowering=False)
    debug = stage not in ("full",)
    c = declare_dram(nc, debug=debug)
    first = True
    ges = ExitStack()
    G = Globals(nc, ges)
    full_like = stage == "full"
    if full_like:
        with Phase(nc, G, "start") as ph:
            K = Ctx()
            alloc_common(ph, c, K)
            K.xio = [ph.sb(f"xio{i}", [128, D], F32) for i in range(2)]
            K.psA = [ph.ps(f"psA{i}", [128, 512]) for i in range(2)]
            emit_consts(ph, c, K)
            gen = load_x_blocks(ph, c, K, 0, act_only=True)
            next(gen, None)
            emit_s5_setup(ph, c, K, hook=lambda: next(gen, None))
            for _ in gen:
                pass
            ph.finish()
    for seq in range(NSEQ):
        with Phase(nc, G, f"f1s{seq}") as ph:
            K = Ctx()
            alloc_ffn_phase(ph, c, K)
            if first and not full_like:
                emit_consts(ph, c, K)
                if stage in ("att", "mix"):
                    emit_cast_mixer_weights(ph, c)
                elif stage != "io":
                    emit_cast_weights(ph, c, 0)
                    emit_cast_mixer_weights(ph, c)
                    emit_cast_weights(ph, c, 1)
            xg = None
            if not (full_like and seq == 0):
                if full_like:
                    xg = load_x_blocks(ph, c, K, seq)
                    for _ in range(4):
                        next(xg, None)
                else:
                    emit_load_x(ph, c, K, seq)
            if stage not in ("io", "iocast", "att", "mix"):
                emit_ffn(ph, c, K, 0, G_FF1_PRE, G_FF1_POST, direct_cast=(full_like and seq == 0), x_gen=xg)
            if stage in ("ffn1", "io", "iocast"):
                emit_store_x(ph, c, K, seq)
            ph.finish()
        first = False
        if stage in ("ffn1", "io", "iocast"):
            continue
        KMIX = int(os.environ.get("KMIX", "9"))
        if seq == 0 and stage != "att" and KMIX >= 1 and not full_like:
            with Phase(nc, G, "s5set") as ph:
                K = Ctx()
                alloc_common(ph, c, K)
                emit_s5_setup(ph, c, K)
                ph.finish()
        with Phase(nc, G, f"pjs{seq}") as ph:
            K = Ctx()
            alloc_common(ph, c, K)
            emit_proj(ph, c, K)
            ph.finish()
        with Phase(nc, G, f"ats{seq}") as ph:
            K = Ctx()
            alloc_common(ph, c, K)
            emit_attn(ph, c, K)
            if stage == "att":
                ph.P.add("sp", lambda e, K=K, seq=seq: e.dma_start(
                    out=c.dbg[seq], in_=K.oTa[:].rearrange("p h t -> p (h t)")),
                    reads=[("oTa", tb) for tb in range(16)], writes=["dbg"], dma=True)
            ph.finish()
        if stage == "att":
            continue
        if KMIX >= 2:
          with Phase(nc, G, f"sas{seq}") as ph:
            K = Ctx()
            alloc_common(ph, c, K)
            emit_s5a(ph, c, K)
            if os.environ.get("KDUMP") == "Ubar":
                ph.P.add("sp", lambda e, K=K, seq=seq: e.dma_start(
                    out=c.dbg[seq], in_=K.Ubar[:].rearrange("p g n -> p (g n)")),
                    reads=[("Ubar", g) for g in range(32)], writes=["dbg"], dma=True)
            if os.environ.get("KDUMP") == "Sbf":
                ph.P.add("sp", lambda e, K=K, seq=seq: e.dma_start(
                    out=c.dbg[seq].rearrange("p (a n) -> p a n", n=256),
                    in_=K.Sbf[:, :, :, 1:257].rearrange("p r g n -> p (r g) n")),
                    reads=[("Sbf", 0), ("Sbf", 1)], writes=["dbg"], dma=True)
            ph.finish()
        if KMIX >= 3:
          with Phase(nc, G, f"sbs{seq}") as ph:
            K = Ctx()
            alloc_common(ph, c, K)
            emit_s5b(ph, c, K)
            if os.environ.get("KDUMP") == "gT":
                ph.P.add("sp", lambda e, K=K, seq=seq: e.dma_start(
                    out=c.dbg[seq], in_=K.gT[:].rearrange("p j t -> p (j t)")),
                    reads=[("gT", j) for j in range(4)], writes=["dbg"], dma=True)
            ph.finish()
        if KMIX >= 4:
          with Phase(nc, G, f"mos{seq}") as ph:
            K = Ctx()
            alloc_common(ph, c, K)
            emit_mixout(ph, c, K)
            ph.finish()
        with Phase(nc, G, f"f2s{seq}") as ph:
            K = Ctx()
            alloc_ffn_phase(ph, c, K)
            if stage != "mix":
                emit_ffn(ph, c, K, 1, G_FF2_PRE, G_FF2_POST)
            emit_store_x(ph, c, K, seq)
            ph.finish()
    ges.close()
    return nc


def host_layout(inp):
    f = lambda a: np.ascontiguousarray(np.asarray(a, dtype=np.float32))
    com = {}
    com["ident"] = np.eye(128, dtype=np.float32)
    kk = np.arange(128)[:, None]
    qq = np.arange(128)[None, :]
    com["maskneg"] = np.where(kk <= qq, 0.0, -30000.0).astype(np.float32)
    gl = [inp["ff1_pre_g"], inp["ff1_post_g"], inp["mix_pre_g"], inp["mix_post_g"],
          inp["ff2_pre_g"], inp["ff2_post_g"]]
    gains = np.zeros((128, 8, 8), np.float32)
    for i, g in enumerate(gl):
        gains[:, i, :] = f(g).reshape(8, 128).T
    com["gains"] = gains
    lamv = np.stack([f(inp[k])[0] for k in ("lambda_q1", "lambda_k1", "lambda_q2", "lambda_k2")], 0)
    com["lamv"] = np.ascontiguousarray(np.broadcast_to(lamv[None], (128, 4, 64)))
    com["gsub"] = np.ascontiguousarray(np.broadcast_to(f(inp["attn_subln_g"])[0][None], (128, 128)))
    vecs = np.zeros((128, 8), np.float32)
    vecs[:, 0:4] = f(inp["ssm_b_glu"])[0].reshape(4, 128).T
    vecs[:, 4:8] = f(inp["ssm_norm_g"])[0].reshape(4, 128).T
    com["vecs"] = vecs
    for i, pre in enumerate(("ff1", "ff2")):
        wg = f(inp[pre + "_w_gate"])[0]
        wu = f(inp[pre + "_w_up"])[0]
        wd = f(inp[pre + "_w_down"])[0]
        g4 = wg.reshape(8, 128, NF, 128).transpose(2, 1, 0, 3)
        u4 = wu.reshape(8, 128, NF, 128).transpose(2, 1, 0, 3)
        com[f"wgu{i}"] = np.ascontiguousarray(np.stack([g4, u4], axis=2)).reshape(NF, 128, 2 * 8 * 128)
        d4 = wd.reshape(NF, 128, 8, 128).transpose(2, 1, 0, 3)
        com[f"wd{i}"] = np.ascontiguousarray(d4).reshape(8, 128, NF * 128)
    win = f(inp["w_in"])[0]
    w4 = win.reshape(8, 128, 16, 128).transpose(2, 1, 0, 3)
    sel = list(range(8)) + list(range(12, 16))
    com["winf"] = np.ascontiguousarray(w4[sel]).reshape(12, 128, 8 * 128)
    wv = win[:, 1024:1536].reshape(8, 128, 512).transpose(1, 0, 2)
    com["winv"] = np.ascontiguousarray(wv).reshape(128, 8 * 512)
    wo = f(inp["w_out"])[0]
    o4 = wo.reshape(8, 128, 8, 128).transpose(2, 1, 0, 3)
    com["wout"] = np.ascontiguousarray(o4).reshape(8, 128, 8 * 128)
    wgl = f(inp["ssm_w_glu"])[0]
    l4 = wgl.reshape(4, 128, 4, 128).transpose(2, 1, 0, 3)
    com["wglu"] = np.ascontiguousarray(l4).reshape(4, 128, 4 * 128)
    a_re, a_im = f(inp["ssm_a_re"])[0], f(inp["ssm_a_im"])[0]
    ldt = f(inp["ssm_log_dt"])[0]
    b_re, b_im = f(inp["ssm_b_re"])[0], f(inp["ssm_b_im"])[0]
    c_re, c_im = f(inp["ssm_c_re"])[0], f(inp["ssm_c_im"])[0]
    dsk = f(inp["ssm_d"])[0]
    s5p = np.zeros((128, S5P_COLS), np.float32)
    lay_a = lambda a: a.reshape(16, 2, 64).transpose(1, 2, 0).reshape(128, 16)
    s5p[:, 0:16] = lay_a(a_re)
    s5p[:, 16:32] = lay_a(a_im)
    s5p[:, 32:48] = np.broadcast_to(ldt.reshape(16, 2).T[:, None, :], (2, 64, 16)).reshape(128, 16)
    lay_b = lambda b: b.reshape(16, 2, 64, 16).transpose(1, 2, 0, 3).reshape(128, 256)
    lay_c = lambda cc: cc.reshape(16, 2, 16, 64).transpose(1, 3, 0, 2).reshape(128, 256)
    s5p[:, 48:304] = lay_b(b_re)
    s5p[:, 304:560] = lay_b(b_im)
    s5p[:, 560:816] = lay_c(c_re)
    s5p[:, 816:1072] = lay_c(c_im)
    kk = np.stack([7.0 - np.arange(8), np.arange(8) - 7.0, np.arange(8) + 1.0]).astype(np.float32)
    s5p[:, 1072:1096] = kk.reshape(1, 24)
    s5p[:, 1096:1128] = np.broadcast_to(dsk.reshape(32, 16).T[None], (8, 16, 32)).reshape(128, 32)
    com["s5p"] = s5p
    sg = np.arange(128) // 16
    com["cmask"] = (sg[None, :] >= sg[:, None]).astype(np.float32)
    r = np.arange(128)
    esel = np.zeros((128, 64, 128), np.float32)
    for glo in range(8):
        for sgm in range(8):
            m = ((r[:, None] // 16 == glo) & (r[:, None] % 16 == r[None, :] % 16) & (r[None, :] // 16 == sgm))
            esel[:, glo * 8 + sgm, :] = m
    com["esel"] = esel.reshape(128, 64 * 128)
    return com


_NC_CACHE = {}


def kernel(**inputs):
    x = np.ascontiguousarray(np.asarray(inputs["x"], dtype=np.float32))
    com = host_layout(inputs)
    if "full" not in _NC_CACHE:
        _NC_CACHE["full"] = build_nc("full")
    nc = _NC_CACHE["full"]
    in_maps = []
    for i in range(NCORES):
        m = dict(com)
        m["x"] = x[i * NSEQ:(i + 1) * NSEQ]
        in_maps.append(m)
    res = run_bass_kernel_spmd(nc, in_maps, core_ids=list(range(NCORES)))
    out = np.concatenate([np.asarray(r["out"]) for r in res.results], axis=0)
    return out.astype(np.float32)
```

```python
import math
import os
from contextlib import ExitStack

import numpy as np
import concourse.bass as bass
import concourse.mybir as mybir
from concourse.bass_utils import run_bass_kernel_spmd

F32 = mybir.dt.float32
BF16 = mybir.dt.bfloat16
AF = mybir.ActivationFunctionType
ALU = mybir.AluOpType

D = 1024
DFF = 2816
NF = DFF // 128
T = 2048
NSEQ = 2
TT = 512
NTT = T // TT
EPS = 1e-6
NCORES = 8

ENGS = ("pe", "act", "dve", "pool", "sp")


class Op:
    __slots__ = ("eng", "fn", "deps", "is_dma", "sig", "sem", "val", "idx", "pre")

    def __init__(self, eng, fn, is_dma):
        self.eng = eng
        self.fn = fn
        self.is_dma = is_dma
        self.deps = set()
        self.sig = False
        self.sem = None
        self.val = 0
        self.pre = None


class Prog:
    def __init__(self, G):
        self.G = G
        self.ops = []
        self.last_w = {}
        self.readers = {}

    def add(self, eng, fn, reads=(), writes=(), dma=False):
        op = Op(eng, fn, dma)
        op.idx = len(self.ops)
        for r in reads:
            w = self.last_w.get(r)
            if w is not None:
                op.deps.add(w)
        for r in writes:
            w = self.last_w.get(r)
            if w is not None:
                op.deps.add(w)
            for q in self.readers.get(r, ()):
                op.deps.add(q)
        for r in reads:
            self.readers.setdefault(r, []).append(op.idx)
        for r in writes:
            self.last_w[r] = op.idx
            self.readers[r] = []
        op.deps.discard(op.idx)
        self.ops.append(op)
        return op.idx

    def emit(self, block):
        ops = self.ops
        G = self.G
        sems, dmasems = G.sems, G.dmasems

        def skip(dop, op):
            return (dop.eng == "pe" and op.eng == "pe" and not dop.is_dma and not op.is_dma)

        for op in ops:
            best = {}
            keep = set()
            for d in op.deps:
                dop = ops[d]
                if skip(dop, op):
                    continue
                if dop.is_dma:
                    keep.add(d)
                elif best.get(dop.eng, -1) < d:
                    best[dop.eng] = d
            keep.update(best.values())
            op.deps = keep
            for d in keep:
                ops[d].sig = True
        last_compute = {}
        for op in ops:
            if not op.is_dma and op.eng in ("pe", "act", "dve", "pool"):
                last_compute[op.eng] = op
        for op in last_compute.values():
            op.sig = True
        cnt, dcnt = G.cnt, G.dcnt
        for op in ops:
            if op.is_dma:
                i = dcnt[op.eng]
                dcnt[op.eng] += 1
                ring = dmasems[op.eng]
                op.sem = ring[i % len(ring)]
                op.val = 16 * (i // len(ring) + 1)
                op.pre = (op.sem, op.val - 16)
            elif op.sig:
                cnt[op.eng] += 1
                op.sem = sems[op.eng]
                op.val = cnt[op.eng]
        per_eng = {e: [o for o in ops if o.eng == e] for e in ENGS}
        if os.environ.get("KVERB"):
            print("sem counts", cnt, "dma counts", dcnt, "nops", len(ops))

        def run(engname, eng):
            waited = G.waited[engname]
            for op in per_eng[engname]:
                need = {}
                for d in op.deps:
                    dop = ops[d]
                    if dop.sem is None:
                        continue
                    k = id(dop.sem)
                    if need.get(k, (None, 0))[1] < dop.val:
                        need[k] = (dop.sem, dop.val)
                if op.pre is not None and op.pre[1] > 0:
                    k = id(op.pre[0])
                    if need.get(k, (None, 0))[1] < op.pre[1]:
                        need[k] = op.pre
                for k, (s, v) in need.items():
                    if waited.get(k, 0) < v:
                        eng.wait_ge(s, v)
                        waited[k] = v
                inst = op.fn(eng)
                if op.is_dma:
                    inst.then_inc(op.sem, 16)
                elif op.sig:
                    inst.then_inc(op.sem, 1)
            last = {}
            for op in per_eng[engname]:
                if op.is_dma:
                    last[id(op.sem)] = (op.sem, op.val)
            for k, (s, v) in last.items():
                if waited.get(k, 0) < v:
                    eng.wait_ge(s, v)
                    waited[k] = v
            lc = last_compute.get(engname)
            if lc is not None and waited.get(id(lc.sem), 0) < lc.val:
                eng.wait_ge(lc.sem, lc.val)
                waited[id(lc.sem)] = lc.val

        @block.tensor
        def _(e):
            run("pe", e)

        @block.scalar
        def _(e):
            run("act", e)

        @block.vector
        def _(e):
            run("dve", e)

        @block.gpsimd
        def _(e):
            run("pool", e)

        @block.sync
        def _(e):
            run("sp", e)


class Globals:
    def __init__(self, nc, es):
        self.sems = {e: es.enter_context(nc.semaphore(f"s_{e}")) for e in ENGS}
        ring = {"sp": 24, "pool": 8, "act": 4, "pe": 1, "dve": 1}
        self.dmasems = {e: [es.enter_context(nc.semaphore(f"d_{e}{i}")) for i in range(ring[e])]
                        for e in ENGS}
        self.cnt = {e: 0 for e in ENGS}
        self.dcnt = {e: 0 for e in ENGS}
        self.waited = {e: {} for e in ENGS}
        self.uid = 0
        self.base = (nc.sbuf_base + 31) // 32 * 32
        self.arena = es.enter_context(nc.sbuf_tensor("arena", [128, SB_LIMIT // 4], F32))


SB_LIMIT = 207 * 1024
DT_SIZE = {F32: 4, BF16: 2}


class Phase:
    def __init__(self, nc, G, name):
        self.nc = nc
        self.G = G
        self.name = name
        self.es = ExitStack()
        self.P = Prog(G)
        self.cur = 0

    def __enter__(self):
        self.es.__enter__()
        return self

    def sb(self, name, shape, dt, at=None):
        n = 1
        for d in shape[1:]:
            n *= d
        size = (n * DT_SIZE[dt] + 31) // 32 * 32
        if at is None:
            at = self.cur
        self.cur = max(self.cur, at + size)
        assert at + size <= SB_LIMIT, (self.name, name, at + size)
        self.G.uid += 1
        return self.nc.alloc_sbuf_tensor_at(f"{self.name}_{name}_{self.G.uid}", shape, dt,
                                            offset=self.G.base + at)

    def ps(self, name, shape, dt=F32):
        return self.es.enter_context(self.nc.psum_tensor(f"{self.name}_{name}", shape, dt))

    def finish(self):
        with self.nc.Block(no_gpsimd_drain=True) as block:
            self.P.emit(block)

    def __exit__(self, *a):
        return self.es.__exit__(*a)


class Ctx:
    pass


def declare_dram(nc, debug=False):
    c = Ctx()
    di = lambda n, s, d=F32: nc.dram_tensor(n, s, d, kind="ExternalInput").ap()
    sc = lambda n, s, d=BF16: nc.dram_tensor(n, s, d).ap()
    c.x = di("x", [NSEQ, T, D])
    c.out = nc.dram_tensor("out", [NSEQ, T, D], F32, kind="ExternalOutput").ap()
    if debug:
        c.dbg = nc.dram_tensor("dbg", [NSEQ, 128, 4 * T], BF16, kind="ExternalOutput").ap()
    c.ident = di("ident", [128, 128])
    c.maskneg = di("maskneg", [128, 128])
    c.gains = di("gains", [128, 8, 8])
    c.lamv = di("lamv", [128, 4, 64])
    c.gsub = di("gsub", [128, 128])
    c.vecs = di("vecs", [128, 8])
    c.wgu = [di(f"wgu{i}", [NF, 128, 2 * 8 * 128]) for i in range(2)]
    c.wd = [di(f"wd{i}", [8, 128, NF * 128]) for i in range(2)]
    c.wgub = [sc(f"wgub{i}", [NF, 128, 2 * 8 * 128]) for i in range(2)]
    c.wdb = [sc(f"wdb{i}", [8, 128, NF * 128]) for i in range(2)]
    c.winf = di("winf", [12, 128, 8 * 128])
    c.winv = di("winv", [128, 8 * 512])
    c.wout = di("wout", [8, 128, 8 * 128])
    c.wglu = di("wglu", [4, 128, 4 * 128])
    c.winfb = sc("winfb", [12, 128, 8 * 128])
    c.winvb = sc("winvb", [128, 8 * 512])
    c.woutb = sc("woutb", [8, 128, 8 * 128])
    c.wglub = sc("wglub", [4, 128, 4 * 128])
    c.s5p = di("s5p", [128, S5P_COLS])
    c.cmask = di("cmask", [128, 128])
    c.esel = di("esel", [128, 64 * 128])
    c.eb = sc("eb", [128, 64 * 128])
    c.xt7b = sc("xt7b", [128, 2 * 32 * 64])
    c.ttb = sc("ttb", [128, 32 * 128])
    c.ypb = sc("ypb", [2, 128, 2 * 16 * 128])
    return c


S5P_COLS = 1128
G_FF1_PRE, G_FF1_POST, G_MIX_PRE, G_MIX_POST, G_FF2_PRE, G_FF2_POST = range(6)


def emit_cast_weights(ph, c, which):
    P = ph.P
    for f in range(NF):
        P.add("pool", lambda e, f=f: e.dma_start(out=c.wgub[which][f], in_=c.wgu[which][f]),
              writes=[("wgub", which, f)], dma=True)
    for m in range(8):
        src = c.wd[which][m].rearrange("p (a b) -> p a b", a=2)
        dst = c.wdb[which][m].rearrange("p (a b) -> p a b", a=2)
        P.add("pool", lambda e, src=src, dst=dst: e.dma_start(out=dst, in_=src),
              writes=[("wdb", which, m)], dma=True)


def emit_cast_mixer_weights(ph, c):
    P = ph.P
    for j in range(12):
        P.add("pool", lambda e, j=j: e.dma_start(out=c.winfb[j], in_=c.winf[j]),
              writes=[("winfb", j)], dma=True)
    src = c.winv.rearrange("p (a b) -> p a b", a=2)
    dst = c.winvb.rearrange("p (a b) -> p a b", a=2)
    P.add("pool", lambda e: e.dma_start(out=dst, in_=src), writes=["winvb"], dma=True)
    for m in range(8):
        P.add("pool", lambda e, m=m: e.dma_start(out=c.woutb[m], in_=c.wout[m]),
              writes=[("woutb", m)], dma=True)
    for m in range(4):
        P.add("pool", lambda e, m=m: e.dma_start(out=c.wglub[m], in_=c.wglu[m]),
              writes=[("wglub", m)], dma=True)


def load_x_blocks(ph, c, K, seq, act_only=False):
    P = ph.P
    for tb in range(T // 128):
        slot = tb % 2
        xin = K.xio[slot]
        P.add("sp", lambda e, tb=tb, xin=xin: e.dma_start(out=xin[:], in_=c.x[seq, tb * 128:(tb + 1) * 128, :]),
              writes=[("xio", slot)], dma=True)
        for half in range(2):
            bank = K.psA[(2 * tb + half) % 2]
            bname = ("psA", (2 * tb + half) % 2)
            for j in range(4):
                cc = half * 4 + j
                P.add("pe", lambda e, bank=bank, j=j, cc=cc, xin=xin:
                      e.transpose(bank[:, j * 128:(j + 1) * 128], xin[:, cc * 128:(cc + 1) * 128], K.ident[:]),
                      reads=[("xio", slot), "ident"], writes=[bname])
            eng = "act" if (half == 0 or act_only) else "dve"
            dst = K.XT[:, half * 4:(half + 1) * 4, tb * 128:(tb + 1) * 128]
            srcv = bank[:].rearrange("p (j t) -> p j t", j=4)
            wr = [("XT", tb // 4, q) for q in range(half * 4, half * 4 + 4)]
            if eng == "act":
                P.add("act", lambda e, dst=dst, srcv=srcv: e.activation(dst, srcv, AF.Copy),
                      reads=[bname], writes=wr)
            else:
                P.add("dve", lambda e, dst=dst, srcv=srcv: e.tensor_copy(dst, srcv),
                      reads=[bname], writes=wr)
        yield tb


def emit_load_x(ph, c, K, seq):
    for _ in load_x_blocks(ph, c, K, seq):
        pass


def emit_store_x(ph, c, K, seq):
    P = ph.P
    for tb in range(T // 128):
        slot = tb % 2
        xo = K.xio[slot]
        for half in range(2):
            bank = K.psA[(2 * tb + half) % 2]
            bname = ("psA", (2 * tb + half) % 2)
            for j in range(4):
                cc = half * 4 + j
                P.add("pe", lambda e, bank=bank, j=j, cc=cc, tb=tb:
                      e.transpose(bank[:, j * 128:(j + 1) * 128], K.XT[:, cc, tb * 128:(tb + 1) * 128], K.ident[:]),
                      reads=[("XT", tb // 4, cc), "ident"], writes=[bname])
            dst = xo[:, half * 512:(half + 1) * 512]
            if half == 0:
                P.add("act", lambda e, dst=dst, bank=bank: e.activation(dst, bank[:], AF.Copy),
                      reads=[bname], writes=[("xio", slot)])
            else:
                P.add("dve", lambda e, dst=dst, bank=bank: e.tensor_copy(dst, bank[:]),
                      reads=[bname], writes=[("xio", slot)])
        P.add("sp", lambda e, tb=tb, xo=xo: e.dma_start(out=c.out[seq, tb * 128:(tb + 1) * 128, :], in_=xo[:]),
              reads=[("xio", slot)], writes=[("out", seq, tb)], dma=True)


def emit_rstd(P, K, ps_stat, sq_res, dim, out_rstd, out_name, nch=8):
    for cc in range(nch):
        P.add("pe", lambda e, cc=cc: e.matmul(ps_stat[:], K.ones[:], K.sq[:, cc, :], start=(cc == 0), stop=(cc == nch - 1)),
              reads=[(sq_res, cc), "ones"], writes=["ps_stat"])
    P.add("act", lambda e: e.activation(K.rt[:], ps_stat[:], AF.Sqrt, bias=K.epsb[:], scale=1.0 / dim),
          reads=["ps_stat", "epsb"], writes=["rt"])
    P.add("dve", lambda e: e.reciprocal(out_rstd[:], K.rt[:]), reads=["rt"], writes=[out_name])


import os
DBG = int(os.environ.get("KDBG", "9"))


def emit_ffn(ph, c, K, which, gpre, gpost, direct_cast=False, x_gen=None):
    P = ph.P
    st = {"wl": 0, "dl": 0}

    def prenorm(tt):
        ts = slice(tt * TT, (tt + 1) * TT)
        xn = K.xnT2[tt % 2]
        for cc in range(8):
            P.add("act", lambda e, cc=cc: e.activation(K.sq[:, cc, :], K.XT[:, cc, ts], AF.Square),
                  reads=[("XT", tt, cc)], writes=[("sq", cc)])
        emit_rstd(P, K, K.ps_stat, "sq", D, K.rstd, "rstd")
        for cc in range(8):
            P.add("dve", lambda e, cc=cc: e.scalar_tensor_tensor(
                out=xn[:, cc, :], in0=K.XT[:, cc, ts], scalar=K.gains[:, gpre, cc:cc + 1],
                in1=K.rstd[:], op0=ALU.mult, op1=ALU.mult),
                reads=[("XT", tt, cc), "rstd", "gains"], writes=[("xnT", tt % 2, cc)])

    def gateup(tt, f):
        xn = K.xnT2[tt % 2]
        slot = st["wl"] % K.NWGU
        st["wl"] += 1
        wt = K.wgu[slot]
        if direct_cast and tt == 0:
            P.add("pool", lambda e: e.dma_start(out=wt[:], in_=c.wgu[which][f]),
                  writes=[("wgu", slot)], dma=True)
            P.add("sp", lambda e: e.dma_start(out=c.wgub[which][f], in_=wt[:]),
                  reads=[("wgu", slot)], writes=[("wgub", which, f)], dma=True)
        else:
            P.add("sp", lambda e: e.dma_start(out=wt[:], in_=c.wgub[which][f]),
                  reads=[("wgub", which, f)], writes=[("wgu", slot)], dma=True)
        pg = K.psA[f % 2]
        pu = K.psB[f % 2]
        for cc in range(8):
            P.add("pe", lambda e, cc=cc: e.matmul(
                pg[:], wt[:, cc * 128:(cc + 1) * 128], xn[:, cc, :], start=(cc == 0), stop=(cc == 7)),
                reads=[("wgu", slot), ("xnT", tt % 2, cc)], writes=[("psA", f % 2)])
        for cc in range(8):
            P.add("pe", lambda e, cc=cc: e.matmul(
                pu[:], wt[:, (8 + cc) * 128:(9 + cc) * 128], xn[:, cc, :], start=(cc == 0), stop=(cc == 7)),
                reads=[("wgu", slot), ("xnT", tt % 2, cc)], writes=[("psB", f % 2)])
        sg = K.sg[f % 2]
        P.add("act", lambda e: e.activation(sg[:], pg[:], AF.Silu),
              reads=[("psA", f % 2)], writes=[("sg", f % 2)])
        P.add("dve", lambda e: e.tensor_tensor(K.hT[:, f, :], sg[:], pu[:], ALU.mult),
              reads=[("sg", f % 2), ("psB", f % 2)], writes=[("hT", f)])

    def down(tt, m):
        slot = st["dl"] % K.NWD
        st["dl"] += 1
        wt = K.wd[slot]
        if direct_cast and tt == 0:
            P.add("pool", lambda e: e.dma_start(out=wt[:].rearrange("p (a b) -> p a b", a=2),
                                                in_=c.wd[which][m].rearrange("p (a b) -> p a b", a=2)),
                  writes=[("wd", slot)], dma=True)
            P.add("sp", lambda e: e.dma_start(out=c.wdb[which][m], in_=wt[:]),
                  reads=[("wd", slot)], writes=[("wdb", which, m)], dma=True)
        else:
            P.add("sp", lambda e: e.dma_start(out=wt[:], in_=c.wdb[which][m]),
                  reads=[("wdb", which, m)], writes=[("wd", slot)], dma=True)
        py = K.psC[m % 2]
        for f in range(NF):
            P.add("pe", lambda e, f=f: e.matmul(
                py[:], wt[:, f * 128:(f + 1) * 128], K.hT[:, f, :], start=(f == 0), stop=(f == NF - 1)),
                reads=[("wd", slot), ("hT", f)], writes=[("psC", m % 2)])
        P.add("dve", lambda e: e.tensor_copy(K.yT[:, m, :], py[:]),
              reads=[("psC", m % 2)], writes=[("yT", m)])
        P.add("act", lambda e: e.activation(K.sq[:, m, :], K.yT[:, m, :], AF.Square),
              reads=[("yT", m)], writes=[("sq", m)])

    def post_piece(tt, k):
        ts = slice(tt * TT, (tt + 1) * TT)
        if k == 0:
            emit_rstd(P, K, K.ps_stat, "sq", D, K.rstd2, "rstd2")
            return
        for m in (k - 1,):
            tmp = K.tmp[m % 4]
            P.add("dve", lambda e, m=m, tmp=tmp: e.scalar_tensor_tensor(
                out=tmp[:], in0=K.yT[:, m, :], scalar=K.gains[:, gpost, m:m + 1],
                in1=K.rstd2[:], op0=ALU.mult, op1=ALU.mult),
                reads=[("yT", m), "rstd2", "gains"], writes=[("tmp", m % 4)])
            P.add("pool", lambda e, m=m, tmp=tmp: e.tensor_tensor(
                K.XT[:, m, ts], K.XT[:, m, ts], tmp[:], ALU.add),
                reads=[("tmp", m % 4), ("XT", tt, m)], writes=[("XT", tt, m)])

    prenorm(0)
    for tt in range(NTT):
        for f in range(NF):
            gateup(tt, f)
            if x_gen is not None and tt == 0:
                next(x_gen, None)
            if tt > 0 and f < 9:
                post_piece(tt - 1, f)
            if f == 9 and tt + 1 < NTT:
                prenorm(tt + 1)
        for m in range(8):
            down(tt, m)
        if direct_cast and tt == 0:
            emit_cast_mixer_weights(ph, c)
            emit_cast_weights(ph, c, 1)
    for k in range(9):
        post_piece(NTT - 1, k)


LAM_INIT = 0.8 - 0.6 * math.exp(-0.3 * 0)
COMMON_END = 68 * 1024


def alloc_common(ph, c, K):
    K.XT = ph.sb("XT", [128, 8, T], F32)
    K.ident = ph.sb("ident", [128, 128], F32)
    K.identb = ph.sb("identb", [128, 128], BF16)
    K.maskneg = ph.sb("maskneg", [128, 128], BF16)
    K.gains = ph.sb("gains", [128, 8, 8], F32)
    K.ones = ph.sb("ones", [128, 128], BF16)
    K.epsb = ph.sb("epsb", [128, 1], F32)
    K.nlam = ph.sb("nlam", [128, 1], F32)
    K.gsubb = ph.sb("gsubb", [128, 128], F32)
    K.vecs = ph.sb("vecs", [128, 8], F32)
    K.a8 = ph.sb("a8", [128, 2, 16], F32)
    K.a8b = ph.sb("a8b", [128, 2, 16], F32)
    K.mhalf = ph.sb("mhalf", [128, 1], F32)
    assert ph.cur <= COMMON_END
    ph.cur = COMMON_END


def emit_consts(ph, c, K):
    P = ph.P
    P.add("sp", lambda e: e.dma_start(out=K.ident[:], in_=c.ident), writes=["ident"], dma=True)
    P.add("sp", lambda e: e.dma_start(out=K.gains[:], in_=c.gains), writes=["gains"], dma=True)
    P.add("sp", lambda e: e.dma_start(out=K.gsubb[:], in_=c.gsub), writes=["gsubb"], dma=True)
    P.add("sp", lambda e: e.dma_start(out=K.vecs[:], in_=c.vecs), writes=["vecs"], dma=True)
    P.add("pool", lambda e: e.dma_start(out=K.maskneg[:], in_=c.maskneg), writes=["maskneg"], dma=True)
    P.add("pool", lambda e: e.memset(K.ones[:], 1.0), writes=["ones"])
    P.add("pool", lambda e: e.memset(K.epsb[:], EPS), writes=["epsb"])
    P.add("pool", lambda e: e.memset(K.mhalf[:], -0.5), writes=["mhalf"])
    P.add("dve", lambda e: e.tensor_copy(K.identb[:], K.ident[:]), reads=["ident"], writes=["identb"])
    for g in (G_FF1_POST, G_FF2_POST):
        P.add("dve", lambda e, g=g: e.tensor_scalar(K.gains[:, g, :], K.gains[:, g, :], 0.5, None, ALU.mult),
              reads=["gains"], writes=["gains"])
    P.add("dve", lambda e: e.tensor_scalar(K.gsubb[:], K.gsubb[:], 1.0 - LAM_INIT, None, ALU.mult),
          reads=["gsubb"], writes=["gsubb"])
    lamv = ph.sb("lamv", [128, 4, 64], F32)
    junk = ph.sb("lamjunk", [128, 64], F32)
    ss = ph.sb("lamss", [128, 4], F32)
    P.add("sp", lambda e: e.dma_start(out=lamv[:], in_=c.lamv), writes=["lamv"], dma=True)
    for i in range(2):
        P.add("dve", lambda e, i=i: e.tensor_tensor(junk[:], lamv[:, 2 * i, :], lamv[:, 2 * i + 1, :], ALU.mult),
              reads=["lamv"], writes=["lamjunk"])
        P.add("dve", lambda e, i=i: e.tensor_reduce(ss[:, i:i + 1], junk[:], mybir.AxisListType.X, ALU.add),
              reads=["lamjunk"], writes=[("lamss", i)])
        P.add("act", lambda e, i=i: e.activation(ss[:, 2 + i:3 + i], ss[:, i:i + 1], AF.Exp),
              reads=[("lamss", i)], writes=[("lamss", 2 + i)])
    P.add("dve", lambda e: e.tensor_tensor(K.nlam[:], ss[:, 3:4], ss[:, 2:3], ALU.subtract),
          reads=[("lamss", 2), ("lamss", 3)], writes=["nlam"])
    P.add("dve", lambda e: e.tensor_scalar(K.nlam[:], K.nlam[:], -LAM_INIT, None, ALU.add),
          reads=["nlam"], writes=["nlam"])


def alloc_ffn(ph, K):
    K.xio = [ph.sb(f"xio{i}", [128, D], F32) for i in range(2)]
    K.sq = ph.sb("sq", [128, 8, TT], BF16)
    K.xnT2 = [ph.sb(f"xnT{i}", [128, 8, TT], BF16) for i in range(2)]
    K.hT = ph.sb("hT", [128, NF, TT], BF16)
    K.yT = ph.sb("yT", [128, 8, TT], F32)
    K.rt = ph.sb("rt", [128, TT], F32)
    K.rstd = ph.sb("rstd", [128, TT], F32)
    K.rstd2 = ph.sb("rstd2", [128, TT], F32)
    K.sg = [ph.sb(f"sg{i}", [128, TT], F32) for i in range(2)]
    K.tmp = [ph.sb(f"tmp{i}", [128, TT], F32) for i in range(4)]
    K.NWGU = 6
    K.NWD = 4
    K.wgu = [ph.sb(f"wgu{i}", [128, 2 * 8 * 128], BF16) for i in range(K.NWGU)]
    K.wd = [ph.sb(f"wd{i}", [128, NF * 128], BF16) for i in range(K.NWD)]
    K.psA = [ph.ps(f"psA{i}", [128, 512]) for i in range(2)]
    K.psB = [ph.ps(f"psB{i}", [128, 512]) for i in range(2)]
    K.psC = [ph.ps(f"psC{i}", [128, 512]) for i in range(2)]
    K.ps_stat = ph.ps("ps_stat", [128, 512])


M_UT = COMMON_END + 0
M_QT = COMMON_END + 16384
M_KT = COMMON_END + 32768
M_VA = COMMON_END + 49152
M_OTA = COMMON_END + 65792
M_TMP = COMMON_END + 82176
M_UBAR = COMMON_END + 16384
M_SBF = COMMON_END + 32768
M_GT = COMMON_END + 0


def emit_prenorm(P, K, tt, gidx):
    ts = slice(tt * TT, (tt + 1) * TT)
    XTr = lambda q: ("XT", tt, q)
    xn = K.xnT2[tt % 2]
    for cc in range(8):
        P.add("act", lambda e, cc=cc: e.activation(K.sq[:, cc, :], K.XT[:, cc, ts], AF.Square),
              reads=[XTr(cc)], writes=[("sq", cc)])
    emit_rstd(P, K, K.ps_stat, "sq", D, K.rstd, "rstd")
    for cc in range(8):
        P.add("dve", lambda e, cc=cc: e.scalar_tensor_tensor(
            out=xn[:, cc, :], in0=K.XT[:, cc, ts], scalar=K.gains[:, gidx, cc:cc + 1],
            in1=K.rstd[:], op0=ALU.mult, op1=ALU.mult),
            reads=[XTr(cc), "rstd", "gains"], writes=[("xnT", tt % 2, cc)])


def emit_proj(ph, c, K):
    P = ph.P
    K.uT = ph.sb("uT", [128, 4, T], BF16, at=M_UT)
    K.qT = ph.sb("qT", [128, 4, T], BF16, at=M_QT)
    K.kT = ph.sb("kT", [128, 4, T], BF16, at=M_KT)
    K.vA = ph.sb("vA", [128, 16, 4, 130], BF16, at=M_VA)
    ph.cur = M_TMP
    K.sq = ph.sb("sq", [128, 8, TT], BF16)
    K.xnT2 = [ph.sb(f"xnT{i}", [128, 8, TT], BF16) for i in range(2)]
    K.rt = ph.sb("rt", [128, TT], F32)
    K.rstd = ph.sb("rstd", [128, TT], F32)
    NW = 8
    wr = [ph.sb(f"wr{i}", [128, 8 * 128], BF16) for i in range(NW)]
    winv = ph.sb("winv", [128, 8 * 512], BF16)
    K.ps_stat = ph.ps("ps_stat", [128, 512])
    NPA = 4
    psA = [ph.ps(f"psA{i}", [128, 512]) for i in range(NPA)]
    psV = [ph.ps(f"psV{i}", [128, 512]) for i in range(2)]
    P.add("sp", lambda e: e.dma_start(out=winv[:], in_=c.winvb), reads=["winvb"], writes=["winv"], dma=True)
    P.add("pool", lambda e: e.memset(K.vA[:, :, :, 128:129], 1.0), writes=["vAones"])
    wl = 0
    ev = 0
    emit_prenorm(P, K, 0, G_MIX_PRE)
    for tt in range(NTT):
        ts = slice(tt * TT, (tt + 1) * TT)
        xn = K.xnT2[tt % 2]
        for j in range(12):
            if j == 6 and tt + 1 < NTT:
                emit_prenorm(P, K, tt + 1, G_MIX_PRE)
            slot = wl % NW
            wl += 1
            wt = wr[slot]
            P.add("sp", lambda e, wt=wt, j=j: e.dma_start(out=wt[:], in_=c.winfb[j]),
                  reads=[("winfb", j)], writes=[("wr", slot)], dma=True)
            pa = psA[j % NPA]
            for cc in range(8):
                P.add("pe", lambda e, wt=wt, cc=cc, pa=pa, xn=xn: e.matmul(
                    pa[:], wt[:, cc * 128:(cc + 1) * 128], xn[:, cc, :], start=(cc == 0), stop=(cc == 7)),
                    reads=[("wr", slot), ("xnT", tt % 2, cc)], writes=[("psA", j % NPA)])
            if j < 4:
                dst, res = K.qT[:, j, ts], ("qT", j, tt)
            elif j < 8:
                dst, res = K.kT[:, j - 4, ts], ("kT", j - 4, tt)
            else:
                uTp = K.uT[:].rearrange("p j (s n) -> p j s n", s=8)
                dst, res = uTp[:, j - 8, :, tt * 64:(tt + 1) * 64], ("uT", j - 8, tt)
            srcp = pa[:] if j < 8 else pa[:].rearrange("p (n s) -> p s n", s=8)
            if ev % 2 == 0:
                P.add("act", lambda e, dst=dst, srcp=srcp: e.activation(dst, srcp, AF.Copy),
                      reads=[("psA", j % NPA)], writes=[res])
            else:
                P.add("dve", lambda e, dst=dst, srcp=srcp: e.tensor_copy(dst, srcp),
                      reads=[("psA", j % NPA)], writes=[res])
            ev += 1
        for b4 in range(4):
            tb = tt * 4 + b4
            pv = psV[b4 % 2]
            for cc in range(8):
                P.add("pe", lambda e, cc=cc, pv=pv, b4=b4, xn=xn: e.matmul(
                    pv[:], xn[:, cc, b4 * 128:(b4 + 1) * 128], winv[:, cc * 512:(cc + 1) * 512],
                    start=(cc == 0), stop=(cc == 7)),
                    reads=["winv", ("xnT", tt % 2, cc)], writes=[("psV", b4 % 2)])
            dst = K.vA[:, tb, :, 0:128]
            srcv = pv[:].rearrange("p (h v) -> p h v", h=4)
            if ev % 2 == 0:
                P.add("act", lambda e, dst=dst, srcv=srcv: e.activation(dst, srcv, AF.Copy),
                      reads=[("psV", b4 % 2)], writes=[("vA", tb)])
            else:
                P.add("dve", lambda e, dst=dst, srcv=srcv: e.tensor_copy(dst, srcv),
                      reads=[("psV", b4 % 2)], writes=[("vA", tb)])
            ev += 1


def emit_attn(ph, c, K):
    P = ph.P
    K.qT = ph.sb("qT", [128, 4, T], BF16, at=M_QT)
    K.kT = ph.sb("kT", [128, 4, T], BF16, at=M_KT)
    K.vA = ph.sb("vA", [128, 16, 4, 130], BF16, at=M_VA)
    K.oTa = ph.sb("oTa", [128, 4, T], BF16, at=M_OTA)
    ph.cur = M_TMP
    NPT = 6
    pT = [ph.sb(f"pT{i}", [128, 512], BF16) for i in range(NPT)]
    o1 = [ph.sb(f"o1_{i}", [128, 4, 128], F32) for i in range(2)]
    od = [ph.sb(f"od{i}", [128, 128], F32) for i in range(8)]
    junk = [ph.sb(f"junk{i}", [128, 128], F32) for i in range(4)]
    sm = [ph.sb(f"sm{i}", [128, 8], F32) for i in range(8)]
    NPS = 4
    psS = [ph.ps(f"psS{i}", [128, 512]) for i in range(NPS)]
    acc = [[ph.ps(f"acc{r}{b}", [128, 512]) for b in range(2)] for r in range(2)]
    ob_tok = ph.sb("ob_tok", [128, 16, 4, 128], BF16)
    items = []
    si = 0
    pi = 0
    rnd = 0
    fin = 0
    for h in range(4):
        for qt in range(NTT):
            for cmap in range(2):
                r = rnd % 2
                rnd += 1
                rows = slice(cmap * 64, (cmap + 1) * 64)
                started = [False, False]
                for kb in range(4 * qt + 4):
                    j = kb - 4 * qt
                    qlo = max(j, 0) * 128
                    sb_ = psS[si % NPS]
                    sres = ("psS", si % NPS)
                    si += 1
                    pt = pT[pi % NPT]
                    pres = ("pT", pi % NPT)
                    pi += 1

                    def s_part(sb_=sb_, sres=sres, qlo=qlo, kb=kb, rows=rows, h=h, qt=qt, j=j):
                        P.add("pe", lambda e: e.matmul(
                            sb_[:, qlo:512], K.kT[rows, h, kb * 128:(kb + 1) * 128],
                            K.qT[rows, h, qt * 512 + qlo:(qt + 1) * 512], start=True, stop=(j < 0)),
                            reads=[("kT", h, kb // 4), ("qT", h, qt)], writes=[sres])
                        if j >= 0:
                            P.add("pe", lambda e: e.matmul(
                                sb_[:, qlo:qlo + 128], K.identb[:], K.maskneg[:], start=False, stop=True,
                                skip_group_check=True),
                                reads=["identb", "maskneg"], writes=[sres])

                    sts = []
                    for qb in range(max(j, 0), 4):
                        sts.append(not started[qb // 2])
                        started[qb // 2] = True

                    def r_part(sb_=sb_, sres=sres, pt=pt, pres=pres, qlo=qlo, kb=kb, h=h, qt=qt, j=j, r=r, sts=sts):
                        P.add("act", lambda e: e.activation(
                            pt[:, qlo:512], sb_[:, qlo:512], AF.Exp, scale=0.125),
                            reads=[sres], writes=[pres])
                        for ii, qb in enumerate(range(max(j, 0), 4)):
                            bank = acc[r][qb // 2]
                            col = (qb % 2) * 256
                            st = sts[ii]
                            P.add("pe", lambda e, bank=bank, col=col, qb=qb, st=st: e.matmul(
                                bank[:, col:col + 129], pt[:, qb * 128:(qb + 1) * 128], K.vA[:, kb, h, 0:129],
                                start=st, stop=(kb == 4 * qt + qb), skip_group_check=True),
                                reads=[pres, ("vA", kb), "vAones"], writes=[("acc", r, qb // 2)])

                    items.append((s_part, r_part))
                items.append((None, (lambda h=h, qt=qt, cmap=cmap, r=r, rnd=rnd: finalize(h, qt, cmap, r, rnd))))
    fin = [0]

    def finalize(h, qt, cmap, r, rnd):
        def acc_of(qb):
            return acc[r][qb // 2], (qb % 2) * 256, ("acc", r, qb // 2)
        par = fin[0] % 2
        fin[0] += 1
        smq = [sm[par * 4 + qb] for qb in range(4)]
        if cmap == 0:
            o1t = o1[(rnd // 2) % 2]
            for qb in range(4):
                bank, col, ares = acc_of(qb)
                P.add("dve", lambda e, qb=qb, bank=bank, col=col: e.reciprocal(
                    smq[qb][:, 0:1], bank[:, col + 128:col + 129]),
                    reads=[ares], writes=[("sm", par, qb, 0)])
            for qb in range(4):
                bank, col, ares = acc_of(qb)
                P.add("dve", lambda e, qb=qb, bank=bank, col=col: e.tensor_scalar(
                    o1t[:, qb, :], bank[:, col:col + 128], smq[qb][:, 0:1], None, ALU.mult),
                    reads=[ares, ("sm", par, qb, 0)], writes=[("o1", (rnd // 2) % 2, qb)])
            return
        o1t = o1[((rnd - 1) // 2) % 2]
        odq = [od[par * 4 + qb] for qb in range(4)]
        for qb in range(4):
            bank, col, ares = acc_of(qb)
            P.add("dve", lambda e, qb=qb, bank=bank, col=col: e.reciprocal(
                smq[qb][:, 1:2], bank[:, col + 128:col + 129]),
                reads=[ares], writes=[("sm", par, qb, 1)])
        for qb in range(4):
            P.add("dve", lambda e, qb=qb: e.tensor_tensor(smq[qb][:, 2:3], smq[qb][:, 1:2], K.nlam[:], ALU.mult),
                  reads=[("sm", par, qb, 1), "nlam"], writes=[("sm", par, qb, 2)])
        for qb in range(4):
            bank, col, ares = acc_of(qb)
            P.add("dve", lambda e, qb=qb, bank=bank, col=col: e.scalar_tensor_tensor(
                out=odq[qb][:], in0=bank[:, col:col + 128], scalar=smq[qb][:, 2:3], in1=o1t[:, qb, :],
                op0=ALU.mult, op1=ALU.add),
                reads=[ares, ("sm", par, qb, 2), ("o1", ((rnd - 1) // 2) % 2, qb)], writes=[("od", par, qb)])
        for qb in range(4):
            P.add("dve", lambda e, qb=qb: e.tensor_tensor(junk[qb][:], odq[qb][:], odq[qb][:], ALU.mult),
                  reads=[("od", par, qb)], writes=[("junk", qb)])
        for qb in range(4):
            P.add("dve", lambda e, qb=qb: e.tensor_reduce(
                smq[qb][:, 3:4], junk[qb][:], mybir.AxisListType.X, ALU.add),
                reads=[("junk", qb)], writes=[("sm", par, qb, 3)])
        for qb in range(4):
            P.add("dve", lambda e, qb=qb: e.tensor_scalar(
                smq[qb][:, 4:5], smq[qb][:, 3:4], 1.0 / 128, EPS, ALU.mult, ALU.add),
                reads=[("sm", par, qb, 3)], writes=[("sm", par, qb, 4)])
        for qb in range(4):
            P.add("pool", lambda e, qb=qb: e.tensor_tensor(smq[qb][:, 5:6], smq[qb][:, 4:5], K.mhalf[:], ALU.pow),
                  reads=[("sm", par, qb, 4), "mhalf"], writes=[("sm", par, qb, 5)])
        for qb in range(4):
            tb = qt * 4 + qb
            P.add("dve", lambda e, qb=qb, tb=tb: e.scalar_tensor_tensor(
                out=ob_tok[:, tb, h, :], in0=odq[qb][:], scalar=smq[qb][:, 5:6], in1=K.gsubb[:],
                op0=ALU.mult, op1=ALU.mult),
                reads=[("od", par, qb), ("sm", par, qb, 5), "gsubb"], writes=[("ob", tb, h)])

    LOOK = NPS - 1
    seq_s = [it for it in items if it[0] is not None]
    ns = 0
    nr = 0
    for it in items:
        if it[0] is None:
            it[1]()
            continue
        while ns < len(seq_s) and ns <= nr + LOOK:
            seq_s[ns][0]()
            ns += 1
        it[1]()
        nr += 1
    for tb in range(16):
        bank = psS[tb % NPS]
        bres = ("psS", tb % NPS)
        for h in range(4):
            P.add("pe", lambda e, bank=bank, tb=tb, h=h: e.matmul(
                bank[:, h * 128:(h + 1) * 128], ob_tok[:, tb, h, :], K.identb[:], start=True, stop=True,
                skip_group_check=True),
                reads=[("ob", tb, h), "identb"], writes=[bres])
        dst = K.oTa[:, :, tb * 128:(tb + 1) * 128]
        srcv = bank[:].rearrange("p (h t) -> p h t", h=4)
        if tb % 2 == 0:
            P.add("act", lambda e, dst=dst, srcv=srcv: e.activation(dst, srcv, AF.Copy),
                  writes=[bres, ("oTa", tb)])
        else:
            P.add("dve", lambda e, dst=dst, srcv=srcv: e.tensor_copy(dst, srcv),
                  writes=[bres, ("oTa", tb)])


TWO_PI = 2.0 * math.pi
MAGIC = 12582912.0


def emit_s5_setup(ph, c, K, hook=None, hook_every=8):
    class _PW:
        def __init__(self, P):
            self.P = P
            self.n = 0
            self.lim = int(os.environ.get("KSTEP", "100000"))

        def add(self, *a, **k):
            self.n += 1
            if self.n > self.lim:
                return None
            if os.environ.get("KSTEPV") and self.n == self.lim:
                import traceback
                traceback.print_stack(limit=4)
            r = self.P.add(*a, **k)
            if hook is not None and self.n % hook_every == 0:
                hook()
            return r
    P = _PW(ph.P)
    sp_ = ph.sb("s5p", [128, S5P_COLS], F32)
    cmask = ph.sb("cmask", [128, 128], F32)
    P.add("sp", lambda e: e.dma_start(out=sp_[:], in_=c.s5p), writes=["s5p"], dma=True)
    P.add("sp", lambda e: e.dma_start(out=cmask[:], in_=c.cmask), writes=["cmask"], dma=True)
    for q in range(4):
        src = c.esel[:, q * 2048:(q + 1) * 2048]
        dst = c.eb[:, q * 2048:(q + 1) * 2048]
        P.add("pool", lambda e, src=src, dst=dst: e.dma_start(out=dst, in_=src), writes=[("eb", q)], dma=True)
    are, aim, ldt = sp_[:, 0:16], sp_[:, 16:32], sp_[:, 32:48]
    bre = sp_[:, 48:304].rearrange("p (g h) -> p g h", g=16)
    bim = sp_[:, 304:560].rearrange("p (g h) -> p g h", g=16)
    cre = sp_[:, 560:816].rearrange("p (g h) -> p g h", g=16)
    cim = sp_[:, 816:1072].rearrange("p (g h) -> p g h", g=16)
    kk = sp_[:, 1072:1096].rearrange("p (w k) -> p w k", w=3)
    dD = sp_[:, 1096:1128]
    cnt = [0]

    def T_(shape):
        cnt[0] += 1
        return ph.sb(f"t{cnt[0]}", shape, F32), f"t{cnt[0]}"

    def tt(out, a, b, op, r, w, eng="dve"):
        P.add(eng, lambda e: e.tensor_tensor(out, a, b, op), reads=r, writes=w)

    def ts(out, a, s1, s2, op0, op1, r, w):
        P.add("dve", lambda e: e.tensor_scalar(out, a, s1, s2, op0, op1), reads=r, writes=w)

    def sin_of(x, xr_, n):
        t, tn = T_([128, n])
        r, rn = T_([128, n])
        ts(t[:], x, 1.0 / TWO_PI, MAGIC, ALU.mult, ALU.add, [xr_], [tn])
        ts(t[:], t[:], -MAGIC, None, ALU.add, ALU.bypass, [tn], [tn])
        P.add("dve", lambda e: e.scalar_tensor_tensor(out=r[:], in0=t[:], scalar=-TWO_PI, in1=x,
                                                       op0=ALU.mult, op1=ALU.add), reads=[tn, xr_], writes=[rn])
        ts(r[:], r[:], 3.141592, -3.141592, ALU.min, ALU.max, [rn], [rn])
        P.add("act", lambda e: e.activation(r[:], r[:], AF.Sin), reads=[rn], writes=[rn])
        return r, rn

    def cexp(xr_t, xr_n, xi_t, xi_n, n, unit=False):
        s_, sn = sin_of(xi_t, xi_n, n)
        x2, x2n = T_([128, n])
        ts(x2[:], xi_t, math.pi / 2, None, ALU.add, ALU.bypass, [xi_n], [x2n])
        c_, cn = sin_of(x2[:], x2n, n)
        if unit:
            return c_, cn, s_, sn
        e_, en = T_([128, n])
        P.add("act", lambda e: e.activation(e_[:], xr_t, AF.Exp), reads=[xr_n], writes=[en])
        tt(c_[:], c_[:], e_[:], ALU.mult, [cn, en], [cn])
        tt(s_[:], s_[:], e_[:], ALU.mult, [sn, en], [sn])
        return c_, cn, s_, sn

    dt, dtn = T_([128, 16])
    P.add("act", lambda e: e.activation(dt[:], ldt, AF.Exp), reads=["s5p"], writes=[dtn])
    xr, xrn = T_([128, 16])
    xi, xin = T_([128, 16])
    tt(xr[:], are, dt[:], ALU.mult, ["s5p", dtn], [xrn])
    tt(xi[:], aim, dt[:], ALU.mult, ["s5p", dtn], [xin])
    if int(os.environ.get('KSET', '9')) < 1:
        return
    c1, c1n, s1, s1n = cexp(xr[:], xrn, xi[:], xin, 16)
    ts(c1[:], c1[:], -1.0, None, ALU.add, ALU.bypass, [c1n], [c1n])
    t1, t1n = T_([128, 16])
    t2, t2n = T_([128, 16])
    rden, rdn = T_([128, 16])
    tt(t1[:], are, are, ALU.mult, ["s5p"], [t1n])
    tt(t2[:], aim, aim, ALU.mult, ["s5p"], [t2n])
    tt(t1[:], t1[:], t2[:], ALU.add, [t1n, t2n], [t1n])
    P.add("dve", lambda e: e.reciprocal(rden[:], t1[:]), reads=[t1n], writes=[rdn])
    cr, crn = T_([128, 16])
    ci, cin = T_([128, 16])
    tt(t1[:], c1[:], are, ALU.mult, [c1n, "s5p"], [t1n])
    tt(t2[:], s1[:], aim, ALU.mult, [s1n, "s5p"], [t2n])
    tt(t1[:], t1[:], t2[:], ALU.add, [t1n, t2n], [t1n])
    tt(cr[:], t1[:], rden[:], ALU.mult, [t1n, rdn], [crn])
    tt(t1[:], s1[:], are, ALU.mult, [s1n, "s5p"], [t1n])
    tt(t2[:], c1[:], aim, ALU.mult, [c1n, "s5p"], [t2n])
    tt(t1[:], t1[:], t2[:], ALU.subtract, [t1n, t2n], [t1n])
    tt(ci[:], t1[:], rden[:], ALU.mult, [t1n, rdn], [cin])
    if int(os.environ.get('KSET', '9')) < 2:
        return
    Br, Brn = T_([128, 16, 16])
    Bi, Bin = T_([128, 16, 16])
    u1, u1n = T_([128, 16, 16])
    crb = cr[:].rearrange("p (g o) -> p g o", o=1).broadcast_to([128, 16, 16])
    cib = ci[:].rearrange("p (g o) -> p g o", o=1).broadcast_to([128, 16, 16])
    tt(Br[:], crb, bre, ALU.mult, [crn, "s5p"], [Brn])
    tt(u1[:], cib, bim, ALU.mult, [cin, "s5p"], [u1n])
    tt(Br[:], Br[:], u1[:], ALU.subtract, [Brn, u1n], [Brn])
    tt(Bi[:], crb, bim, ALU.mult, [crn, "s5p"], [Bin])
    tt(u1[:], cib, bre, ALU.mult, [cin, "s5p"], [u1n])
    tt(Bi[:], Bi[:], u1[:], ALU.add, [Bin, u1n], [Bin])
    if int(os.environ.get('KSET', '9')) < 3:
        return
    k8r, k8rn = T_([128, 16])
    k8i, k8in = T_([128, 16])
    ts(k8r[:], xr[:], 8.0, None, ALU.mult, ALU.bypass, [xrn], [k8rn])
    ts(k8i[:], xi[:], 8.0, None, ALU.mult, ALU.bypass, [xin], [k8in])
    a8c, a8cn, a8s, a8sn = cexp(k8r[:], k8rn, k8i[:], k8in, 16)
    P.add("dve", lambda e: e.tensor_copy(K.a8[:, 0, :], a8c[:]), reads=[a8cn], writes=["a8"])
    P.add("dve", lambda e: e.tensor_copy(K.a8[:, 1, :], a8s[:]), reads=[a8sn], writes=["a8"])
    P.add("dve", lambda e: e.tensor_copy(K.a8b[:, 0, :], a8s[:]), reads=[a8sn], writes=["a8b"])
    ts(K.a8b[:, 1, :], a8s[:], -1.0, None, ALU.mult, ALU.bypass, [a8sn], ["a8b"])
    if os.environ.get('KVERB'):
        print('setup ops before powers', P.n)
    if int(os.environ.get('KSET', '9')) < 4:
        return
    outs = []
    xrb = xr[:].rearrange("p (g o) -> p g o", o=1).broadcast_to([128, 16, 8])
    xib = xi[:].rearrange("p (g o) -> p g o", o=1).broadcast_to([128, 16, 8])
    W4 = [128, 16, 8, 16]
    f1, f1n = T_(W4)
    f2, f2n = T_(W4)
    X7 = [ph.sb(f"X7{i}", W4, BF16) for i in range(2)]
    Zt = [ph.sb(f"Zt{i}", W4, BF16) for i in range(2)]
    Yp = ph.sb("Yp", [128, 2, 16, 128], BF16)
    for w in range(3):
        kb = kk[:, w:w + 1, :].broadcast_to([128, 16, 8])
        kr, krn = T_([128, 16, 8])
        ki, kin = T_([128, 16, 8])
        tt(kr[:], xrb, kb, ALU.mult, [xrn, "s5p"], [krn])
        tt(ki[:], xib, kb, ALU.mult, [xin, "s5p"], [kin])
        pr, prn, pi_, pin = cexp(kr[:].rearrange("p g k -> p (g k)"), krn,
                                 ki[:].rearrange("p g k -> p (g k)"), kin, 128)
        prb = pr[:].rearrange("p (g k o) -> p g k o", g=16, o=1).broadcast_to(W4)
        pib = pi_[:].rearrange("p (g k o) -> p g k o", g=16, o=1).broadcast_to(W4)
        if w == 0:
            mre = Br[:].rearrange("p g (o h) -> p g o h", o=1).broadcast_to(W4)
            mim = Bi[:].rearrange("p g (o h) -> p g o h", o=1).broadcast_to(W4)
            mrn, min_ = Brn, Bin
            ore, oim = X7[0][:], X7[1][:]
            orn, oin = "X7re", "X7im"
        else:
            mre = cre.rearrange("p g (o h) -> p g o h", o=1).broadcast_to(W4)
            mim = cim.rearrange("p g (o h) -> p g o h", o=1).broadcast_to(W4)
            mrn, min_ = "s5p", "s5p"
            if w == 1:
                ore, oim = Zt[0][:], Zt[1][:]
                orn, oin = "Zre", "Zim"
            else:
                ore = Yp[:, 0, :, :].rearrange("p g (k h) -> p g k h", k=8)
                oim = Yp[:, 1, :, :].rearrange("p g (k h) -> p g k h", k=8)
                orn, oin = "Ypre", "Ypim"
        tt(f1[:], prb, mre, ALU.mult, [prn, mrn], [f1n])
        tt(f2[:], pib, mim, ALU.mult, [pin, min_], [f2n])
        tt(ore, f1[:], f2[:], ALU.subtract, [f1n, f2n], [orn])
        tt(f1[:], prb, mim, ALU.mult, [prn, min_], [f1n])
        tt(f2[:], pib, mre, ALU.mult, [pin, mrn], [f2n])
        if w == 0:
            tt(oim, f1[:], f2[:], ALU.add, [f1n, f2n], [oin])
        else:
            P.add("dve", lambda e, oim=oim: e.scalar_tensor_tensor(
                out=oim, in0=f1[:], scalar=-1.0, in1=f2[:], op0=ALU.mult, op1=ALU.subtract),
                reads=[f1n, f2n], writes=[oin])
    Ypm = [ph.sb(f"Ypm{a}", [128, 2, 16, 128], BF16) for a in range(2)]
    for a in range(2):
        keep = slice(a * 64, (a + 1) * 64)
        P.add("pool", lambda e, a=a: e.memset(Ypm[a][:], 0.0), writes=[("Ypmz", a)])
        P.add("dve", lambda e, a=a, keep=keep: e.tensor_copy(Ypm[a][keep], Yp[keep]),
              reads=["Ypre", "Ypim", ("Ypmz", a)], writes=[("Ypm", a)])
        P.add("sp", lambda e, a=a: e.dma_start(out=c.ypb[a], in_=Ypm[a][:].rearrange("p r g k -> p (r g k)")),
              reads=[("Ypm", a)], writes=[("ypb", a)], dma=True)
    if os.environ.get('KVERB'):
        print('setup ops after powers', P.n)
    if int(os.environ.get('KSET', '9')) < 5:
        return
    xt7 = ph.sb("xt7", [128, 2, 32, 64], BF16)
    psX = [ph.ps(f"psX{i}", [128, 512]) for i in range(2)]
    bi = 0
    for ri in range(2):
        for gq in range(4):
            bank = psX[bi % 2]
            bres = ("psX", bi % 2)
            bi += 1
            for gi in range(4):
                gp = gq * 4 + gi
                src = X7[ri][:, gp, :, :].rearrange("p k h -> p (k h)")
                P.add("pe", lambda e, bank=bank, gi=gi, src=src: e.matmul(
                    bank[:, gi * 128:(gi + 1) * 128], src, K.identb[:], start=True, stop=True,
                    skip_group_check=True),
                    reads=["X7re" if ri == 0 else "X7im", "identb"], writes=[bres])
            dst = xt7[:, ri, gq * 8:(gq + 1) * 8, :]
            srcv = bank[:, 0:512].rearrange("p (g q) -> p g q", g=8)
            P.add("act" if gq % 2 == 0 else "dve",
                  (lambda e, dst=dst, srcv=srcv: e.activation(dst, srcv, AF.Copy)) if gq % 2 == 0 else
                  (lambda e, dst=dst, srcv=srcv: e.tensor_copy(dst, srcv)),
                  reads=[bres], writes=[bres, "xt7"])
    P.add("sp", lambda e: e.dma_start(out=c.xt7b, in_=xt7[:].rearrange("p r g q -> p (r g q)")),
          reads=["xt7"], writes=["xt7b"], dma=True)
    if int(os.environ.get('KSET', '9')) < 6:
        return
    ttl = ph.sb("ttl", [128, 32, 128], BF16)
    Zm = [[ph.sb(f"Zm{a}{b}", W4, BF16) for b in range(2)] for a in range(2)]
    for a in range(2):
        keep = slice(a * 64, (a + 1) * 64)
        for b in range(2):
            P.add("pool", lambda e, a=a, b=b: e.memset(Zm[a][b][:], 0.0), writes=[("Zmz", a, b)])
            P.add("dve", lambda e, a=a, b=b, keep=keep: e.tensor_copy(Zm[a][b][keep], Zt[b][keep]),
                  reads=["Zre", "Zim", ("Zmz", a, b)], writes=["Zm"])
    tm = [ph.sb(f"tm{i}", [128, 4, 128], F32) for i in range(2)]
    psT = [ph.ps(f"psTT{i}", [128, 512]) for i in range(2)]
    cmb = cmask[:].rearrange("p (o q) -> p o q", o=1).broadcast_to([128, 4, 128])
    for gq in range(8):
        bank = psT[gq % 2]
        bres = ("psTT", gq % 2)
        for gi in range(4):
            g = gq * 4 + gi
            gp, g2 = g // 2, g % 2
            rows = slice(g2 * 64, (g2 + 1) * 64)
            for ri in range(2):
                lh = X7[ri][:, gp, :, :].rearrange("p k h -> p (k h)")
                rh = Zm[g2][ri][:, gp, :, :].rearrange("p k h -> p (k h)")
                P.add("pe", lambda e, bank=bank, gi=gi, lh=lh, rh=rh, ri=ri, first=(gi == 0 and ri == 0): e.matmul(
                    bank[:, gi * 128:(gi + 1) * 128], lh, rh, start=first, stop=(ri == 1),
                    skip_group_check=True),
                    reads=["X7re", "X7im", "Zm"], writes=[bres])
        tmt = tm[gq % 2]
        P.add("dve", lambda e, tmt=tmt, bank=bank: e.tensor_tensor(
            tmt[:], bank[:].rearrange("p (g q) -> p g q", g=4), cmb, ALU.mult),
            reads=["cmask"], writes=[bres, ("tm", gq % 2)])
        for gi in range(4):
            g = gq * 4 + gi
            P.add("dve", lambda e, tmt=tmt, gi=gi, g=g: e.scalar_tensor_tensor(
                out=ttl[:, g, :], in0=K.ident[:], scalar=dD[:, g:g + 1], in1=tmt[:, gi, :],
                op0=ALU.mult, op1=ALU.add),
                reads=[("tm", gq % 2), "ident", "s5p"], writes=["ttl"])
    P.add("sp", lambda e: e.dma_start(out=c.ttb, in_=ttl[:].rearrange("p g q -> p (g q)")),
          reads=["ttl"], writes=["ttb"], dma=True)


def emit_s5a(ph, c, K):
    P = ph.P
    K.uT = ph.sb("uT", [128, 4, T], BF16, at=M_UT)
    K.Ubar = ph.sb("Ubar", [128, 32, 256], BF16, at=M_UBAR)
    K.Sbf = ph.sb("Sbf", [128, 2, 16, 257], BF16, at=M_SBF)
    ph.cur = M_TMP
    NCH, CL = 9, 32
    GRP = [(0, 5), (5, 9)]
    ph.cur = COMMON_END + 49280
    P1 = [[ph.sb(f"P1_{g}{i}", [128, b - a, 2, 16], F32) for i in range(2)] for g, (a, b) in enumerate(GRP)]
    P2 = [[ph.sb(f"P2_{g}{i}", [128, b - a, 2, 16], F32) for i in range(2)] for g, (a, b) in enumerate(GRP)]
    XT7_OFF = ph.cur
    xt7 = ph.sb("xt7", [128, 2, 32, 64], BF16)
    assert ph.cur <= M_OTA
    ph.cur = M_TMP
    E = ph.sb("E", [128, 64 * 128], BF16)
    L = ph.sb("L", [128, NCH * CL, 2, 16], F32)
    F1 = [ph.sb("F1", [128, 32, 16], F32, at=XT7_OFF)] * 2
    F2 = [ph.sb("F2", [128, 32, 16], F32, at=XT7_OFF + 2048)] * 2
    F3 = [ph.sb("F3", [128, 32, 16], F32, at=XT7_OFF + 4096)] * 2
    F4 = [ph.sb("F4", [128, 32, 16], F32, at=XT7_OFF + 6144)] * 2
    NPU, NPL = 4, 4
    psU = [ph.ps(f"psU{i}", [128, 512]) for i in range(NPU)]
    psL = [ph.ps(f"psL{i}", [128, 512]) for i in range(NPL)]
    P.add("sp", lambda e: e.dma_start(out=E[:], in_=c.eb), reads=[("eb", q) for q in range(4)],
          writes=["E"], dma=True)
    P.add("sp", lambda e: e.dma_start(out=xt7[:].rearrange("p r g q -> p (r g q)"), in_=c.xt7b),
          reads=["xt7b"], writes=["xt7"], dma=True)
    P.add("pool", lambda e: e.memset(K.Sbf[:, :, :, 0:1], 0.0), writes=["Sbf0"])
    uTv = K.uT[:].rearrange("p j (s n) -> p j s n", s=8)
    for g in range(32):
        j, glo = g // 8, g % 8
        bank = psU[g % NPU]
        for sg in range(8):
            idx = glo * 8 + sg
            P.add("pe", lambda e, bank=bank, idx=idx, j=j, sg=sg: e.matmul(
                bank[:, 0:256], E[:, idx * 128:(idx + 1) * 128], uTv[:, j, sg, :],
                start=(sg == 0), stop=(sg == 7)),
                reads=["E"], writes=[("psU", g % NPU)])
        if g % 2 == 0:
            P.add("act", lambda e, g=g, bank=bank: e.activation(K.Ubar[:, g, :], bank[:, 0:256], AF.Copy),
                  writes=[("psU", g % NPU), ("Ubar", g)])
        else:
            P.add("dve", lambda e, g=g, bank=bank: e.tensor_copy(K.Ubar[:, g, :], bank[:, 0:256]),
                  writes=[("psU", g % NPU), ("Ubar", g)])
    for gp in range(16):
        bank = psL[gp % NPL]
        for g2 in range(2):
            g = 2 * gp + g2
            for ri in range(2):
                P.add("pe", lambda e, bank=bank, g2=g2, g=g, ri=ri: e.matmul(
                    bank[g2 * 64:(g2 + 1) * 64, ri * 256:(ri + 1) * 256], xt7[:, ri, g, :], K.Ubar[:, g, :],
                    start=True, stop=True, skip_group_check=True),
                    reads=["xt7", ("Ubar", g)], writes=[("psL", gp % NPL)])
        dst = L[:, 0:256, :, gp].rearrange("p n r -> p r n")
        srcv = bank[:].rearrange("p (r n) -> p r n", r=2)
        if gp % 2 == 0:
            P.add("act", lambda e, dst=dst, srcv=srcv: e.activation(dst, srcv, AF.Copy),
                  writes=[("psL", gp % NPL), ("Lgp", gp)])
        else:
            P.add("dve", lambda e, dst=dst, srcv=srcv: e.tensor_copy(dst, srcv),
                  writes=[("psL", gp % NPL), ("Lgp", gp)])
    Lc = L[:].rearrange("p (c j) r g -> p c j r g", c=NCH)
    allL = [("Lgp", gp) for gp in range(16)]
    P.add("dve", lambda e: e.memset(L[:, 256:288, :, :], 0.0), reads=allL, writes=[("Lg", 1), "xt7"])
    P.add("dve", lambda e: e.tensor_copy(L[:, 256, :, :], K.a8[:]), writes=[("Lg", 1)])
    for j in range(1, CL):
        for stage in range(5):
            for g, (ca, cb) in enumerate(GRP):
                nchk = cb - ca
                p1, p2 = P1[g][j % 2], P2[g][j % 2]
                a1c = K.a8[:, 0:1, :].rearrange("p (c r) g -> p c r g", c=1).broadcast_to([128, nchk, 2, 16])
                a2c = K.a8b[:].rearrange("p (c r) g -> p c r g", c=1).broadcast_to([128, nchk, 2, 16])
                lg = ("Lg", g)
                if stage == 0:
                    P.add("dve", lambda e, p1=p1, j=j, ca=ca, cb=cb, a1c=a1c: e.tensor_tensor(
                        p1[:], Lc[:, ca:cb, j - 1, :, :], a1c, ALU.mult),
                        reads=[lg], writes=[("P1", g, j % 2)])
                elif stage == 1:
                    P.add("dve", lambda e, p2=p2, j=j, ca=ca, cb=cb, a2c=a2c: e.tensor_tensor(
                        p2[:], Lc[:, ca:cb, j - 1, :, :], a2c, ALU.mult),
                        reads=[lg], writes=[("P2", g, j % 2)])
                elif stage == 2:
                    P.add("dve", lambda e, p1=p1, j=j, ca=ca, cb=cb: e.tensor_tensor(
                        Lc[:, ca:cb, j, :, :], Lc[:, ca:cb, j, :, :], p1[:], ALU.add),
                        reads=[("P1", g, j % 2)], writes=[lg])
                elif stage == 3:
                    P.add("dve", lambda e, p2=p2, j=j, ca=ca, cb=cb: e.tensor_tensor(
                        Lc[:, ca:cb, j, 0, :], Lc[:, ca:cb, j, 0, :], p2[:, :, 1, :], ALU.add),
                        reads=[("P2", g, j % 2)], writes=[lg])
                else:
                    P.add("dve", lambda e, p2=p2, j=j, ca=ca, cb=cb: e.tensor_tensor(
                        Lc[:, ca:cb, j, 1, :], Lc[:, ca:cb, j, 1, :], p2[:, :, 0, :], ALU.add),
                        reads=[("P2", g, j % 2)], writes=[lg])
    pwr = Lc[:, 8, :, 0, :]
    pwi = Lc[:, 8, :, 1, :]
    LG = [("Lg", 0), ("Lg", 1)]
    for cch in range(1, 8):
        cr = Lc[:, cch - 1, CL - 1:CL, 0, :].broadcast_to([128, CL, 16])
        ci = Lc[:, cch - 1, CL - 1:CL, 1, :].broadcast_to([128, CL, 16])
        f = 0
        P.add("dve", lambda e, cr=cr, f=f: e.tensor_tensor(F1[f][:], pwr, cr, ALU.mult),
              reads=LG, writes=[("F1", f)])
        P.add("dve", lambda e, ci=ci, f=f: e.tensor_tensor(F2[f][:], pwi, ci, ALU.mult),
              reads=LG, writes=[("F2", f)])
        P.add("dve", lambda e, ci=ci, f=f: e.tensor_tensor(F3[f][:], pwr, ci, ALU.mult),
              reads=LG, writes=[("F3", f)])
        P.add("dve", lambda e, cr=cr, f=f: e.tensor_tensor(F4[f][:], pwi, cr, ALU.mult),
              reads=LG, writes=[("F4", f)])
        P.add("dve", lambda e, f=f: e.tensor_tensor(F1[f][:], F1[f][:], F2[f][:], ALU.subtract),
              reads=[("F2", f)], writes=[("F1", f)])
        P.add("dve", lambda e, f=f: e.tensor_tensor(F3[f][:], F3[f][:], F4[f][:], ALU.add),
              reads=[("F4", f)], writes=[("F3", f)])
        P.add("dve", lambda e, cch=cch, f=f: e.tensor_tensor(Lc[:, cch, :, 0, :], Lc[:, cch, :, 0, :], F1[f][:], ALU.add),
              reads=[("F1", f)], writes=LG)
        P.add("dve", lambda e, cch=cch, f=f: e.tensor_tensor(Lc[:, cch, :, 1, :], Lc[:, cch, :, 1, :], F3[f][:], ALU.add),
              reads=[("F3", f)], writes=LG)
    for ri in range(2):
        dst = K.Sbf[:, ri, :, 1:257]
        srcv = L[:, 0:256, ri, :].rearrange("p n g -> p g n")
        P.add("act" if ri == 0 else "dve",
              (lambda e, dst=dst, srcv=srcv: e.activation(dst, srcv, AF.Copy)) if ri == 0 else
              (lambda e, dst=dst, srcv=srcv: e.tensor_copy(dst, srcv)),
              reads=[("Lg", 0), ("Lg", 1), "Sbf0"], writes=[("Sbf", ri)])


def emit_s5b(ph, c, K):
    P = ph.P
    K.gT = ph.sb("gT", [128, 4, T], BF16, at=M_GT)
    K.Ubar = ph.sb("Ubar", [128, 32, 256], BF16, at=M_UBAR)
    K.Sbf = ph.sb("Sbf", [128, 2, 16, 257], BF16, at=M_SBF)
    ph.cur = M_TMP
    E = ph.sb("E", [128, 64 * 128], BF16)
    ttl = ph.sb("ttl", [128, 32, 128], BF16)
    yp = [ph.sb(f"yp{a}", [128, 2, 16, 128], BF16) for a in range(2)]
    gst = [ph.sb(f"gst{i}", [128, 8, 256], BF16) for i in range(2)]
    psY = [ph.ps(f"psY{i}", [128, 512]) for i in range(2)]
    psG = [ph.ps(f"psG{i}", [128, 512]) for i in range(2)]
    P.add("sp", lambda e: e.dma_start(out=ttl[:].rearrange("p g q -> p (g q)"), in_=c.ttb),
          writes=["ttl"], dma=True)
    for a in range(2):
        P.add("sp", lambda e, a=a: e.dma_start(out=yp[a][:].rearrange("p r g q -> p (r g q)"), in_=c.ypb[a]),
              writes=["yp"], dma=True)
    P.add("sp", lambda e: e.dma_start(out=E[:], in_=c.eb), writes=["E"], dma=True)
    gTv = K.gT[:].rearrange("p j (n s) -> p j s n", s=8)
    for j in range(4):
        gs = gst[j % 2]
        for glo in range(8):
            g = 8 * j + glo
            gp, g2 = g // 2, g % 2
            rows = slice(g2 * 64, (g2 + 1) * 64)
            bank = psY[g % 2]
            P.add("pe", lambda e, bank=bank, g=g: e.matmul(
                bank[:, 0:256], ttl[:, g, :], K.Ubar[:, g, :], start=True, stop=False),
                reads=["ttl"], writes=[("psY", g % 2)])
            for ri in range(2):
                P.add("pe", lambda e, bank=bank, g2=g2, ri=ri, gp=gp: e.matmul(
                    bank[:, 0:256], yp[g2][:, ri, gp, :], K.Sbf[:, ri, gp, 0:256],
                    start=False, stop=(ri == 1)),
                    reads=["yp"], writes=[("psY", g % 2)])
            P.add("act", lambda e, gs=gs, glo=glo, bank=bank: e.activation(
                gs[:, glo, :], bank[:, 0:256], AF.Gelu_apprx_tanh),
                writes=[("psY", g % 2), ("gst", j % 2, glo)])
        for tau in range(8):
            bank = psG[tau % 2]
            for glo in range(8):
                idx = tau * 8 + glo
                P.add("pe", lambda e, bank=bank, idx=idx, gs=gs, glo=glo: e.matmul(
                    bank[:, 0:256], E[:, idx * 128:(idx + 1) * 128], gs[:, glo, :],
                    start=(glo == 0), stop=(glo == 7)),
                    reads=["E", ("gst", j % 2, glo)], writes=[("psG", tau % 2)])
            dst = gTv[:, j, tau, :]
            if tau % 2 == 0:
                P.add("act", lambda e, dst=dst, bank=bank: e.activation(dst, bank[:, 0:256], AF.Copy),
                      writes=[("psG", tau % 2), ("gT", j)])
            else:
                P.add("dve", lambda e, dst=dst, bank=bank: e.tensor_copy(dst, bank[:, 0:256]),
                      writes=[("psG", tau % 2), ("gT", j)])


def emit_mixout(ph, c, K):
    P = ph.P
    K.gT = ph.sb("gT", [128, 4, T], BF16, at=M_GT)
    K.oTa = ph.sb("oTa", [128, 4, T], BF16, at=M_OTA)
    ph.cur = COMMON_END + 16384
    wgl = ph.sb("wgl", [128, 4, 4 * 128], BF16)
    NW = 8
    K.sq = ph.sb("sq", [128, 8, TT], BF16)
    osT = ph.sb("osT", [128, 4, TT], F32)
    onT = ph.sb("onT", [128, 4, TT], BF16)
    K.rt = ph.sb("rt", [128, TT], F32)
    K.rstd = ph.sb("rstd", [128, TT], F32)
    K.rstd2 = ph.sb("rstd2", [128, TT], F32)
    sig = [ph.sb(f"sig{i}", [128, TT], F32) for i in range(2)]
    sqA = ph.sb("sqA", [128, 4, TT], BF16)
    rtA = ph.sb("rtA", [128, TT], F32)
    assert ph.cur <= M_OTA
    ph.cur = M_TMP
    K.yT = ph.sb("yT", [128, 8, TT], F32)
    tmp = [ph.sb(f"tmp{i}", [128, TT], F32) for i in range(8)]
    wo = [ph.sb(f"wo{i}", [128, 8 * 128], BF16) for i in range(NW)]
    psA = [ph.ps(f"psA{i}", [128, 512]) for i in range(2)]
    psC = [ph.ps(f"psC{i}", [128, 512]) for i in range(4)]
    K.ps_stat = ph.ps("ps_stat", [128, 512])
    for m in range(4):
        P.add("sp", lambda e, m=m: e.dma_start(out=wgl[:, m, :], in_=c.wglub[m]), writes=[("wgl", m)], dma=True)
    st = {"wl": 0}

    def glu_a(tt):
        ts = slice(tt * TT, (tt + 1) * TT)
        for m in range(4):
            pa = psA[m % 2]
            for cc in range(4):
                P.add("pe", lambda e, pa=pa, m=m, cc=cc: e.matmul(
                    pa[:], wgl[:, m, cc * 128:(cc + 1) * 128], K.gT[:, cc, ts], start=(cc == 0), stop=(cc == 3)),
                    reads=[("wgl", m)], writes=[("psA", m % 2)])
            sg = sig[m % 2]
            P.add("act", lambda e, sg=sg, pa=pa, m=m: e.activation(
                sg[:], pa[:], AF.Sigmoid, bias=K.vecs[:, m:m + 1]),
                writes=[("psA", m % 2), ("sig", m % 2)])
            P.add("dve", lambda e, sg=sg, m=m: e.tensor_tensor(osT[:, m, :], K.gT[:, m, ts], sg[:], ALU.mult),
                  reads=[("sig", m % 2)], writes=[("osT", m)])
            P.add("act", lambda e, m=m: e.activation(sqA[:, m, :], osT[:, m, :], AF.Square),
                  reads=[("osT", m)], writes=[("sqA", m)])

    def glu_b(tt):
        for cc in range(4):
            P.add("pe", lambda e, cc=cc: e.matmul(K.ps_stat[:], K.ones[:], sqA[:, cc, :], start=(cc == 0), stop=(cc == 3)),
                  reads=[("sqA", cc), "ones"], writes=["ps_stat"])
        P.add("act", lambda e: e.activation(rtA[:], K.ps_stat[:], AF.Sqrt, bias=K.epsb[:], scale=1.0 / 512),
              reads=["epsb"], writes=["ps_stat", "rtA"])
        P.add("dve", lambda e: e.reciprocal(K.rstd[:], rtA[:]), reads=["rtA"], writes=["rstd"])
        for m in range(4):
            P.add("dve", lambda e, m=m: e.scalar_tensor_tensor(
                out=onT[:, m, :], in0=osT[:, m, :], scalar=K.vecs[:, 4 + m:5 + m], in1=K.rstd[:],
                op0=ALU.mult, op1=ALU.mult),
                reads=[("osT", m), "rstd"], writes=[("onT", m)])

    def outproj(tt):
        ts = slice(tt * TT, (tt + 1) * TT)
        for m in range(8):
            slot = st["wl"] % NW
            st["wl"] += 1
            wt = wo[slot]
            P.add("sp", lambda e, wt=wt, m=m: e.dma_start(out=wt[:], in_=c.woutb[m]),
                  writes=[("wo", slot)], dma=True)
            py = psC[m % 4]
            for cc in range(8):
                rhs = K.oTa[:, cc, ts] if cc < 4 else onT[:, cc - 4, :]
                P.add("pe", lambda e, wt=wt, cc=cc, py=py, rhs=rhs: e.matmul(
                    py[:], wt[:, cc * 128:(cc + 1) * 128], rhs, start=(cc == 0), stop=(cc == 7)),
                    reads=[("wo", slot)] + ([("onT", cc - 4)] if cc >= 4 else []), writes=[("psC", m % 4)])
            P.add("act", lambda e, m=m, py=py: e.activation(K.yT[:, m, :], py[:], AF.Copy),
                  reads=[("psC", m % 4)], writes=[("yT", m)])
            P.add("act", lambda e, m=m: e.activation(K.sq[:, m, :], K.yT[:, m, :], AF.Square),
                  reads=[("yT", m)], writes=[("sq", m)])

    def post_stats(tt):
        emit_rstd(P, K, K.ps_stat, "sq", D, K.rstd2, "rstd2")

    def post_apply(tt):
        ts = slice(tt * TT, (tt + 1) * TT)
        for m in range(8):
            tm_ = tmp[m]
            P.add("dve", lambda e, m=m, tm_=tm_: e.scalar_tensor_tensor(
                out=tm_[:], in0=K.yT[:, m, :], scalar=K.gains[:, G_MIX_POST, m:m + 1],
                in1=K.rstd2[:], op0=ALU.mult, op1=ALU.mult),
                reads=[("yT", m), "rstd2"], writes=[("tmp", m)])
            P.add("pool", lambda e, m=m, tm_=tm_: e.tensor_tensor(
                K.XT[:, m, ts], K.XT[:, m, ts], tm_[:], ALU.add),
                reads=[("tmp", m)], writes=[("XT", tt, m)])

    glu_a(0)
    glu_b(0)
    if NTT > 1:
        glu_a(1)
    for tt in range(NTT):
        outproj(tt)
        if tt + 1 < NTT:
            glu_b(tt + 1)
        if tt + 2 < NTT:
            glu_a(tt + 2)
        post_stats(tt)
        post_apply(tt)


def alloc_ffn_phase(ph, c, K):
    alloc_common(ph, c, K)
    alloc_ffn(ph, K)


def build_nc(stage="full"):
    nc = bass.Bass("TRN2", target_bir_lowering=False)
    debug = stage not in ("full",)
    c = declare_dram(nc, debug=debug)
    first = True
    ges = ExitStack()
    G = Globals(nc, ges)
    full_like = stage == "full"
    if full_like:
        with Phase(nc, G, "start") as ph:
            K = Ctx()
            alloc_common(ph, c, K)
            K.xio = [ph.sb(f"xio{i}", [128, D], F32) for i in range(2)]
            K.psA = [ph.ps(f"psA{i}", [128, 512]) for i in range(2)]
            emit_consts(ph, c, K)
            gen = load_x_blocks(ph, c, K, 0, act_only=True)
            next(gen, None)
            emit_s5_setup(ph, c, K, hook=lambda: next(gen, None))
            for _ in gen:
                pass
            ph.finish()
    for seq in range(NSEQ):
        with Phase(nc, G, f"f1s{seq}") as ph:
            K = Ctx()
            alloc_ffn_phase(ph, c, K)
            if first and not full_like:
                emit_consts(ph, c, K)
                if stage in ("att", "mix"):
                    emit_cast_mixer_weights(ph, c)
                elif stage != "io":
                    emit_cast_weights(ph, c, 0)
                    emit_cast_mixer_weights(ph, c)
                    emit_cast_weights(ph, c, 1)
            xg = None
            if not (full_like and seq == 0):
                if full_like:
                    xg = load_x_blocks(ph, c, K, seq)
                    for _ in range(4):
                        next(xg, None)
                else:
                    emit_load_x(ph, c, K, seq)
            if stage not in ("io", "iocast", "att", "mix"):
                emit_ffn(ph, c, K, 0, G_FF1_PRE, G_FF1_POST, direct_cast=(full_like and seq == 0), x_gen=xg)
            if stage in ("ffn1", "io", "iocast"):
                emit_store_x(ph, c, K, seq)
            ph.finish()
        first = False
        if stage in ("ffn1", "io", "iocast"):
            continue
        KMIX = int(os.environ.get("KMIX", "9"))
        if seq == 0 and stage != "att" and KMIX >= 1 and not full_like:
            with Phase(nc, G, "s5set") as ph:
                K = Ctx()
                alloc_common(ph, c, K)
                emit_s5_setup(ph, c, K)
                ph.finish()
        with Phase(nc, G, f"pjs{seq}") as ph:
            K = Ctx()
            alloc_common(ph, c, K)
            emit_proj(ph, c, K)
            ph.finish()
        with Phase(nc, G, f"ats{seq}") as ph:
            K = Ctx()
            alloc_common(ph, c, K)
            emit_attn(ph, c, K)
            if stage == "att":
                ph.P.add("sp", lambda e, K=K, seq=seq: e.dma_start(
                    out=c.dbg[seq], in_=K.oTa[:].rearrange("p h t -> p (h t)")),
                    reads=[("oTa", tb) for tb in range(16)], writes=["dbg"], dma=True)
            ph.finish()
        if stage == "att":
            continue
        if KMIX >= 2:
          with Phase(nc, G, f"sas{seq}") as ph:
            K = Ctx()
            alloc_common(ph, c, K)
            emit_s5a(ph, c, K)
            if os.environ.get("KDUMP") == "Ubar":
                ph.P.add("sp", lambda e, K=K, seq=seq: e.dma_start(
                    out=c.dbg[seq], in_=K.Ubar[:].rearrange("p g n -> p (g n)")),
                    reads=[("Ubar", g) for g in range(32)], writes=["dbg"], dma=True)
            if os.environ.get("KDUMP") == "Sbf":
                ph.P.add("sp", lambda e, K=K, seq=seq: e.dma_start(
                    out=c.dbg[seq].rearrange("p (a n) -> p a n", n=256),
                    in_=K.Sbf[:, :, :, 1:257].rearrange("p r g n -> p (r g) n")),
                    reads=[("Sbf", 0), ("Sbf", 1)], writes=["dbg"], dma=True)
            ph.finish()
        if KMIX >= 3:
          with Phase(nc, G, f"sbs{seq}") as ph:
            K = Ctx()
            alloc_common(ph, c, K)
            emit_s5b(ph, c, K)
            if os.environ.get("KDUMP") == "gT":
                ph.P.add("sp", lambda e, K=K, seq=seq: e.dma_start(
                    out=c.dbg[seq], in_=K.gT[:].rearrange("p j t -> p (j t)")),
                    reads=[("gT", j) for j in range(4)], writes=["dbg"], dma=True)
            ph.finish()
        if KMIX >= 4:
          with Phase(nc, G, f"mos{seq}") as ph:
            K = Ctx()
            alloc_common(ph, c, K)
            emit_mixout(ph, c, K)
            ph.finish()
        with Phase(nc, G, f"f2s{seq}") as ph:
            K = Ctx()
            alloc_ffn_phase(ph, c, K)
            if stage != "mix":
                emit_ffn(ph, c, K, 1, G_FF2_PRE, G_FF2_POST)
            emit_store_x(ph, c, K, seq)
            ph.finish()
    ges.close()
    return nc


def host_layout(inp):
    f = lambda a: np.ascontiguousarray(np.asarray(a, dtype=np.float32))
    com = {}
    com["ident"] = np.eye(128, dtype=np.float32)
    kk = np.arange(128)[:, None]
    qq = np.arange(128)[None, :]
    com["maskneg"] = np.where(kk <= qq, 0.0, -30000.0).astype(np.float32)
    gl = [inp["ff1_pre_g"], inp["ff1_post_g"], inp["mix_pre_g"], inp["mix_post_g"],
          inp["ff2_pre_g"], inp["ff2_post_g"]]
    gains = np.zeros((128, 8, 8), np.float32)
    for i, g in enumerate(gl):
        gains[:, i, :] = f(g).reshape(8, 128).T
    com["gains"] = gains
    lamv = np.stack([f(inp[k])[0] for k in ("lambda_q1", "lambda_k1", "lambda_q2", "lambda_k2")], 0)
    com["lamv"] = np.ascontiguousarray(np.broadcast_to(lamv[None], (128, 4, 64)))
    com["gsub"] = np.ascontiguousarray(np.broadcast_to(f(inp["attn_subln_g"])[0][None], (128, 128)))
    vecs = np.zeros((128, 8), np.float32)
    vecs[:, 0:4] = f(inp["ssm_b_glu"])[0].reshape(4, 128).T
    vecs[:, 4:8] = f(inp["ssm_norm_g"])[0].reshape(4, 128).T
    com["vecs"] = vecs
    for i, pre in enumerate(("ff1", "ff2")):
        wg = f(inp[pre + "_w_gate"])[0]
        wu = f(inp[pre + "_w_up"])[0]
        wd = f(inp[pre + "_w_down"])[0]
        g4 = wg.reshape(8, 128, NF, 128).transpose(2, 1, 0, 3)
        u4 = wu.reshape(8, 128, NF, 128).transpose(2, 1, 0, 3)
        com[f"wgu{i}"] = np.ascontiguousarray(np.stack([g4, u4], axis=2)).reshape(NF, 128, 2 * 8 * 128)
        d4 = wd.reshape(NF, 128, 8, 128).transpose(2, 1, 0, 3)
        com[f"wd{i}"] = np.ascontiguousarray(d4).reshape(8, 128, NF * 128)
    win = f(inp["w_in"])[0]
    w4 = win.reshape(8, 128, 16, 128).transpose(2, 1, 0, 3)
    sel = list(range(8)) + list(range(12, 16))
    com["winf"] = np.ascontiguousarray(w4[sel]).reshape(12, 128, 8 * 128)
    wv = win[:, 1024:1536].reshape(8, 128, 512).transpose(1, 0, 2)
    com["winv"] = np.ascontiguousarray(wv).reshape(128, 8 * 512)
    wo = f(inp["w_out"])[0]
    o4 = wo.reshape(8, 128, 8, 128).transpose(2, 1, 0, 3)
    com["wout"] = np.ascontiguousarray(o4).reshape(8, 128, 8 * 128)
    wgl = f(inp["ssm_w_glu"])[0]
    l4 = wgl.reshape(4, 128, 4, 128).transpose(2, 1, 0, 3)
    com["wglu"] = np.ascontiguousarray(l4).reshape(4, 128, 4 * 128)
    a_re, a_im = f(inp["ssm_a_re"])[0], f(inp["ssm_a_im"])[0]
    ldt = f(inp["ssm_log_dt"])[0]
    b_re, b_im = f(inp["ssm_b_re"])[0], f(inp["ssm_b_im"])[0]
    c_re, c_im = f(inp["ssm_c_re"])[0], f(inp["ssm_c_im"])[0]
    dsk = f(inp["ssm_d"])[0]
    s5p = np.zeros((128, S5P_COLS), np.float32)
    lay_a = lambda a: a.reshape(16, 2, 64).transpose(1, 2, 0).reshape(128, 16)
    s5p[:, 0:16] = lay_a(a_re)
    s5p[:, 16:32] = lay_a(a_im)
    s5p[:, 32:48] = np.broadcast_to(ldt.reshape(16, 2).T[:, None, :], (2, 64, 16)).reshape(128, 16)
    lay_b = lambda b: b.reshape(16, 2, 64, 16).transpose(1, 2, 0, 3).reshape(128, 256)
    lay_c = lambda cc: cc.reshape(16, 2, 16, 64).transpose(1, 3, 0, 2).reshape(128, 256)
    s5p[:, 48:304] = lay_b(b_re)
    s5p[:, 304:560] = lay_b(b_im)
    s5p[:, 560:816] = lay_c(c_re)
    s5p[:, 816:1072] = lay_c(c_im)
    kk = np.stack([7.0 - np.arange(8), np.arange(8) - 7.0, np.arange(8) + 1.0]).astype(np.float32)
    s5p[:, 1072:1096] = kk.reshape(1, 24)
    s5p[:, 1096:1128] = np.broadcast_to(dsk.reshape(32, 16).T[None], (8, 16, 32)).reshape(128, 32)
    com["s5p"] = s5p
    sg = np.arange(128) // 16
    com["cmask"] = (sg[None, :] >= sg[:, None]).astype(np.float32)
    r = np.arange(128)
    esel = np.zeros((128, 64, 128), np.float32)
    for glo in range(8):
        for sgm in range(8):
            m = ((r[:, None] // 16 == glo) & (r[:, None] % 16 == r[None, :] % 16) & (r[None, :] // 16 == sgm))
            esel[:, glo * 8 + sgm, :] = m
    com["esel"] = esel.reshape(128, 64 * 128)
    return com


_NC_CACHE = {}


def kernel(**inputs):
    x = np.ascontiguousarray(np.asarray(inputs["x"], dtype=np.float32))
    com = host_layout(inputs)
    if "full" not in _NC_CACHE:
        _NC_CACHE["full"] = build_nc("full")
    nc = _NC_CACHE["full"]
    in_maps = []
    for i in range(NCORES):
        m = dict(com)
        m["x"] = x[i * NSEQ:(i + 1) * NSEQ]
        in_maps.append(m)
    res = run_bass_kernel_spmd(nc, in_maps, core_ids=list(range(NCORES)))
    out = np.concatenate([np.asarray(r["out"]) for r in res.results], axis=0)
    return out.astype(np.float32)
```

```python
import math
import os
from contextlib import ExitStack

import numpy as np
import concourse.bass as bass
import concourse.mybir as mybir
from concourse.bass_utils import run_bass_kernel_spmd

F32 = mybir.dt.float32
BF16 = mybir.dt.bfloat16
AF = mybir.ActivationFunctionType
ALU = mybir.AluOpType

D = 1024
DFF = 2816
NF = DFF // 128
T = 2048
NSEQ = 2
TT = 512
NTT = T // TT
EPS = 1e-6
NCORES = 8

ENGS = ("pe", "act", "dve", "pool", "sp")


class Op:
    __slots__ = ("eng", "fn", "deps", "is_dma", "sig", "sem", "val", "idx", "pre")

    def __init__(self, eng, fn, is_dma):
        self.eng = eng
        self.fn = fn
        self.is_dma = is_dma
        self.deps = set()
        self.sig = False
        self.sem = None
        self.val = 0
        self.pre = None


class Prog:
    def __init__(self, G):
        self.G = G
        self.ops = []
        self.last_w = {}
        self.readers = {}

    def add(self, eng, fn, reads=(), writes=(), dma=False):
        op = Op(eng, fn, dma)
        op.idx = len(self.ops)
        for r in reads:
            w = self.last_w.get(r)
            if w is not None:
                op.deps.add(w)
        for r in writes:
            w = self.last_w.get(r)
            if w is not None:
                op.deps.add(w)
            for q in self.readers.get(r, ()):
                op.deps.add(q)
        for r in reads:
            self.readers.setdefault(r, []).append(op.idx)
        for r in writes:
            self.last_w[r] = op.idx
            self.readers[r] = []
        op.deps.discard(op.idx)
        self.ops.append(op)
        return op.idx

    def emit(self, block):
        ops = self.ops
        G = self.G
        sems, dmasems = G.sems, G.dmasems

        def skip(dop, op):
            return (dop.eng == "pe" and op.eng == "pe" and not dop.is_dma and not op.is_dma)

        for op in ops:
            best = {}
            keep = set()
            for d in op.deps:
                dop = ops[d]
                if skip(dop, op):
                    continue
                if dop.is_dma:
                    keep.add(d)
                elif best.get(dop.eng, -1) < d:
                    best[dop.eng] = d
            keep.update(best.values())
            op.deps = keep
            for d in keep:
                ops[d].sig = True
        last_compute = {}
        for op in ops:
            if not op.is_dma and op.eng in ("pe", "act", "dve", "pool"):
                last_compute[op.eng] = op
        for op in last_compute.values():
            op.sig = True
        cnt, dcnt = G.cnt, G.dcnt
        for op in ops:
            if op.is_dma:
                i = dcnt[op.eng]
                dcnt[op.eng] += 1
                ring = dmasems[op.eng]
                op.sem = ring[i % len(ring)]
                op.val = 16 * (i // len(ring) + 1)
                op.pre = (op.sem, op.val - 16)
            elif op.sig:
                cnt[op.eng] += 1
                op.sem = sems[op.eng]
                op.val = cnt[op.eng]
        per_eng = {e: [o for o in ops if o.eng == e] for e in ENGS}
        if os.environ.get("KVERB"):
            print("sem counts", cnt, "dma counts", dcnt, "nops", len(ops))

        def run(engname, eng):
            waited = G.waited[engname]
            for op in per_eng[engname]:
                need = {}
                for d in op.deps:
                    dop = ops[d]
                    if dop.sem is None:
                        continue
                    k = id(dop.sem)
                    if need.get(k, (None, 0))[1] < dop.val:
                        need[k] = (dop.sem, dop.val)
                if op.pre is not None and op.pre[1] > 0:
                    k = id(op.pre[0])
                    if need.get(k, (None, 0))[1] < op.pre[1]:
                        need[k] = op.pre
                for k, (s, v) in need.items():
                    if waited.get(k, 0) < v:
                        eng.wait_ge(s, v)
                        waited[k] = v
                inst = op.fn(eng)
                if op.is_dma:
                    inst.then_inc(op.sem, 16)
                elif op.sig:
                    inst.then_inc(op.sem, 1)
            last = {}
            for op in per_eng[engname]:
                if op.is_dma:
                    last[id(op.sem)] = (op.sem, op.val)
            for k, (s, v) in last.items():
                if waited.get(k, 0) < v:
                    eng.wait_ge(s, v)
                    waited[k] = v
            lc = last_compute.get(engname)
            if lc is not None and waited.get(id(lc.sem), 0) < lc.val:
                eng.wait_ge(lc.sem, lc.val)
                waited[id(lc.sem)] = lc.val

        @block.tensor
        def _(e):
            run("pe", e)

        @block.scalar
        def _(e):
            run("act", e)

        @block.vector
        def _(e):
            run("dve", e)

        @block.gpsimd
        def _(e):
            run("pool", e)

        @block.sync
        def _(e):
            run("sp", e)


class Globals:
    def __init__(self, nc, es):
        self.sems = {e: es.enter_context(nc.semaphore(f"s_{e}")) for e in ENGS}
        ring = {"sp": 24, "pool": 8, "act": 4, "pe": 1, "dve": 1}
        self.dmasems = {e: [es.enter_context(nc.semaphore(f"d_{e}{i}")) for i in range(ring[e])]
                        for e in ENGS}
        self.cnt = {e: 0 for e in ENGS}
        self.dcnt = {e: 0 for e in ENGS}
        self.waited = {e: {} for e in ENGS}
        self.uid = 0
        self.base = (nc.sbuf_base + 31) // 32 * 32
        self.arena = es.enter_context(nc.sbuf_tensor("arena", [128, SB_LIMIT // 4], F32))


SB_LIMIT = 207 * 1024
DT_SIZE = {F32: 4, BF16: 2}


class Phase:
    def __init__(self, nc, G, name):
        self.nc = nc
        self.G = G
        self.name = name
        self.es = ExitStack()
        self.P = Prog(G)
        self.cur = 0

    def __enter__(self):
        self.es.__enter__()
        return self

    def sb(self, name, shape, dt, at=None):
        n = 1
        for d in shape[1:]:
            n *= d
        size = (n * DT_SIZE[dt] + 31) // 32 * 32
        if at is None:
            at = self.cur
        self.cur = max(self.cur, at + size)
        assert at + size <= SB_LIMIT, (self.name, name, at + size)
        self.G.uid += 1
        return self.nc.alloc_sbuf_tensor_at(f"{self.name}_{name}_{self.G.uid}", shape, dt,
                                            offset=self.G.base + at)

    def ps(self, name, shape, dt=F32):
        return self.es.enter_context(self.nc.psum_tensor(f"{self.name}_{name}", shape, dt))

    def finish(self):
        with self.nc.Block(no_gpsimd_drain=True) as block:
            self.P.emit(block)

    def __exit__(self, *a):
        return self.es.__exit__(*a)


class Ctx:
    pass


def declare_dram(nc, debug=False):
    c = Ctx()
    di = lambda n, s, d=F32: nc.dram_tensor(n, s, d, kind="ExternalInput").ap()
    sc = lambda n, s, d=BF16: nc.dram_tensor(n, s, d).ap()
    c.x = di("x", [NSEQ, T, D])
    c.out = nc.dram_tensor("out", [NSEQ, T, D], F32, kind="ExternalOutput").ap()
    if debug:
        c.dbg = nc.dram_tensor("dbg", [NSEQ, 128, 4 * T], BF16, kind="ExternalOutput").ap()
    c.ident = di("ident", [128, 128])
    c.maskneg = di("maskneg", [128, 128])
    c.gains = di("gains", [128, 8, 8])
    c.lamv = di("lamv", [128, 4, 64])
    c.gsub = di("gsub", [128, 128])
    c.vecs = di("vecs", [128, 8])
    c.wgu = [di(f"wgu{i}", [NF, 128, 2 * 8 * 128]) for i in range(2)]
    c.wd = [di(f"wd{i}", [8, 128, NF * 128]) for i in range(2)]
    c.wgub = [sc(f"wgub{i}", [NF, 128, 2 * 8 * 128]) for i in range(2)]
    c.wdb = [sc(f"wdb{i}", [8, 128, NF * 128]) for i in range(2)]
    c.winf = di("winf", [12, 128, 8 * 128])
    c.winv = di("winv", [128, 8 * 512])
    c.wout = di("wout", [8, 128, 8 * 128])
    c.wglu = di("wglu", [4, 128, 4 * 128])
    c.winfb = sc("winfb", [12, 128, 8 * 128])
    c.winvb = sc("winvb", [128, 8 * 512])
    c.woutb = sc("woutb", [8, 128, 8 * 128])
    c.wglub = sc("wglub", [4, 128, 4 * 128])
    c.s5p = di("s5p", [128, S5P_COLS])
    c.cmask = di("cmask", [128, 128])
    c.esel = di("esel", [128, 64 * 128])
    c.eb = sc("eb", [128, 64 * 128])
    c.xt7b = sc("xt7b", [128, 2 * 32 * 64])
    c.ttb = sc("ttb", [128, 32 * 128])
    c.ypb = sc("ypb", [2, 128, 2 * 16 * 128])
    return c


S5P_COLS = 1128
G_FF1_PRE, G_FF1_POST, G_MIX_PRE, G_MIX_POST, G_FF2_PRE, G_FF2_POST = range(6)


def emit_cast_weights(ph, c, which, part="all"):
    P = ph.P
    for f in range(NF if part in ("all", "gu") else 0):
        P.add("pool", lambda e, f=f: e.dma_start(out=c.wgub[which][f], in_=c.wgu[which][f]),
              writes=[("wgub", which, f)], dma=True)
    for m in range(8 if part in ("all", "d") else 0):
        src = c.wd[which][m].rearrange("p (a b) -> p a b", a=2)
        dst = c.wdb[which][m].rearrange("p (a b) -> p a b", a=2)
        P.add("pool", lambda e, src=src, dst=dst: e.dma_start(out=dst, in_=src),
              writes=[("wdb", which, m)], dma=True)


def emit_cast_mixer_weights(ph, c):
    P = ph.P
    for j in range(12):
        P.add("pool", lambda e, j=j: e.dma_start(out=c.winfb[j], in_=c.winf[j]),
              writes=[("winfb", j)], dma=True)
    src = c.winv.rearrange("p (a b) -> p a b", a=2)
    dst = c.winvb.rearrange("p (a b) -> p a b", a=2)
    P.add("pool", lambda e: e.dma_start(out=dst, in_=src), writes=["winvb"], dma=True)
    for m in range(8):
        P.add("pool", lambda e, m=m: e.dma_start(out=c.woutb[m], in_=c.wout[m]),
              writes=[("woutb", m)], dma=True)
    for m in range(4):
        P.add("pool", lambda e, m=m: e.dma_start(out=c.wglub[m], in_=c.wglu[m]),
              writes=[("wglub", m)], dma=True)


def load_x_blocks(ph, c, K, seq, act_only=False):
    P = ph.P
    for tb in range(T // 128):
        slot = tb % 2
        xin = K.xio[slot]
        P.add("sp", lambda e, tb=tb, xin=xin: e.dma_start(out=xin[:], in_=c.x[seq, tb * 128:(tb + 1) * 128, :]),
              writes=[("xio", slot)], dma=True)
        for half in range(2):
            bank = K.psA[(2 * tb + half) % 2]
            bname = ("psA", (2 * tb + half) % 2)
            for j in range(4):
                cc = half * 4 + j
                P.add("pe", lambda e, bank=bank, j=j, cc=cc, xin=xin:
                      e.transpose(bank[:, j * 128:(j + 1) * 128], xin[:, cc * 128:(cc + 1) * 128], K.ident[:]),
                      reads=[("xio", slot), "ident"], writes=[bname])
            eng = "act" if (half == 0 or act_only) else "dve"
            dst = K.XT[:, half * 4:(half + 1) * 4, tb * 128:(tb + 1) * 128]
            srcv = bank[:].rearrange("p (j t) -> p j t", j=4)
            wr = [("XT", tb // 4, q) for q in range(half * 4, half * 4 + 4)]
            if eng == "act":
                P.add("act", lambda e, dst=dst, srcv=srcv: e.activation(dst, srcv, AF.Copy),
                      reads=[bname], writes=wr)
            else:
                P.add("dve", lambda e, dst=dst, srcv=srcv: e.tensor_copy(dst, srcv),
                      reads=[bname], writes=wr)
        yield tb


def emit_load_x(ph, c, K, seq):
    for _ in load_x_blocks(ph, c, K, seq):
        pass


def emit_store_x(ph, c, K, seq):
    P = ph.P
    for tb in range(T // 128):
        slot = tb % 2
        xo = K.xio[slot]
        for half in range(2):
            bank = K.psA[(2 * tb + half) % 2]
            bname = ("psA", (2 * tb + half) % 2)
            for j in range(4):
                cc = half * 4 + j
                P.add("pe", lambda e, bank=bank, j=j, cc=cc, tb=tb:
                      e.transpose(bank[:, j * 128:(j + 1) * 128], K.XT[:, cc, tb * 128:(tb + 1) * 128], K.ident[:]),
                      reads=[("XT", tb // 4, cc), "ident"], writes=[bname])
            dst = xo[:, half * 512:(half + 1) * 512]
            if half == 0:
                P.add("act", lambda e, dst=dst, bank=bank: e.activation(dst, bank[:], AF.Copy),
                      reads=[bname], writes=[("xio", slot)])
            else:
                P.add("dve", lambda e, dst=dst, bank=bank: e.tensor_copy(dst, bank[:]),
                      reads=[bname], writes=[("xio", slot)])
        P.add("sp", lambda e, tb=tb, xo=xo: e.dma_start(out=c.out[seq, tb * 128:(tb + 1) * 128, :], in_=xo[:]),
              reads=[("xio", slot)], writes=[("out", seq, tb)], dma=True)


def emit_rstd(P, K, ps_stat, sq_res, dim, out_rstd, out_name, nch=8):
    for cc in range(nch):
        P.add("pe", lambda e, cc=cc: e.matmul(ps_stat[:], K.ones[:], K.sq[:, cc, :], start=(cc == 0), stop=(cc == nch - 1)),
              reads=[(sq_res, cc), "ones"], writes=["ps_stat"])
    P.add("act", lambda e: e.activation(K.rt[:], ps_stat[:], AF.Sqrt, bias=K.epsb[:], scale=1.0 / dim),
          reads=["ps_stat", "epsb"], writes=["rt"])
    P.add("dve", lambda e: e.reciprocal(out_rstd[:], K.rt[:]), reads=["rt"], writes=[out_name])


import os
DBG = int(os.environ.get("KDBG", "9"))


def emit_ffn(ph, c, K, which, gpre, gpost, direct_cast=False, x_gen=None):
    P = ph.P
    st = {"wl": 0, "dl": 0}

    def prenorm(tt):
        ts = slice(tt * TT, (tt + 1) * TT)
        xn = K.xnT2[tt % 2]
        for cc in range(8):
            P.add("act", lambda e, cc=cc: e.activation(K.sq[:, cc, :], K.XT[:, cc, ts], AF.Square),
                  reads=[("XT", tt, cc)], writes=[("sq", cc)])
        emit_rstd(P, K, K.ps_stat, "sq", D, K.rstd, "rstd")
        for cc in range(8):
            P.add("dve", lambda e, cc=cc: e.scalar_tensor_tensor(
                out=xn[:, cc, :], in0=K.XT[:, cc, ts], scalar=K.gains[:, gpre, cc:cc + 1],
                in1=K.rstd[:], op0=ALU.mult, op1=ALU.mult),
                reads=[("XT", tt, cc), "rstd", "gains"], writes=[("xnT", tt % 2, cc)])

    def gateup(tt, f):
        xn = K.xnT2[tt % 2]
        slot = st["wl"] % K.NWGU
        st["wl"] += 1
        wt = K.wgu[slot]
        if direct_cast and tt == 0:
            P.add("pool", lambda e: e.dma_start(out=wt[:], in_=c.wgu[which][f]),
                  writes=[("wgu", slot)], dma=True)
            P.add("sp", lambda e: e.dma_start(out=c.wgub[which][f], in_=wt[:]),
                  reads=[("wgu", slot)], writes=[("wgub", which, f)], dma=True)
        else:
            P.add("sp", lambda e: e.dma_start(out=wt[:], in_=c.wgub[which][f]),
                  reads=[("wgub", which, f)], writes=[("wgu", slot)], dma=True)
        pg = K.psA[f % 2]
        pu = K.psB[f % 2]
        for cc in range(8):
            P.add("pe", lambda e, cc=cc: e.matmul(
                pg[:], wt[:, cc * 128:(cc + 1) * 128], xn[:, cc, :], start=(cc == 0), stop=(cc == 7)),
                reads=[("wgu", slot), ("xnT", tt % 2, cc)], writes=[("psA", f % 2)])
        for cc in range(8):
            P.add("pe", lambda e, cc=cc: e.matmul(
                pu[:], wt[:, (8 + cc) * 128:(9 + cc) * 128], xn[:, cc, :], start=(cc == 0), stop=(cc == 7)),
                reads=[("wgu", slot), ("xnT", tt % 2, cc)], writes=[("psB", f % 2)])
        sg = K.sg[f % 2]
        P.add("act", lambda e: e.activation(sg[:], pg[:], AF.Silu),
              reads=[("psA", f % 2)], writes=[("sg", f % 2)])
        P.add("dve", lambda e: e.tensor_tensor(K.hT[:, f, :], sg[:], pu[:], ALU.mult),
              reads=[("sg", f % 2), ("psB", f % 2)], writes=[("hT", f)])

    def down(tt, m):
        slot = st["dl"] % K.NWD
        st["dl"] += 1
        wt = K.wd[slot]
        if direct_cast and tt == 0:
            P.add("pool", lambda e: e.dma_start(out=wt[:].rearrange("p (a b) -> p a b", a=2),
                                                in_=c.wd[which][m].rearrange("p (a b) -> p a b", a=2)),
                  writes=[("wd", slot)], dma=True)
            P.add("sp", lambda e: e.dma_start(out=c.wdb[which][m], in_=wt[:]),
                  reads=[("wd", slot)], writes=[("wdb", which, m)], dma=True)
        else:
            P.add("sp", lambda e: e.dma_start(out=wt[:], in_=c.wdb[which][m]),
                  reads=[("wdb", which, m)], writes=[("wd", slot)], dma=True)
        py = K.psC[m % 2]
        for f in range(NF):
            P.add("pe", lambda e, f=f: e.matmul(
                py[:], wt[:, f * 128:(f + 1) * 128], K.hT[:, f, :], start=(f == 0), stop=(f == NF - 1)),
                reads=[("wd", slot), ("hT", f)], writes=[("psC", m % 2)])
        P.add("dve", lambda e: e.tensor_copy(K.yT[:, m, :], py[:]),
              reads=[("psC", m % 2)], writes=[("yT", m)])
        P.add("act", lambda e: e.activation(K.sq[:, m, :], K.yT[:, m, :], AF.Square),
              reads=[("yT", m)], writes=[("sq", m)])

    def post_piece(tt, k):
        ts = slice(tt * TT, (tt + 1) * TT)
        if k == 0:
            emit_rstd(P, K, K.ps_stat, "sq", D, K.rstd2, "rstd2")
            return
        for m in (k - 1,):
            tmp = K.tmp[m % 4]
            P.add("dve", lambda e, m=m, tmp=tmp: e.scalar_tensor_tensor(
                out=tmp[:], in0=K.yT[:, m, :], scalar=K.gains[:, gpost, m:m + 1],
                in1=K.rstd2[:], op0=ALU.mult, op1=ALU.mult),
                reads=[("yT", m), "rstd2", "gains"], writes=[("tmp", m % 4)])
            P.add("pool", lambda e, m=m, tmp=tmp: e.tensor_tensor(
                K.XT[:, m, ts], K.XT[:, m, ts], tmp[:], ALU.add),
                reads=[("tmp", m % 4), ("XT", tt, m)], writes=[("XT", tt, m)])

    prenorm(0)
    for tt in range(NTT):
        for f in range(NF):
            gateup(tt, f)
            if x_gen is not None and tt == 0:
                next(x_gen, None)
            if tt > 0 and f < 9:
                post_piece(tt - 1, f)
            if f == 9 and tt + 1 < NTT:
                prenorm(tt + 1)
        for m in range(8):
            down(tt, m)
        if direct_cast and tt == 0:
            emit_cast_mixer_weights(ph, c)
            pass
    for k in range(9):
        post_piece(NTT - 1, k)


LAM_INIT = 0.8 - 0.6 * math.exp(-0.3 * 0)
COMMON_END = 68 * 1024


def alloc_common(ph, c, K):
    K.XT = ph.sb("XT", [128, 8, T], F32)
    K.ident = ph.sb("ident", [128, 128], F32)
    K.identb = ph.sb("identb", [128, 128], BF16)
    K.maskneg = ph.sb("maskneg", [128, 128], BF16)
    K.gains = ph.sb("gains", [128, 8, 8], F32)
    K.ones = ph.sb("ones", [128, 128], BF16)
    K.epsb = ph.sb("epsb", [128, 1], F32)
    K.nlam = ph.sb("nlam", [128, 1], F32)
    K.gsubb = ph.sb("gsubb", [128, 128], F32)
    K.vecs = ph.sb("vecs", [128, 8], F32)
    K.a8 = ph.sb("a8", [128, 2, 16], F32)
    K.a8b = ph.sb("a8b", [128, 2, 16], F32)
    K.mhalf = ph.sb("mhalf", [128, 1], F32)
    assert ph.cur <= COMMON_END
    ph.cur = COMMON_END


def emit_consts(ph, c, K):
    P = ph.P
    P.add("sp", lambda e: e.dma_start(out=K.ident[:], in_=c.ident), writes=["ident"], dma=True)
    P.add("sp", lambda e: e.dma_start(out=K.gains[:], in_=c.gains), writes=["gains"], dma=True)
    P.add("sp", lambda e: e.dma_start(out=K.gsubb[:], in_=c.gsub), writes=["gsubb"], dma=True)
    P.add("sp", lambda e: e.dma_start(out=K.vecs[:], in_=c.vecs), writes=["vecs"], dma=True)
    P.add("pool", lambda e: e.dma_start(out=K.maskneg[:], in_=c.maskneg), writes=["maskneg"], dma=True)
    P.add("pool", lambda e: e.memset(K.ones[:], 1.0), writes=["ones"])
    P.add("pool", lambda e: e.memset(K.epsb[:], EPS), writes=["epsb"])
    P.add("pool", lambda e: e.memset(K.mhalf[:], -0.5), writes=["mhalf"])
    P.add("dve", lambda e: e.tensor_copy(K.identb[:], K.ident[:]), reads=["ident"], writes=["identb"])
    for g in (G_FF1_POST, G_FF2_POST):
        P.add("dve", lambda e, g=g: e.tensor_scalar(K.gains[:, g, :], K.gains[:, g, :], 0.5, None, ALU.mult),
              reads=["gains"], writes=["gains"])
    P.add("dve", lambda e: e.tensor_scalar(K.gsubb[:], K.gsubb[:], 1.0 - LAM_INIT, None, ALU.mult),
          reads=["gsubb"], writes=["gsubb"])
    lamv = ph.sb("lamv", [128, 4, 64], F32)
    junk = ph.sb("lamjunk", [128, 64], F32)
    ss = ph.sb("lamss", [128, 4], F32)
    P.add("sp", lambda e: e.dma_start(out=lamv[:], in_=c.lamv), writes=["lamv"], dma=True)
    for i in range(2):
        P.add("dve", lambda e, i=i: e.tensor_tensor(junk[:], lamv[:, 2 * i, :], lamv[:, 2 * i + 1, :], ALU.mult),
              reads=["lamv"], writes=["lamjunk"])
        P.add("dve", lambda e, i=i: e.tensor_reduce(ss[:, i:i + 1], junk[:], mybir.AxisListType.X, ALU.add),
              reads=["lamjunk"], writes=[("lamss", i)])
        P.add("act", lambda e, i=i: e.activation(ss[:, 2 + i:3 + i], ss[:, i:i + 1], AF.Exp),
              reads=[("lamss", i)], writes=[("lamss", 2 + i)])
    P.add("dve", lambda e: e.tensor_tensor(K.nlam[:], ss[:, 3:4], ss[:, 2:3], ALU.subtract),
          reads=[("lamss", 2), ("lamss", 3)], writes=["nlam"])
    P.add("dve", lambda e: e.tensor_scalar(K.nlam[:], K.nlam[:], -LAM_INIT, None, ALU.add),
          reads=["nlam"], writes=["nlam"])


def alloc_ffn(ph, K):
    K.xio = [ph.sb(f"xio{i}", [128, D], F32) for i in range(2)]
    K.sq = ph.sb("sq", [128, 8, TT], BF16)
    K.xnT2 = [ph.sb(f"xnT{i}", [128, 8, TT], BF16) for i in range(2)]
    K.hT = ph.sb("hT", [128, NF, TT], BF16)
    K.yT = ph.sb("yT", [128, 8, TT], F32)
    K.rt = ph.sb("rt", [128, TT], F32)
    K.rstd = ph.sb("rstd", [128, TT], F32)
    K.rstd2 = ph.sb("rstd2", [128, TT], F32)
    K.sg = [ph.sb(f"sg{i}", [128, TT], F32) for i in range(2)]
    K.tmp = [ph.sb(f"tmp{i}", [128, TT], F32) for i in range(4)]
    K.NWGU = 6
    K.NWD = 4
    K.wgu = [ph.sb(f"wgu{i}", [128, 2 * 8 * 128], BF16) for i in range(K.NWGU)]
    K.wd = [ph.sb(f"wd{i}", [128, NF * 128], BF16) for i in range(K.NWD)]
    K.psA = [ph.ps(f"psA{i}", [128, 512]) for i in range(2)]
    K.psB = [ph.ps(f"psB{i}", [128, 512]) for i in range(2)]
    K.psC = [ph.ps(f"psC{i}", [128, 512]) for i in range(2)]
    K.ps_stat = ph.ps("ps_stat", [128, 512])


M_UT = COMMON_END + 0
M_QT = COMMON_END + 16384
M_KT = COMMON_END + 32768
M_VA = COMMON_END + 49152
M_OTA = COMMON_END + 65792
M_TMP = COMMON_END + 82176
M_UBAR = COMMON_END + 16384
M_SBF = COMMON_END + 32768
M_GT = COMMON_END + 0


def emit_prenorm(P, K, tt, gidx):
    ts = slice(tt * TT, (tt + 1) * TT)
    XTr = lambda q: ("XT", tt, q)
    xn = K.xnT2[tt % 2]
    for cc in range(8):
        P.add("act", lambda e, cc=cc: e.activation(K.sq[:, cc, :], K.XT[:, cc, ts], AF.Square),
              reads=[XTr(cc)], writes=[("sq", cc)])
    emit_rstd(P, K, K.ps_stat, "sq", D, K.rstd, "rstd")
    for cc in range(8):
        P.add("dve", lambda e, cc=cc: e.scalar_tensor_tensor(
            out=xn[:, cc, :], in0=K.XT[:, cc, ts], scalar=K.gains[:, gidx, cc:cc + 1],
            in1=K.rstd[:], op0=ALU.mult, op1=ALU.mult),
            reads=[XTr(cc), "rstd", "gains"], writes=[("xnT", tt % 2, cc)])


def emit_proj(ph, c, K):
    P = ph.P
    K.uT = ph.sb("uT", [128, 4, T], BF16, at=M_UT)
    K.qT = ph.sb("qT", [128, 4, T], BF16, at=M_QT)
    K.kT = ph.sb("kT", [128, 4, T], BF16, at=M_KT)
    K.vA = ph.sb("vA", [128, 16, 4, 130], BF16, at=M_VA)
    ph.cur = M_TMP
    K.sq = ph.sb("sq", [128, 8, TT], BF16)
    K.xnT2 = [ph.sb(f"xnT{i}", [128, 8, TT], BF16) for i in range(2)]
    K.rt = ph.sb("rt", [128, TT], F32)
    K.rstd = ph.sb("rstd", [128, TT], F32)
    NW = 8
    wr = [ph.sb(f"wr{i}", [128, 8 * 128], BF16) for i in range(NW)]
    winv = ph.sb("winv", [128, 8 * 512], BF16)
    K.ps_stat = ph.ps("ps_stat", [128, 512])
    NPA = 4
    psA = [ph.ps(f"psA{i}", [128, 512]) for i in range(NPA)]
    psV = [ph.ps(f"psV{i}", [128, 512]) for i in range(2)]
    P.add("sp", lambda e: e.dma_start(out=winv[:], in_=c.winvb), reads=["winvb"], writes=["winv"], dma=True)
    P.add("pool", lambda e: e.memset(K.vA[:, :, :, 128:129], 1.0), writes=["vAones"])
    wl = 0
    ev = 0
    emit_prenorm(P, K, 0, G_MIX_PRE)
    for tt in range(NTT):
        ts = slice(tt * TT, (tt + 1) * TT)
        xn = K.xnT2[tt % 2]
        for j in range(12):
            if j == 6 and tt + 1 < NTT:
                emit_prenorm(P, K, tt + 1, G_MIX_PRE)
            slot = wl % NW
            wl += 1
            wt = wr[slot]
            P.add("sp", lambda e, wt=wt, j=j: e.dma_start(out=wt[:], in_=c.winfb[j]),
                  reads=[("winfb", j)], writes=[("wr", slot)], dma=True)
            pa = psA[j % NPA]
            for cc in range(8):
                P.add("pe", lambda e, wt=wt, cc=cc, pa=pa, xn=xn: e.matmul(
                    pa[:], wt[:, cc * 128:(cc + 1) * 128], xn[:, cc, :], start=(cc == 0), stop=(cc == 7)),
                    reads=[("wr", slot), ("xnT", tt % 2, cc)], writes=[("psA", j % NPA)])
            if j < 4:
                dst, res = K.qT[:, j, ts], ("qT", j, tt)
            elif j < 8:
                dst, res = K.kT[:, j - 4, ts], ("kT", j - 4, tt)
            else:
                uTp = K.uT[:].rearrange("p j (s n) -> p j s n", s=8)
                dst, res = uTp[:, j - 8, :, tt * 64:(tt + 1) * 64], ("uT", j - 8, tt)
            srcp = pa[:] if j < 8 else pa[:].rearrange("p (n s) -> p s n", s=8)
            if ev % 2 == 0:
                P.add("act", lambda e, dst=dst, srcp=srcp: e.activation(dst, srcp, AF.Copy),
                      reads=[("psA", j % NPA)], writes=[res])
            else:
                P.add("dve", lambda e, dst=dst, srcp=srcp: e.tensor_copy(dst, srcp),
                      reads=[("psA", j % NPA)], writes=[res])
            ev += 1
        for b4 in range(4):
            tb = tt * 4 + b4
            pv = psV[b4 % 2]
            for cc in range(8):
                P.add("pe", lambda e, cc=cc, pv=pv, b4=b4, xn=xn: e.matmul(
                    pv[:], xn[:, cc, b4 * 128:(b4 + 1) * 128], winv[:, cc * 512:(cc + 1) * 512],
                    start=(cc == 0), stop=(cc == 7)),
                    reads=["winv", ("xnT", tt % 2, cc)], writes=[("psV", b4 % 2)])
            dst = K.vA[:, tb, :, 0:128]
            srcv = pv[:].rearrange("p (h v) -> p h v", h=4)
            if ev % 2 == 0:
                P.add("act", lambda e, dst=dst, srcv=srcv: e.activation(dst, srcv, AF.Copy),
                      reads=[("psV", b4 % 2)], writes=[("vA", tb)])
            else:
                P.add("dve", lambda e, dst=dst, srcv=srcv: e.tensor_copy(dst, srcv),
                      reads=[("psV", b4 % 2)], writes=[("vA", tb)])
            ev += 1


def emit_attn(ph, c, K):
    P = ph.P
    K.qT = ph.sb("qT", [128, 4, T], BF16, at=M_QT)
    K.kT = ph.sb("kT", [128, 4, T], BF16, at=M_KT)
    K.vA = ph.sb("vA", [128, 16, 4, 130], BF16, at=M_VA)
    K.oTa = ph.sb("oTa", [128, 4, T], BF16, at=M_OTA)
    ph.cur = M_TMP
    NPT = 6
    pT = [ph.sb(f"pT{i}", [128, 512], BF16) for i in range(NPT)]
    o1 = [ph.sb(f"o1_{i}", [128, 4, 128], F32) for i in range(2)]
    od = [ph.sb(f"od{i}", [128, 128], F32) for i in range(8)]
    junk = [ph.sb(f"junk{i}", [128, 128], F32) for i in range(4)]
    sm = [ph.sb(f"sm{i}", [128, 8], F32) for i in range(8)]
    NPS = 4
    psS = [ph.ps(f"psS{i}", [128, 512]) for i in range(NPS)]
    acc = [[ph.ps(f"acc{r}{b}", [128, 512]) for b in range(2)] for r in range(2)]
    ob_tok = ph.sb("ob_tok", [128, 16, 4, 128], BF16)
    items = []
    si = 0
    pi = 0
    rnd = 0
    fin = 0
    for h in range(4):
        for qt in range(NTT):
            for cmap in range(2):
                r = rnd % 2
                rnd += 1
                rows = slice(cmap * 64, (cmap + 1) * 64)
                started = [False, False]
                for kb in range(4 * qt + 4):
                    j = kb - 4 * qt
                    qlo = max(j, 0) * 128
                    sb_ = psS[si % NPS]
                    sres = ("psS", si % NPS)
                    si += 1
                    pt = pT[pi % NPT]
                    pres = ("pT", pi % NPT)
                    pi += 1

                    def s_part(sb_=sb_, sres=sres, qlo=qlo, kb=kb, rows=rows, h=h, qt=qt, j=j):
                        P.add("pe", lambda e: e.matmul(
                            sb_[:, qlo:512], K.kT[rows, h, kb * 128:(kb + 1) * 128],
                            K.qT[rows, h, qt * 512 + qlo:(qt + 1) * 512], start=True, stop=(j < 0)),
                            reads=[("kT", h, kb // 4), ("qT", h, qt)], writes=[sres])
                        if j >= 0:
                            P.add("pe", lambda e: e.matmul(
                                sb_[:, qlo:qlo + 128], K.identb[:], K.maskneg[:], start=False, stop=True,
                                skip_group_check=True),
                                reads=["identb", "maskneg"], writes=[sres])

                    sts = []
                    for qb in range(max(j, 0), 4):
                        sts.append(not started[qb // 2])
                        started[qb // 2] = True

                    def r_part(sb_=sb_, sres=sres, pt=pt, pres=pres, qlo=qlo, kb=kb, h=h, qt=qt, j=j, r=r, sts=sts):
                        P.add("act", lambda e: e.activation(
                            pt[:, qlo:512], sb_[:, qlo:512], AF.Exp, scale=0.125),
                            reads=[sres], writes=[pres])
                        for ii, qb in enumerate(range(max(j, 0), 4)):
                            bank = acc[r][qb // 2]
                            col = (qb % 2) * 256
                            st = sts[ii]
                            P.add("pe", lambda e, bank=bank, col=col, qb=qb, st=st: e.matmul(
                                bank[:, col:col + 129], pt[:, qb * 128:(qb + 1) * 128], K.vA[:, kb, h, 0:129],
                                start=st, stop=(kb == 4 * qt + qb), skip_group_check=True),
                                reads=[pres, ("vA", kb), "vAones"], writes=[("acc", r, qb // 2)])

                    items.append((s_part, r_part))
                items.append((None, (lambda h=h, qt=qt, cmap=cmap, r=r, rnd=rnd: finalize(h, qt, cmap, r, rnd))))
    fin = [0]

    def finalize(h, qt, cmap, r, rnd):
        def acc_of(qb):
            return acc[r][qb // 2], (qb % 2) * 256, ("acc", r, qb // 2)
        par = fin[0] % 2
        fin[0] += 1
        smq = [sm[par * 4 + qb] for qb in range(4)]
        if cmap == 0:
            o1t = o1[(rnd // 2) % 2]
            for qb in range(4):
                bank, col, ares = acc_of(qb)
                P.add("dve", lambda e, qb=qb, bank=bank, col=col: e.reciprocal(
                    smq[qb][:, 0:1], bank[:, col + 128:col + 129]),
                    reads=[ares], writes=[("sm", par, qb, 0)])
            for qb in range(4):
                bank, col, ares = acc_of(qb)
                P.add("dve", lambda e, qb=qb, bank=bank, col=col: e.tensor_scalar(
                    o1t[:, qb, :], bank[:, col:col + 128], smq[qb][:, 0:1], None, ALU.mult),
                    reads=[ares, ("sm", par, qb, 0)], writes=[("o1", (rnd // 2) % 2, qb)])
            return
        o1t = o1[((rnd - 1) // 2) % 2]
        odq = [od[par * 4 + qb] for qb in range(4)]
        for qb in range(4):
            bank, col, ares = acc_of(qb)
            P.add("dve", lambda e, qb=qb, bank=bank, col=col: e.reciprocal(
                smq[qb][:, 1:2], bank[:, col + 128:col + 129]),
                reads=[ares], writes=[("sm", par, qb, 1)])
        for qb in range(4):
            P.add("dve", lambda e, qb=qb: e.tensor_tensor(smq[qb][:, 2:3], smq[qb][:, 1:2], K.nlam[:], ALU.mult),
                  reads=[("sm", par, qb, 1), "nlam"], writes=[("sm", par, qb, 2)])
        for qb in range(4):
            bank, col, ares = acc_of(qb)
            P.add("dve", lambda e, qb=qb, bank=bank, col=col: e.scalar_tensor_tensor(
                out=odq[qb][:], in0=bank[:, col:col + 128], scalar=smq[qb][:, 2:3], in1=o1t[:, qb, :],
                op0=ALU.mult, op1=ALU.add),
                reads=[ares, ("sm", par, qb, 2), ("o1", ((rnd - 1) // 2) % 2, qb)], writes=[("od", par, qb)])
        for qb in range(4):
            P.add("dve", lambda e, qb=qb: e.tensor_tensor(junk[qb][:], odq[qb][:], odq[qb][:], ALU.mult),
                  reads=[("od", par, qb)], writes=[("junk", qb)])
        for qb in range(4):
            P.add("dve", lambda e, qb=qb: e.tensor_reduce(
                smq[qb][:, 3:4], junk[qb][:], mybir.AxisListType.X, ALU.add),
                reads=[("junk", qb)], writes=[("sm", par, qb, 3)])
        for qb in range(4):
            P.add("dve", lambda e, qb=qb: e.tensor_scalar(
                smq[qb][:, 4:5], smq[qb][:, 3:4], 1.0 / 128, EPS, ALU.mult, ALU.add),
                reads=[("sm", par, qb, 3)], writes=[("sm", par, qb, 4)])
        for qb in range(4):
            P.add("pool", lambda e, qb=qb: e.tensor_tensor(smq[qb][:, 5:6], smq[qb][:, 4:5], K.mhalf[:], ALU.pow),
                  reads=[("sm", par, qb, 4), "mhalf"], writes=[("sm", par, qb, 5)])
        for qb in range(4):
            tb = qt * 4 + qb
            P.add("dve", lambda e, qb=qb, tb=tb: e.scalar_tensor_tensor(
                out=ob_tok[:, tb, h, :], in0=odq[qb][:], scalar=smq[qb][:, 5:6], in1=K.gsubb[:],
                op0=ALU.mult, op1=ALU.mult),
                reads=[("od", par, qb), ("sm", par, qb, 5), "gsubb"], writes=[("ob", tb, h)])

    LOOK = NPS - 1
    seq_s = [it for it in items if it[0] is not None]
    ns = 0
    nr = 0
    for it in items:
        if it[0] is None:
            it[1]()
            continue
        while ns < len(seq_s) and ns <= nr + LOOK:
            seq_s[ns][0]()
            ns += 1
        it[1]()
        nr += 1
    for tb in range(16):
        bank = psS[tb % NPS]
        bres = ("psS", tb % NPS)
        for h in range(4):
            P.add("pe", lambda e, bank=bank, tb=tb, h=h: e.matmul(
                bank[:, h * 128:(h + 1) * 128], ob_tok[:, tb, h, :], K.identb[:], start=True, stop=True,
                skip_group_check=True),
                reads=[("ob", tb, h), "identb"], writes=[bres])
        dst = K.oTa[:, :, tb * 128:(tb + 1) * 128]
        srcv = bank[:].rearrange("p (h t) -> p h t", h=4)
        if tb % 2 == 0:
            P.add("act", lambda e, dst=dst, srcv=srcv: e.activation(dst, srcv, AF.Copy),
                  writes=[bres, ("oTa", tb)])
        else:
            P.add("dve", lambda e, dst=dst, srcv=srcv: e.tensor_copy(dst, srcv),
                  writes=[bres, ("oTa", tb)])


TWO_PI = 2.0 * math.pi
MAGIC = 12582912.0


def emit_s5_setup(ph, c, K, hook=None, hook_every=8):
    class _PW:
        def __init__(self, P):
            self.P = P
            self.n = 0
            self.lim = int(os.environ.get("KSTEP", "100000"))

        def add(self, *a, **k):
            self.n += 1
            if self.n > self.lim:
                return None
            if os.environ.get("KSTEPV") and self.n == self.lim:
                import traceback
                traceback.print_stack(limit=4)
            r = self.P.add(*a, **k)
            if hook is not None and self.n % hook_every == 0:
                hook()
            return r
    P = _PW(ph.P)
    sp_ = ph.sb("s5p", [128, S5P_COLS], F32)
    cmask = ph.sb("cmask", [128, 128], F32)
    P.add("sp", lambda e: e.dma_start(out=sp_[:], in_=c.s5p), writes=["s5p"], dma=True)
    P.add("sp", lambda e: e.dma_start(out=cmask[:], in_=c.cmask), writes=["cmask"], dma=True)
    for q in range(4):
        src = c.esel[:, q * 2048:(q + 1) * 2048]
        dst = c.eb[:, q * 2048:(q + 1) * 2048]
        P.add("pool", lambda e, src=src, dst=dst: e.dma_start(out=dst, in_=src), writes=[("eb", q)], dma=True)
    are, aim, ldt = sp_[:, 0:16], sp_[:, 16:32], sp_[:, 32:48]
    bre = sp_[:, 48:304].rearrange("p (g h) -> p g h", g=16)
    bim = sp_[:, 304:560].rearrange("p (g h) -> p g h", g=16)
    cre = sp_[:, 560:816].rearrange("p (g h) -> p g h", g=16)
    cim = sp_[:, 816:1072].rearrange("p (g h) -> p g h", g=16)
    kk = sp_[:, 1072:1096].rearrange("p (w k) -> p w k", w=3)
    dD = sp_[:, 1096:1128]
    cnt = [0]

    def T_(shape):
        cnt[0] += 1
        return ph.sb(f"t{cnt[0]}", shape, F32), f"t{cnt[0]}"

    def tt(out, a, b, op, r, w, eng="dve"):
        P.add(eng, lambda e: e.tensor_tensor(out, a, b, op), reads=r, writes=w)

    def ts(out, a, s1, s2, op0, op1, r, w):
        P.add("dve", lambda e: e.tensor_scalar(out, a, s1, s2, op0, op1), reads=r, writes=w)

    def sin_of(x, xr_, n):
        t, tn = T_([128, n])
        r, rn = T_([128, n])
        ts(t[:], x, 1.0 / TWO_PI, MAGIC, ALU.mult, ALU.add, [xr_], [tn])
        ts(t[:], t[:], -MAGIC, None, ALU.add, ALU.bypass, [tn], [tn])
        P.add("dve", lambda e: e.scalar_tensor_tensor(out=r[:], in0=t[:], scalar=-TWO_PI, in1=x,
                                                       op0=ALU.mult, op1=ALU.add), reads=[tn, xr_], writes=[rn])
        ts(r[:], r[:], 3.141592, -3.141592, ALU.min, ALU.max, [rn], [rn])
        P.add("act", lambda e: e.activation(r[:], r[:], AF.Sin), reads=[rn], writes=[rn])
        return r, rn

    def cexp(xr_t, xr_n, xi_t, xi_n, n, unit=False):
        s_, sn = sin_of(xi_t, xi_n, n)
        x2, x2n = T_([128, n])
        ts(x2[:], xi_t, math.pi / 2, None, ALU.add, ALU.bypass, [xi_n], [x2n])
        c_, cn = sin_of(x2[:], x2n, n)
        if unit:
            return c_, cn, s_, sn
        e_, en = T_([128, n])
        P.add("act", lambda e: e.activation(e_[:], xr_t, AF.Exp), reads=[xr_n], writes=[en])
        tt(c_[:], c_[:], e_[:], ALU.mult, [cn, en], [cn])
        tt(s_[:], s_[:], e_[:], ALU.mult, [sn, en], [sn])
        return c_, cn, s_, sn

    dt, dtn = T_([128, 16])
    P.add("act", lambda e: e.activation(dt[:], ldt, AF.Exp), reads=["s5p"], writes=[dtn])
    xr, xrn = T_([128, 16])
    xi, xin = T_([128, 16])
    tt(xr[:], are, dt[:], ALU.mult, ["s5p", dtn], [xrn])
    tt(xi[:], aim, dt[:], ALU.mult, ["s5p", dtn], [xin])
    if int(os.environ.get('KSET', '9')) < 1:
        return
    c1, c1n, s1, s1n = cexp(xr[:], xrn, xi[:], xin, 16)
    ts(c1[:], c1[:], -1.0, None, ALU.add, ALU.bypass, [c1n], [c1n])
    t1, t1n = T_([128, 16])
    t2, t2n = T_([128, 16])
    rden, rdn = T_([128, 16])
    tt(t1[:], are, are, ALU.mult, ["s5p"], [t1n])
    tt(t2[:], aim, aim, ALU.mult, ["s5p"], [t2n])
    tt(t1[:], t1[:], t2[:], ALU.add, [t1n, t2n], [t1n])
    P.add("dve", lambda e: e.reciprocal(rden[:], t1[:]), reads=[t1n], writes=[rdn])
    cr, crn = T_([128, 16])
    ci, cin = T_([128, 16])
    tt(t1[:], c1[:], are, ALU.mult, [c1n, "s5p"], [t1n])
    tt(t2[:], s1[:], aim, ALU.mult, [s1n, "s5p"], [t2n])
    tt(t1[:], t1[:], t2[:], ALU.add, [t1n, t2n], [t1n])
    tt(cr[:], t1[:], rden[:], ALU.mult, [t1n, rdn], [crn])
    tt(t1[:], s1[:], are, ALU.mult, [s1n, "s5p"], [t1n])
    tt(t2[:], c1[:], aim, ALU.mult, [c1n, "s5p"], [t2n])
    tt(t1[:], t1[:], t2[:], ALU.subtract, [t1n, t2n], [t1n])
    tt(ci[:], t1[:], rden[:], ALU.mult, [t1n, rdn], [cin])
    if int(os.environ.get('KSET', '9')) < 2:
        return
    Br, Brn = T_([128, 16, 16])
    Bi, Bin = T_([128, 16, 16])
    u1, u1n = T_([128, 16, 16])
    crb = cr[:].rearrange("p (g o) -> p g o", o=1).broadcast_to([128, 16, 16])
    cib = ci[:].rearrange("p (g o) -> p g o", o=1).broadcast_to([128, 16, 16])
    tt(Br[:], crb, bre, ALU.mult, [crn, "s5p"], [Brn])
    tt(u1[:], cib, bim, ALU.mult, [cin, "s5p"], [u1n])
    tt(Br[:], Br[:], u1[:], ALU.subtract, [Brn, u1n], [Brn])
    tt(Bi[:], crb, bim, ALU.mult, [crn, "s5p"], [Bin])
    tt(u1[:], cib, bre, ALU.mult, [cin, "s5p"], [u1n])
    tt(Bi[:], Bi[:], u1[:], ALU.add, [Bin, u1n], [Bin])
    if int(os.environ.get('KSET', '9')) < 3:
        return
    k8r, k8rn = T_([128, 16])
    k8i, k8in = T_([128, 16])
    ts(k8r[:], xr[:], 8.0, None, ALU.mult, ALU.bypass, [xrn], [k8rn])
    ts(k8i[:], xi[:], 8.0, None, ALU.mult, ALU.bypass, [xin], [k8in])
    a8c, a8cn, a8s, a8sn = cexp(k8r[:], k8rn, k8i[:], k8in, 16)
    P.add("dve", lambda e: e.tensor_copy(K.a8[:, 0, :], a8c[:]), reads=[a8cn], writes=["a8"])
    P.add("dve", lambda e: e.tensor_copy(K.a8[:, 1, :], a8s[:]), reads=[a8sn], writes=["a8"])
    P.add("dve", lambda e: e.tensor_copy(K.a8b[:, 0, :], a8s[:]), reads=[a8sn], writes=["a8b"])
    ts(K.a8b[:, 1, :], a8s[:], -1.0, None, ALU.mult, ALU.bypass, [a8sn], ["a8b"])
    if os.environ.get('KVERB'):
        print('setup ops before powers', P.n)
    if int(os.environ.get('KSET', '9')) < 4:
        return
    outs = []
    xrb = xr[:].rearrange("p (g o) -> p g o", o=1).broadcast_to([128, 16, 8])
    xib = xi[:].rearrange("p (g o) -> p g o", o=1).broadcast_to([128, 16, 8])
    W4 = [128, 16, 8, 16]
    f1, f1n = T_(W4)
    f2, f2n = T_(W4)
    X7 = [ph.sb(f"X7{i}", W4, BF16) for i in range(2)]
    Zt = [ph.sb(f"Zt{i}", W4, BF16) for i in range(2)]
    Yp = ph.sb("Yp", [128, 2, 16, 128], BF16)
    for w in range(3):
        kb = kk[:, w:w + 1, :].broadcast_to([128, 16, 8])
        kr, krn = T_([128, 16, 8])
        ki, kin = T_([128, 16, 8])
        tt(kr[:], xrb, kb, ALU.mult, [xrn, "s5p"], [krn])
        tt(ki[:], xib, kb, ALU.mult, [xin, "s5p"], [kin])
        pr, prn, pi_, pin = cexp(kr[:].rearrange("p g k -> p (g k)"), krn,
                                 ki[:].rearrange("p g k -> p (g k)"), kin, 128)
        prb = pr[:].rearrange("p (g k o) -> p g k o", g=16, o=1).broadcast_to(W4)
        pib = pi_[:].rearrange("p (g k o) -> p g k o", g=16, o=1).broadcast_to(W4)
        if w == 0:
            mre = Br[:].rearrange("p g (o h) -> p g o h", o=1).broadcast_to(W4)
            mim = Bi[:].rearrange("p g (o h) -> p g o h", o=1).broadcast_to(W4)
            mrn, min_ = Brn, Bin
            ore, oim = X7[0][:], X7[1][:]
            orn, oin = "X7re", "X7im"
        else:
            mre = cre.rearrange("p g (o h) -> p g o h", o=1).broadcast_to(W4)
            mim = cim.rearrange("p g (o h) -> p g o h", o=1).broadcast_to(W4)
            mrn, min_ = "s5p", "s5p"
            if w == 1:
                ore, oim = Zt[0][:], Zt[1][:]
                orn, oin = "Zre", "Zim"
            else:
                ore = Yp[:, 0, :, :].rearrange("p g (k h) -> p g k h", k=8)
                oim = Yp[:, 1, :, :].rearrange("p g (k h) -> p g k h", k=8)
                orn, oin = "Ypre", "Ypim"
        tt(f1[:], prb, mre, ALU.mult, [prn, mrn], [f1n])
        tt(f2[:], pib, mim, ALU.mult, [pin, min_], [f2n])
        tt(ore, f1[:], f2[:], ALU.subtract, [f1n, f2n], [orn])
        tt(f1[:], prb, mim, ALU.mult, [prn, min_], [f1n])
        tt(f2[:], pib, mre, ALU.mult, [pin, mrn], [f2n])
        if w == 0:
            tt(oim, f1[:], f2[:], ALU.add, [f1n, f2n], [oin])
        else:
            P.add("dve", lambda e, oim=oim: e.scalar_tensor_tensor(
                out=oim, in0=f1[:], scalar=-1.0, in1=f2[:], op0=ALU.mult, op1=ALU.subtract),
                reads=[f1n, f2n], writes=[oin])
    Ypm = [ph.sb(f"Ypm{a}", [128, 2, 16, 128], BF16) for a in range(2)]
    for a in range(2):
        keep = slice(a * 64, (a + 1) * 64)
        P.add("pool", lambda e, a=a: e.memset(Ypm[a][:], 0.0), writes=[("Ypmz", a)])
        P.add("dve", lambda e, a=a, keep=keep: e.tensor_copy(Ypm[a][keep], Yp[keep]),
              reads=["Ypre", "Ypim", ("Ypmz", a)], writes=[("Ypm", a)])
        P.add("sp", lambda e, a=a: e.dma_start(out=c.ypb[a], in_=Ypm[a][:].rearrange("p r g k -> p (r g k)")),
              reads=[("Ypm", a)], writes=[("ypb", a)], dma=True)
    if os.environ.get('KVERB'):
        print('setup ops after powers', P.n)
    if int(os.environ.get('KSET', '9')) < 5:
        return
    xt7 = ph.sb("xt7", [128, 2, 32, 64], BF16)
    psX = [ph.ps(f"psX{i}", [128, 512]) for i in range(2)]
    bi = 0
    for ri in range(2):
        for gq in range(4):
            bank = psX[bi % 2]
            bres = ("psX", bi % 2)
            bi += 1
            for gi in range(4):
                gp = gq * 4 + gi
                src = X7[ri][:, gp, :, :].rearrange("p k h -> p (k h)")
                P.add("pe", lambda e, bank=bank, gi=gi, src=src: e.matmul(
                    bank[:, gi * 128:(gi + 1) * 128], src, K.identb[:], start=True, stop=True,
                    skip_group_check=True),
                    reads=["X7re" if ri == 0 else "X7im", "identb"], writes=[bres])
            dst = xt7[:, ri, gq * 8:(gq + 1) * 8, :]
            srcv = bank[:, 0:512].rearrange("p (g q) -> p g q", g=8)
            P.add("act" if gq % 2 == 0 else "dve",
                  (lambda e, dst=dst, srcv=srcv: e.activation(dst, srcv, AF.Copy)) if gq % 2 == 0 else
                  (lambda e, dst=dst, srcv=srcv: e.tensor_copy(dst, srcv)),
                  reads=[bres], writes=[bres, "xt7"])
    P.add("sp", lambda e: e.dma_start(out=c.xt7b, in_=xt7[:].rearrange("p r g q -> p (r g q)")),
          reads=["xt7"], writes=["xt7b"], dma=True)
    if int(os.environ.get('KSET', '9')) < 6:
        return
    ttl = ph.sb("ttl", [128, 32, 128], BF16)
    Zm = [[ph.sb(f"Zm{a}{b}", W4, BF16) for b in range(2)] for a in range(2)]
    for a in range(2):
        keep = slice(a * 64, (a + 1) * 64)
        for b in range(2):
            P.add("pool", lambda e, a=a, b=b: e.memset(Zm[a][b][:], 0.0), writes=[("Zmz", a, b)])
            P.add("dve", lambda e, a=a, b=b, keep=keep: e.tensor_copy(Zm[a][b][keep], Zt[b][keep]),
                  reads=["Zre", "Zim", ("Zmz", a, b)], writes=["Zm"])
    tm = [ph.sb(f"tm{i}", [128, 4, 128], F32) for i in range(2)]
    psT = [ph.ps(f"psTT{i}", [128, 512]) for i in range(2)]
    cmb = cmask[:].rearrange("p (o q) -> p o q", o=1).broadcast_to([128, 4, 128])
    for gq in range(8):
        bank = psT[gq % 2]
        bres = ("psTT", gq % 2)
        for gi in range(4):
            g = gq * 4 + gi
            gp, g2 = g // 2, g % 2
            rows = slice(g2 * 64, (g2 + 1) * 64)
            for ri in range(2):
                lh = X7[ri][:, gp, :, :].rearrange("p k h -> p (k h)")
                rh = Zm[g2][ri][:, gp, :, :].rearrange("p k h -> p (k h)")
                P.add("pe", lambda e, bank=bank, gi=gi, lh=lh, rh=rh, ri=ri, first=(gi == 0 and ri == 0): e.matmul(
                    bank[:, gi * 128:(gi + 1) * 128], lh, rh, start=first, stop=(ri == 1),
                    skip_group_check=True),
                    reads=["X7re", "X7im", "Zm"], writes=[bres])
        tmt = tm[gq % 2]
        P.add("dve", lambda e, tmt=tmt, bank=bank: e.tensor_tensor(
            tmt[:], bank[:].rearrange("p (g q) -> p g q", g=4), cmb, ALU.mult),
            reads=["cmask"], writes=[bres, ("tm", gq % 2)])
        for gi in range(4):
            g = gq * 4 + gi
            P.add("dve", lambda e, tmt=tmt, gi=gi, g=g: e.scalar_tensor_tensor(
                out=ttl[:, g, :], in0=K.ident[:], scalar=dD[:, g:g + 1], in1=tmt[:, gi, :],
                op0=ALU.mult, op1=ALU.add),
                reads=[("tm", gq % 2), "ident", "s5p"], writes=["ttl"])
    P.add("sp", lambda e: e.dma_start(out=c.ttb, in_=ttl[:].rearrange("p g q -> p (g q)")),
          reads=["ttl"], writes=["ttb"], dma=True)


def emit_s5a(ph, c, K):
    P = ph.P
    K.uT = ph.sb("uT", [128, 4, T], BF16, at=M_UT)
    K.Ubar = ph.sb("Ubar", [128, 32, 256], BF16, at=M_UBAR)
    K.Sbf = ph.sb("Sbf", [128, 2, 16, 257], BF16, at=M_SBF)
    ph.cur = M_TMP
    NCH, CL = 9, 32
    GRP = [(0, 5), (5, 9)]
    ph.cur = COMMON_END + 49280
    P1 = [[ph.sb(f"P1_{g}{i}", [128, b - a, 2, 16], F32) for i in range(2)] for g, (a, b) in enumerate(GRP)]
    P2 = [[ph.sb(f"P2_{g}{i}", [128, b - a, 2, 16], F32) for i in range(2)] for g, (a, b) in enumerate(GRP)]
    XT7_OFF = ph.cur
    xt7 = ph.sb("xt7", [128, 2, 32, 64], BF16)
    assert ph.cur <= M_OTA
    ph.cur = M_TMP
    E = ph.sb("E", [128, 64 * 128], BF16)
    L = ph.sb("L", [128, NCH * CL, 2, 16], F32)
    F1 = [ph.sb("F1", [128, 32, 16], F32, at=XT7_OFF)] * 2
    F2 = [ph.sb("F2", [128, 32, 16], F32, at=XT7_OFF + 2048)] * 2
    F3 = [ph.sb("F3", [128, 32, 16], F32, at=XT7_OFF + 4096)] * 2
    F4 = [ph.sb("F4", [128, 32, 16], F32, at=XT7_OFF + 6144)] * 2
    NPU, NPL = 4, 4
    psU = [ph.ps(f"psU{i}", [128, 512]) for i in range(NPU)]
    psL = [ph.ps(f"psL{i}", [128, 512]) for i in range(NPL)]
    P.add("sp", lambda e: e.dma_start(out=E[:], in_=c.eb), reads=[("eb", q) for q in range(4)],
          writes=["E"], dma=True)
    P.add("sp", lambda e: e.dma_start(out=xt7[:].rearrange("p r g q -> p (r g q)"), in_=c.xt7b),
          reads=["xt7b"], writes=["xt7"], dma=True)
    P.add("pool", lambda e: e.memset(K.Sbf[:, :, :, 0:1], 0.0), writes=["Sbf0"])
    uTv = K.uT[:].rearrange("p j (s n) -> p j s n", s=8)
    for g in range(32):
        j, glo = g // 8, g % 8
        bank = psU[g % NPU]
        for sg in range(8):
            idx = glo * 8 + sg
            P.add("pe", lambda e, bank=bank, idx=idx, j=j, sg=sg: e.matmul(
                bank[:, 0:256], E[:, idx * 128:(idx + 1) * 128], uTv[:, j, sg, :],
                start=(sg == 0), stop=(sg == 7)),
                reads=["E"], writes=[("psU", g % NPU)])
        if g % 2 == 0:
            P.add("act", lambda e, g=g, bank=bank: e.activation(K.Ubar[:, g, :], bank[:, 0:256], AF.Copy),
                  writes=[("psU", g % NPU), ("Ubar", g)])
        else:
            P.add("dve", lambda e, g=g, bank=bank: e.tensor_copy(K.Ubar[:, g, :], bank[:, 0:256]),
                  writes=[("psU", g % NPU), ("Ubar", g)])
    for gp in range(16):
        bank = psL[gp % NPL]
        for g2 in range(2):
            g = 2 * gp + g2
            for ri in range(2):
                P.add("pe", lambda e, bank=bank, g2=g2, g=g, ri=ri: e.matmul(
                    bank[g2 * 64:(g2 + 1) * 64, ri * 256:(ri + 1) * 256], xt7[:, ri, g, :], K.Ubar[:, g, :],
                    start=True, stop=True, skip_group_check=True),
                    reads=["xt7", ("Ubar", g)], writes=[("psL", gp % NPL)])
        dst = L[:, 0:256, :, gp].rearrange("p n r -> p r n")
        srcv = bank[:].rearrange("p (r n) -> p r n", r=2)
        if gp % 2 == 0:
            P.add("act", lambda e, dst=dst, srcv=srcv: e.activation(dst, srcv, AF.Copy),
                  writes=[("psL", gp % NPL), ("Lgp", gp)])
        else:
            P.add("dve", lambda e, dst=dst, srcv=srcv: e.tensor_copy(dst, srcv),
                  writes=[("psL", gp % NPL), ("Lgp", gp)])
    Lc = L[:].rearrange("p (c j) r g -> p c j r g", c=NCH)
    allL = [("Lgp", gp) for gp in range(16)]
    P.add("dve", lambda e: e.memset(L[:, 256:288, :, :], 0.0), reads=allL, writes=[("Lg", 1), "xt7"])
    P.add("dve", lambda e: e.tensor_copy(L[:, 256, :, :], K.a8[:]), writes=[("Lg", 1)])
    for j in range(1, CL):
        for stage in range(5):
            for g, (ca, cb) in enumerate(GRP):
                nchk = cb - ca
                p1, p2 = P1[g][j % 2], P2[g][j % 2]
                a1c = K.a8[:, 0:1, :].rearrange("p (c r) g -> p c r g", c=1).broadcast_to([128, nchk, 2, 16])
                a2c = K.a8b[:].rearrange("p (c r) g -> p c r g", c=1).broadcast_to([128, nchk, 2, 16])
                lg = ("Lg", g)
                if stage == 0:
                    P.add("dve", lambda e, p1=p1, j=j, ca=ca, cb=cb, a1c=a1c: e.tensor_tensor(
                        p1[:], Lc[:, ca:cb, j - 1, :, :], a1c, ALU.mult),
                        reads=[lg], writes=[("P1", g, j % 2)])
                elif stage == 1:
                    P.add("dve", lambda e, p2=p2, j=j, ca=ca, cb=cb, a2c=a2c: e.tensor_tensor(
                        p2[:], Lc[:, ca:cb, j - 1, :, :], a2c, ALU.mult),
                        reads=[lg], writes=[("P2", g, j % 2)])
                elif stage == 2:
                    P.add("dve", lambda e, p1=p1, j=j, ca=ca, cb=cb: e.tensor_tensor(
                        Lc[:, ca:cb, j, :, :], Lc[:, ca:cb, j, :, :], p1[:], ALU.add),
                        reads=[("P1", g, j % 2)], writes=[lg])
                elif stage == 3:
                    P.add("dve", lambda e, p2=p2, j=j, ca=ca, cb=cb: e.tensor_tensor(
                        Lc[:, ca:cb, j, 0, :], Lc[:, ca:cb, j, 0, :], p2[:, :, 1, :], ALU.add),
                        reads=[("P2", g, j % 2)], writes=[lg])
                else:
                    P.add("dve", lambda e, p2=p2, j=j, ca=ca, cb=cb: e.tensor_tensor(
                        Lc[:, ca:cb, j, 1, :], Lc[:, ca:cb, j, 1, :], p2[:, :, 0, :], ALU.add),
                        reads=[("P2", g, j % 2)], writes=[lg])
    pwr = Lc[:, 8, :, 0, :]
    pwi = Lc[:, 8, :, 1, :]
    LG = [("Lg", 0), ("Lg", 1)]
    for cch in range(1, 8):
        cr = Lc[:, cch - 1, CL - 1:CL, 0, :].broadcast_to([128, CL, 16])
        ci = Lc[:, cch - 1, CL - 1:CL, 1, :].broadcast_to([128, CL, 16])
        f = 0
        P.add("dve", lambda e, cr=cr, f=f: e.tensor_tensor(F1[f][:], pwr, cr, ALU.mult),
              reads=LG, writes=[("F1", f)])
        P.add("dve", lambda e, ci=ci, f=f: e.tensor_tensor(F2[f][:], pwi, ci, ALU.mult),
              reads=LG, writes=[("F2", f)])
        P.add("dve", lambda e, ci=ci, f=f: e.tensor_tensor(F3[f][:], pwr, ci, ALU.mult),
              reads=LG, writes=[("F3", f)])
        P.add("dve", lambda e, cr=cr, f=f: e.tensor_tensor(F4[f][:], pwi, cr, ALU.mult),
              reads=LG, writes=[("F4", f)])
        P.add("dve", lambda e, f=f: e.tensor_tensor(F1[f][:], F1[f][:], F2[f][:], ALU.subtract),
              reads=[("F2", f)], writes=[("F1", f)])
        P.add("dve", lambda e, f=f: e.tensor_tensor(F3[f][:], F3[f][:], F4[f][:], ALU.add),
              reads=[("F4", f)], writes=[("F3", f)])
        P.add("dve", lambda e, cch=cch, f=f: e.tensor_tensor(Lc[:, cch, :, 0, :], Lc[:, cch, :, 0, :], F1[f][:], ALU.add),
              reads=[("F1", f)], writes=LG)
        P.add("dve", lambda e, cch=cch, f=f: e.tensor_tensor(Lc[:, cch, :, 1, :], Lc[:, cch, :, 1, :], F3[f][:], ALU.add),
              reads=[("F3", f)], writes=LG)
    for ri in range(2):
        dst = K.Sbf[:, ri, :, 1:257]
        srcv = L[:, 0:256, ri, :].rearrange("p n g -> p g n")
        P.add("act" if ri == 0 else "dve",
              (lambda e, dst=dst, srcv=srcv: e.activation(dst, srcv, AF.Copy)) if ri == 0 else
              (lambda e, dst=dst, srcv=srcv: e.tensor_copy(dst, srcv)),
              reads=[("Lg", 0), ("Lg", 1), "Sbf0"], writes=[("Sbf", ri)])


def emit_s5b(ph, c, K):
    P = ph.P
    K.gT = ph.sb("gT", [128, 4, T], BF16, at=M_GT)
    K.Ubar = ph.sb("Ubar", [128, 32, 256], BF16, at=M_UBAR)
    K.Sbf = ph.sb("Sbf", [128, 2, 16, 257], BF16, at=M_SBF)
    ph.cur = M_TMP
    E = ph.sb("E", [128, 64 * 128], BF16)
    ttl = ph.sb("ttl", [128, 32, 128], BF16)
    yp = [ph.sb(f"yp{a}", [128, 2, 16, 128], BF16) for a in range(2)]
    gst = [ph.sb(f"gst{i}", [128, 8, 256], BF16) for i in range(2)]
    psY = [ph.ps(f"psY{i}", [128, 512]) for i in range(2)]
    psG = [ph.ps(f"psG{i}", [128, 512]) for i in range(2)]
    P.add("sp", lambda e: e.dma_start(out=ttl[:].rearrange("p g q -> p (g q)"), in_=c.ttb),
          writes=["ttl"], dma=True)
    for a in range(2):
        P.add("sp", lambda e, a=a: e.dma_start(out=yp[a][:].rearrange("p r g q -> p (r g q)"), in_=c.ypb[a]),
              writes=["yp"], dma=True)
    P.add("sp", lambda e: e.dma_start(out=E[:], in_=c.eb), writes=["E"], dma=True)
    gTv = K.gT[:].rearrange("p j (n s) -> p j s n", s=8)
    for j in range(4):
        gs = gst[j % 2]
        for glo in range(8):
            g = 8 * j + glo
            gp, g2 = g // 2, g % 2
            rows = slice(g2 * 64, (g2 + 1) * 64)
            bank = psY[g % 2]
            P.add("pe", lambda e, bank=bank, g=g: e.matmul(
                bank[:, 0:256], ttl[:, g, :], K.Ubar[:, g, :], start=True, stop=False),
                reads=["ttl"], writes=[("psY", g % 2)])
            for ri in range(2):
                P.add("pe", lambda e, bank=bank, g2=g2, ri=ri, gp=gp: e.matmul(
                    bank[:, 0:256], yp[g2][:, ri, gp, :], K.Sbf[:, ri, gp, 0:256],
                    start=False, stop=(ri == 1)),
                    reads=["yp"], writes=[("psY", g % 2)])
            P.add("act", lambda e, gs=gs, glo=glo, bank=bank: e.activation(
                gs[:, glo, :], bank[:, 0:256], AF.Gelu_apprx_tanh),
                writes=[("psY", g % 2), ("gst", j % 2, glo)])
        for tau in range(8):
            bank = psG[tau % 2]
            for glo in range(8):
                idx = tau * 8 + glo
                P.add("pe", lambda e, bank=bank, idx=idx, gs=gs, glo=glo: e.matmul(
                    bank[:, 0:256], E[:, idx * 128:(idx + 1) * 128], gs[:, glo, :],
                    start=(glo == 0), stop=(glo == 7)),
                    reads=["E", ("gst", j % 2, glo)], writes=[("psG", tau % 2)])
            dst = gTv[:, j, tau, :]
            if tau % 2 == 0:
                P.add("act", lambda e, dst=dst, bank=bank: e.activation(dst, bank[:, 0:256], AF.Copy),
                      writes=[("psG", tau % 2), ("gT", j)])
            else:
                P.add("dve", lambda e, dst=dst, bank=bank: e.tensor_copy(dst, bank[:, 0:256]),
                      writes=[("psG", tau % 2), ("gT", j)])


def emit_mixout(ph, c, K):
    P = ph.P
    K.gT = ph.sb("gT", [128, 4, T], BF16, at=M_GT)
    K.oTa = ph.sb("oTa", [128, 4, T], BF16, at=M_OTA)
    ph.cur = COMMON_END + 16384
    wgl = ph.sb("wgl", [128, 4, 4 * 128], BF16)
    NW = 8
    K.sq = ph.sb("sq", [128, 8, TT], BF16)
    osT = ph.sb("osT", [128, 4, TT], F32)
    onT = ph.sb("onT", [128, 4, TT], BF16)
    K.rt = ph.sb("rt", [128, TT], F32)
    K.rstd = ph.sb("rstd", [128, TT], F32)
    K.rstd2 = ph.sb("rstd2", [128, TT], F32)
    sig = [ph.sb(f"sig{i}", [128, TT], F32) for i in range(2)]
    sqA = ph.sb("sqA", [128, 4, TT], BF16)
    rtA = ph.sb("rtA", [128, TT], F32)
    assert ph.cur <= M_OTA
    ph.cur = M_TMP
    K.yT = ph.sb("yT", [128, 8, TT], F32)
    tmp = [ph.sb(f"tmp{i}", [128, TT], F32) for i in range(8)]
    wo = [ph.sb(f"wo{i}", [128, 8 * 128], BF16) for i in range(NW)]
    psA = [ph.ps(f"psA{i}", [128, 512]) for i in range(2)]
    psC = [ph.ps(f"psC{i}", [128, 512]) for i in range(4)]
    K.ps_stat = ph.ps("ps_stat", [128, 512])
    for m in range(4):
        P.add("sp", lambda e, m=m: e.dma_start(out=wgl[:, m, :], in_=c.wglub[m]), writes=[("wgl", m)], dma=True)
    st = {"wl": 0}

    def glu_a(tt):
        ts = slice(tt * TT, (tt + 1) * TT)
        for m in range(4):
            pa = psA[m % 2]
            for cc in range(4):
                P.add("pe", lambda e, pa=pa, m=m, cc=cc: e.matmul(
                    pa[:], wgl[:, m, cc * 128:(cc + 1) * 128], K.gT[:, cc, ts], start=(cc == 0), stop=(cc == 3)),
                    reads=[("wgl", m)], writes=[("psA", m % 2)])
            sg = sig[m % 2]
            P.add("act", lambda e, sg=sg, pa=pa, m=m: e.activation(
                sg[:], pa[:], AF.Sigmoid, bias=K.vecs[:, m:m + 1]),
                writes=[("psA", m % 2), ("sig", m % 2)])
            P.add("dve", lambda e, sg=sg, m=m: e.tensor_tensor(osT[:, m, :], K.gT[:, m, ts], sg[:], ALU.mult),
                  reads=[("sig", m % 2)], writes=[("osT", m)])
            P.add("act", lambda e, m=m: e.activation(sqA[:, m, :], osT[:, m, :], AF.Square),
                  reads=[("osT", m)], writes=[("sqA", m)])

    def glu_b(tt):
        for cc in range(4):
            P.add("pe", lambda e, cc=cc: e.matmul(K.ps_stat[:], K.ones[:], sqA[:, cc, :], start=(cc == 0), stop=(cc == 3)),
                  reads=[("sqA", cc), "ones"], writes=["ps_stat"])
        P.add("act", lambda e: e.activation(rtA[:], K.ps_stat[:], AF.Sqrt, bias=K.epsb[:], scale=1.0 / 512),
              reads=["epsb"], writes=["ps_stat", "rtA"])
        P.add("dve", lambda e: e.reciprocal(K.rstd[:], rtA[:]), reads=["rtA"], writes=["rstd"])
        for m in range(4):
            P.add("dve", lambda e, m=m: e.scalar_tensor_tensor(
                out=onT[:, m, :], in0=osT[:, m, :], scalar=K.vecs[:, 4 + m:5 + m], in1=K.rstd[:],
                op0=ALU.mult, op1=ALU.mult),
                reads=[("osT", m), "rstd"], writes=[("onT", m)])

    def outproj(tt):
        ts = slice(tt * TT, (tt + 1) * TT)
        for m in range(8):
            slot = st["wl"] % NW
            st["wl"] += 1
            wt = wo[slot]
            P.add("sp", lambda e, wt=wt, m=m: e.dma_start(out=wt[:], in_=c.woutb[m]),
                  writes=[("wo", slot)], dma=True)
            py = psC[m % 4]
            for cc in range(8):
                rhs = K.oTa[:, cc, ts] if cc < 4 else onT[:, cc - 4, :]
                P.add("pe", lambda e, wt=wt, cc=cc, py=py, rhs=rhs: e.matmul(
                    py[:], wt[:, cc * 128:(cc + 1) * 128], rhs, start=(cc == 0), stop=(cc == 7)),
                    reads=[("wo", slot)] + ([("onT", cc - 4)] if cc >= 4 else []), writes=[("psC", m % 4)])
            P.add("act", lambda e, m=m, py=py: e.activation(K.yT[:, m, :], py[:], AF.Copy),
                  reads=[("psC", m % 4)], writes=[("yT", m)])
            P.add("act", lambda e, m=m: e.activation(K.sq[:, m, :], K.yT[:, m, :], AF.Square),
                  reads=[("yT", m)], writes=[("sq", m)])

    def post_stats(tt):
        emit_rstd(P, K, K.ps_stat, "sq", D, K.rstd2, "rstd2")

    def post_apply(tt):
        ts = slice(tt * TT, (tt + 1) * TT)
        for m in range(8):
            tm_ = tmp[m]
            P.add("dve", lambda e, m=m, tm_=tm_: e.scalar_tensor_tensor(
                out=tm_[:], in0=K.yT[:, m, :], scalar=K.gains[:, G_MIX_POST, m:m + 1],
                in1=K.rstd2[:], op0=ALU.mult, op1=ALU.mult),
                reads=[("yT", m), "rstd2"], writes=[("tmp", m)])
            P.add("pool", lambda e, m=m, tm_=tm_: e.tensor_tensor(
                K.XT[:, m, ts], K.XT[:, m, ts], tm_[:], ALU.add),
                reads=[("tmp", m)], writes=[("XT", tt, m)])

    glu_a(0)
    glu_b(0)
    if NTT > 1:
        glu_a(1)
    for tt in range(NTT):
        outproj(tt)
        if tt + 1 < NTT:
            glu_b(tt + 1)
        if tt + 2 < NTT:
            glu_a(tt + 2)
        post_stats(tt)
        post_apply(tt)


def alloc_ffn_phase(ph, c, K):
    alloc_common(ph, c, K)
    alloc_ffn(ph, K)


def build_nc(stage="full"):
    nc = bass.Bass("TRN2", target_bir_lowering=False)
    debug = stage not in ("full",)
    c = declare_dram(nc, debug=debug)
    first = True
    ges = ExitStack()
    G = Globals(nc, ges)
    full_like = stage == "full"
    if full_like:
        with Phase(nc, G, "start") as ph:
            K = Ctx()
            alloc_common(ph, c, K)
            K.xio = [ph.sb(f"xio{i}", [128, D], F32) for i in range(2)]
            K.psA = [ph.ps(f"psA{i}", [128, 512]) for i in range(2)]
            emit_consts(ph, c, K)
            gen = load_x_blocks(ph, c, K, 0, act_only=True)
            next(gen, None)
            emit_s5_setup(ph, c, K, hook=lambda: next(gen, None))
            for _ in gen:
                pass
            ph.finish()
    for seq in range(NSEQ):
        with Phase(nc, G, f"f1s{seq}") as ph:
            K = Ctx()
            alloc_ffn_phase(ph, c, K)
            if first and not full_like:
                emit_consts(ph, c, K)
                if stage in ("att", "mix"):
                    emit_cast_mixer_weights(ph, c)
                elif stage != "io":
                    emit_cast_weights(ph, c, 0)
                    emit_cast_mixer_weights(ph, c)
                    emit_cast_weights(ph, c, 1)
            xg = None
            if not (full_like and seq == 0):
                if full_like:
                    xg = load_x_blocks(ph, c, K, seq)
                    for _ in range(4):
                        next(xg, None)
                else:
                    emit_load_x(ph, c, K, seq)
            if stage not in ("io", "iocast", "att", "mix"):
                emit_ffn(ph, c, K, 0, G_FF1_PRE, G_FF1_POST, direct_cast=(full_like and seq == 0), x_gen=xg)
            if stage in ("ffn1", "io", "iocast"):
                emit_store_x(ph, c, K, seq)
            ph.finish()
        first = False
        if stage in ("ffn1", "io", "iocast"):
            continue
        KMIX = int(os.environ.get("KMIX", "9"))
        if seq == 0 and stage != "att" and KMIX >= 1 and not full_like:
            with Phase(nc, G, "s5set") as ph:
                K = Ctx()
                alloc_common(ph, c, K)
                emit_s5_setup(ph, c, K)
                ph.finish()
        with Phase(nc, G, f"pjs{seq}") as ph:
            K = Ctx()
            alloc_common(ph, c, K)
            emit_proj(ph, c, K)
            ph.finish()
        with Phase(nc, G, f"ats{seq}") as ph:
            K = Ctx()
            alloc_common(ph, c, K)
            emit_attn(ph, c, K)
            if stage == "att":
                ph.P.add("sp", lambda e, K=K, seq=seq: e.dma_start(
                    out=c.dbg[seq], in_=K.oTa[:].rearrange("p h t -> p (h t)")),
                    reads=[("oTa", tb) for tb in range(16)], writes=["dbg"], dma=True)
            ph.finish()
        if stage == "att":
            continue
        if KMIX >= 2:
          with Phase(nc, G, f"sas{seq}") as ph:
            K = Ctx()
            alloc_common(ph, c, K)
            if full_like and seq == 0:
                emit_cast_weights(ph, c, 1, part="gu")
            emit_s5a(ph, c, K)
            if os.environ.get("KDUMP") == "Ubar":
                ph.P.add("sp", lambda e, K=K, seq=seq: e.dma_start(
                    out=c.dbg[seq], in_=K.Ubar[:].rearrange("p g n -> p (g n)")),
                    reads=[("Ubar", g) for g in range(32)], writes=["dbg"], dma=True)
            if os.environ.get("KDUMP") == "Sbf":
                ph.P.add("sp", lambda e, K=K, seq=seq: e.dma_start(
                    out=c.dbg[seq].rearrange("p (a n) -> p a n", n=256),
                    in_=K.Sbf[:, :, :, 1:257].rearrange("p r g n -> p (r g) n")),
                    reads=[("Sbf", 0), ("Sbf", 1)], writes=["dbg"], dma=True)
            ph.finish()
        if KMIX >= 3:
          with Phase(nc, G, f"sbs{seq}") as ph:
            K = Ctx()
            alloc_common(ph, c, K)
            if full_like and seq == 0:
                emit_cast_weights(ph, c, 1, part="d")
            emit_s5b(ph, c, K)
            if os.environ.get("KDUMP") == "gT":
                ph.P.add("sp", lambda e, K=K, seq=seq: e.dma_start(
                    out=c.dbg[seq], in_=K.gT[:].rearrange("p j t -> p (j t)")),
                    reads=[("gT", j) for j in range(4)], writes=["dbg"], dma=True)
            ph.finish()
        if KMIX >= 4:
          with Phase(nc, G, f"mos{seq}") as ph:
            K = Ctx()
            alloc_common(ph, c, K)
            emit_mixout(ph, c, K)
            ph.finish()
        with Phase(nc, G, f"f2s{seq}") as ph:
            K = Ctx()
            alloc_ffn_phase(ph, c, K)
            if stage != "mix":
                emit_ffn(ph, c, K, 1, G_FF2_PRE, G_FF2_POST)
            emit_store_x(ph, c, K, seq)
            ph.finish()
    ges.close()
    return nc


def host_layout(inp):
    f = lambda a: np.ascontiguousarray(np.asarray(a, dtype=np.float32))
    com = {}
    com["ident"] = np.eye(128, dtype=np.float32)
    kk = np.arange(128)[:, None]
    qq = np.arange(128)[None, :]
    com["maskneg"] = np.where(kk <= qq, 0.0, -30000.0).astype(np.float32)
    gl = [inp["ff1_pre_g"], inp["ff1_post_g"], inp["mix_pre_g"], inp["mix_post_g"],
          inp["ff2_pre_g"], inp["ff2_post_g"]]
    gains = np.zeros((128, 8, 8), np.float32)
    for i, g in enumerate(gl):
        gains[:, i, :] = f(g).reshape(8, 128).T
    com["gains"] = gains
    lamv = np.stack([f(inp[k])[0] for k in ("lambda_q1", "lambda_k1", "lambda_q2", "lambda_k2")], 0)
    com["lamv"] = np.ascontiguousarray(np.broadcast_to(lamv[None], (128, 4, 64)))
    com["gsub"] = np.ascontiguousarray(np.broadcast_to(f(inp["attn_subln_g"])[0][None], (128, 128)))
    vecs = np.zeros((128, 8), np.float32)
    vecs[:, 0:4] = f(inp["ssm_b_glu"])[0].reshape(4, 128).T
    vecs[:, 4:8] = f(inp["ssm_norm_g"])[0].reshape(4, 128).T
    com["vecs"] = vecs
    for i, pre in enumerate(("ff1", "ff2")):
        wg = f(inp[pre + "_w_gate"])[0]
        wu = f(inp[pre + "_w_up"])[0]
        wd = f(inp[pre + "_w_down"])[0]
        g4 = wg.reshape(8, 128, NF, 128).transpose(2, 1, 0, 3)
        u4 = wu.reshape(8, 128, NF, 128).transpose(2, 1, 0, 3)
        com[f"wgu{i}"] = np.ascontiguousarray(np.stack([g4, u4], axis=2)).reshape(NF, 128, 2 * 8 * 128)
        d4 = wd.reshape(NF, 128, 8, 128).transpose(2, 1, 0, 3)
        com[f"wd{i}"] = np.ascontiguousarray(d4).reshape(8, 128, NF * 128)
    win = f(inp["w_in"])[0]
    w4 = win.reshape(8, 128, 16, 128).transpose(2, 1, 0, 3)
    sel = list(range(8)) + list(range(12, 16))
    com["winf"] = np.ascontiguousarray(w4[sel]).reshape(12, 128, 8 * 128)
    wv = win[:, 1024:1536].reshape(8, 128, 512).transpose(1, 0, 2)
    com["winv"] = np.ascontiguousarray(wv).reshape(128, 8 * 512)
    wo = f(inp["w_out"])[0]
    o4 = wo.reshape(8, 128, 8, 128).transpose(2, 1, 0, 3)
    com["wout"] = np.ascontiguousarray(o4).reshape(8, 128, 8 * 128)
    wgl = f(inp["ssm_w_glu"])[0]
    l4 = wgl.reshape(4, 128, 4, 128).transpose(2, 1, 0, 3)
    com["wglu"] = np.ascontiguousarray(l4).reshape(4, 128, 4 * 128)
    a_re, a_im = f(inp["ssm_a_re"])[0], f(inp["ssm_a_im"])[0]
    ldt = f(inp["ssm_log_dt"])[0]
    b_re, b_im = f(inp["ssm_b_re"])[0], f(inp["ssm_b_im"])[0]
    c_re, c_im = f(inp["ssm_c_re"])[0], f(inp["ssm_c_im"])[0]
    dsk = f(inp["ssm_d"])[0]
    s5p = np.zeros((128, S5P_COLS), np.float32)
    lay_a = lambda a: a.reshape(16, 2, 64).transpose(1, 2, 0).reshape(128, 16)
    s5p[:, 0:16] = lay_a(a_re)
    s5p[:, 16:32] = lay_a(a_im)
    s5p[:, 32:48] = np.broadcast_to(ldt.reshape(16, 2).T[:, None, :], (2, 64, 16)).reshape(128, 16)
    lay_b = lambda b: b.reshape(16, 2, 64, 16).transpose(1, 2, 0, 3).reshape(128, 256)
    lay_c = lambda cc: cc.reshape(16, 2, 16, 64).transpose(1, 3, 0, 2).reshape(128, 256)
    s5p[:, 48:304] = lay_b(b_re)
    s5p[:, 304:560] = lay_b(b_im)
    s5p[:, 560:816] = lay_c(c_re)
    s5p[:, 816:1072] = lay_c(c_im)
    kk = np.stack([7.0 - np.arange(8), np.arange(8) - 7.0, np.arange(8) + 1.0]).astype(np.float32)
    s5p[:, 1072:1096] = kk.reshape(1, 24)
    s5p[:, 1096:1128] = np.broadcast_to(dsk.reshape(32, 16).T[None], (8, 16, 32)).reshape(128, 32)
    com["s5p"] = s5p
    sg = np.arange(128) // 16
    com["cmask"] = (sg[None, :] >= sg[:, None]).astype(np.float32)
    r = np.arange(128)
    esel = np.zeros((128, 64, 128), np.float32)
    for glo in range(8):
        for sgm in range(8):
            m = ((r[:, None] // 16 == glo) & (r[:, None] % 16 == r[None, :] % 16) & (r[None, :] // 16 == sgm))
            esel[:, glo * 8 + sgm, :] = m
    com["esel"] = esel.reshape(128, 64 * 128)
    return com


_NC_CACHE = {}


def kernel(**inputs):
    x = np.ascontiguousarray(np.asarray(inputs["x"], dtype=np.float32))
    com = host_layout(inputs)
    if "full" not in _NC_CACHE:
        _NC_CACHE["full"] = build_nc("full")
    nc = _NC_CACHE["full"]
    in_maps = []
    for i in range(NCORES):
        m = dict(com)
        m["x"] = x[i * NSEQ:(i + 1) * NSEQ]
        in_maps.append(m)
    res = run_bass_kernel_spmd(nc, in_maps, core_ids=list(range(NCORES)))
    out = np.concatenate([np.asarray(r["out"]) for r in res.results], axis=0)
    return out.astype(np.float32)
```

```python
import math
import os
from contextlib import ExitStack

import numpy as np
import concourse.bass as bass
import concourse.mybir as mybir
from concourse.bass_utils import run_bass_kernel_spmd

F32 = mybir.dt.float32
BF16 = mybir.dt.bfloat16
AF = mybir.ActivationFunctionType
ALU = mybir.AluOpType

D = 1024
DFF = 2816
NF = DFF // 128
T = 2048
NSEQ = 2
TT = 512
NTT = T // TT
EPS = 1e-6
NCORES = 8

ENGS = ("pe", "act", "dve", "pool", "sp")


class Op:
    __slots__ = ("eng", "fn", "deps", "is_dma", "sig", "sem", "val", "idx", "pre")

    def __init__(self, eng, fn, is_dma):
        self.eng = eng
        self.fn = fn
        self.is_dma = is_dma
        self.deps = set()
        self.sig = False
        self.sem = None
        self.val = 0
        self.pre = None


class Prog:
    def __init__(self, G):
        self.G = G
        self.ops = []
        self.last_w = {}
        self.readers = {}

    def add(self, eng, fn, reads=(), writes=(), dma=False):
        op = Op(eng, fn, dma)
        op.idx = len(self.ops)
        for r in reads:
            w = self.last_w.get(r)
            if w is not None:
                op.deps.add(w)
        for r in writes:
            w = self.last_w.get(r)
            if w is not None:
                op.deps.add(w)
            for q in self.readers.get(r, ()):
                op.deps.add(q)
        for r in reads:
            self.readers.setdefault(r, []).append(op.idx)
        for r in writes:
            self.last_w[r] = op.idx
            self.readers[r] = []
        op.deps.discard(op.idx)
        self.ops.append(op)
        return op.idx

    def emit(self, block):
        ops = self.ops
        G = self.G
        sems, dmasems = G.sems, G.dmasems

        def skip(dop, op):
            return (dop.eng == "pe" and op.eng == "pe" and not dop.is_dma and not op.is_dma)

        for op in ops:
            best = {}
            keep = set()
            for d in op.deps:
                dop = ops[d]
                if skip(dop, op):
                    continue
                if dop.is_dma:
                    keep.add(d)
                elif best.get(dop.eng, -1) < d:
                    best[dop.eng] = d
            keep.update(best.values())
            op.deps = keep
            for d in keep:
                ops[d].sig = True
        last_compute = {}
        for op in ops:
            if not op.is_dma and op.eng in ("pe", "act", "dve", "pool"):
                last_compute[op.eng] = op
        for op in last_compute.values():
            op.sig = True
        cnt, dcnt = G.cnt, G.dcnt
        for op in ops:
            if op.is_dma:
                i = dcnt[op.eng]
                dcnt[op.eng] += 1
                ring = dmasems[op.eng]
                op.sem = ring[i % len(ring)]
                op.val = 16 * (i // len(ring) + 1)
                op.pre = (op.sem, op.val - 16)
            elif op.sig:
                cnt[op.eng] += 1
                op.sem = sems[op.eng]
                op.val = cnt[op.eng]
        per_eng = {e: [o for o in ops if o.eng == e] for e in ENGS}
        if os.environ.get("KVERB"):
            print("sem counts", cnt, "dma counts", dcnt, "nops", len(ops))

        def run(engname, eng):
            waited = G.waited[engname]
            for op in per_eng[engname]:
                need = {}
                for d in op.deps:
                    dop = ops[d]
                    if dop.sem is None:
                        continue
                    k = id(dop.sem)
                    if need.get(k, (None, 0))[1] < dop.val:
                        need[k] = (dop.sem, dop.val)
                if op.pre is not None and op.pre[1] > 0:
                    k = id(op.pre[0])
                    if need.get(k, (None, 0))[1] < op.pre[1]:
                        need[k] = op.pre
                for k, (s, v) in need.items():
                    if waited.get(k, 0) < v:
                        eng.wait_ge(s, v)
                        waited[k] = v
                inst = op.fn(eng)
                if op.is_dma:
                    inst.then_inc(op.sem, 16)
                elif op.sig:
                    inst.then_inc(op.sem, 1)
            last = {}
            for op in per_eng[engname]:
                if op.is_dma:
                    last[id(op.sem)] = (op.sem, op.val)
            for k, (s, v) in last.items():
                if waited.get(k, 0) < v:
                    eng.wait_ge(s, v)
                    waited[k] = v
            lc = last_compute.get(engname)
            if lc is not None and waited.get(id(lc.sem), 0) < lc.val:
                eng.wait_ge(lc.sem, lc.val)
                waited[id(lc.sem)] = lc.val

        @block.tensor
        def _(e):
            run("pe", e)

        @block.scalar
        def _(e):
            run("act", e)

        @block.vector
        def _(e):
            run("dve", e)

        @block.gpsimd
        def _(e):
            run("pool", e)

        @block.sync
        def _(e):
            run("sp", e)


class Globals:
    def __init__(self, nc, es):
        self.sems = {e: es.enter_context(nc.semaphore(f"s_{e}")) for e in ENGS}
        ring = {"sp": 24, "pool": 8, "act": 4, "pe": 1, "dve": 1}
        self.dmasems = {e: [es.enter_context(nc.semaphore(f"d_{e}{i}")) for i in range(ring[e])]
                        for e in ENGS}
        self.cnt = {e: 0 for e in ENGS}
        self.dcnt = {e: 0 for e in ENGS}
        self.waited = {e: {} for e in ENGS}
        self.uid = 0
        self.base = (nc.sbuf_base + 31) // 32 * 32
        self.arena = es.enter_context(nc.sbuf_tensor("arena", [128, SB_LIMIT // 4], F32))


SB_LIMIT = 207 * 1024
DT_SIZE = {F32: 4, BF16: 2}


class Phase:
    def __init__(self, nc, G, name):
        self.nc = nc
        self.G = G
        self.name = name
        self.es = ExitStack()
        self.P = Prog(G)
        self.cur = 0

    def __enter__(self):
        self.es.__enter__()
        return self

    def sb(self, name, shape, dt, at=None):
        n = 1
        for d in shape[1:]:
            n *= d
        size = (n * DT_SIZE[dt] + 31) // 32 * 32
        if at is None:
            at = self.cur
        self.cur = max(self.cur, at + size)
        assert at + size <= SB_LIMIT, (self.name, name, at + size)
        self.G.uid += 1
        return self.nc.alloc_sbuf_tensor_at(f"{self.name}_{name}_{self.G.uid}", shape, dt,
                                            offset=self.G.base + at)

    def ps(self, name, shape, dt=F32):
        return self.es.enter_context(self.nc.psum_tensor(f"{self.name}_{name}", shape, dt))

    def finish(self):
        with self.nc.Block(no_gpsimd_drain=True) as block:
            self.P.emit(block)

    def __exit__(self, *a):
        return self.es.__exit__(*a)


class Ctx:
    pass


def declare_dram(nc, debug=False):
    c = Ctx()
    di = lambda n, s, d=F32: nc.dram_tensor(n, s, d, kind="ExternalInput").ap()
    sc = lambda n, s, d=BF16: nc.dram_tensor(n, s, d).ap()
    c.x = di("x", [NSEQ, T, D])
    c.out = nc.dram_tensor("out", [NSEQ, T, D], F32, kind="ExternalOutput").ap()
    if debug:
        c.dbg = nc.dram_tensor("dbg", [NSEQ, 128, 4 * T], BF16, kind="ExternalOutput").ap()
    c.ident = di("ident", [128, 128])
    c.maskneg = di("maskneg", [128, 128])
    c.gains = di("gains", [128, 8, 8])
    c.lamv = di("lamv", [128, 4, 64])
    c.gsub = di("gsub", [128, 128])
    c.vecs = di("vecs", [128, 8])
    c.wgu = [di(f"wgu{i}", [NF, 128, 2 * 8 * 128]) for i in range(2)]
    c.wd = [di(f"wd{i}", [8, 128, NF * 128]) for i in range(2)]
    c.wgub = [sc(f"wgub{i}", [NF, 128, 2 * 8 * 128]) for i in range(2)]
    c.wdb = [sc(f"wdb{i}", [8, 128, NF * 128]) for i in range(2)]
    c.winf = di("winf", [12, 128, 8 * 128])
    c.winv = di("winv", [128, 8 * 512])
    c.wout = di("wout", [8, 128, 8 * 128])
    c.wglu = di("wglu", [4, 128, 4 * 128])
    c.winfb = sc("winfb", [12, 128, 8 * 128])
    c.winvb = sc("winvb", [128, 8 * 512])
    c.woutb = sc("woutb", [8, 128, 8 * 128])
    c.wglub = sc("wglub", [4, 128, 4 * 128])
    c.s5p = di("s5p", [128, S5P_COLS])
    c.cmask = di("cmask", [128, 128])
    c.esel = di("esel", [128, 64 * 128])
    c.eb = sc("eb", [128, 64 * 128])
    c.xt7b = sc("xt7b", [128, 2 * 32 * 64])
    c.ttb = sc("ttb", [128, 32 * 128])
    c.ypb = sc("ypb", [2, 128, 2 * 16 * 128])
    return c


S5P_COLS = 1128
G_FF1_PRE, G_FF1_POST, G_MIX_PRE, G_MIX_POST, G_FF2_PRE, G_FF2_POST = range(6)


def emit_cast_weights(ph, c, which, part="all"):
    P = ph.P
    for f in range(NF if part in ("all", "gu") else 0):
        P.add("pool", lambda e, f=f: e.dma_start(out=c.wgub[which][f], in_=c.wgu[which][f]),
              writes=[("wgub", which, f)], dma=True)
    for m in range(8 if part in ("all", "d") else 0):
        src = c.wd[which][m].rearrange("p (a b) -> p a b", a=2)
        dst = c.wdb[which][m].rearrange("p (a b) -> p a b", a=2)
        P.add("pool", lambda e, src=src, dst=dst: e.dma_start(out=dst, in_=src),
              writes=[("wdb", which, m)], dma=True)


def emit_cast_mixer_weights(ph, c, part="all"):
    P = ph.P
    if part in ("all", "in"):
        for j in range(12):
            P.add("pool", lambda e, j=j: e.dma_start(out=c.winfb[j], in_=c.winf[j]),
                  writes=[("winfb", j)], dma=True)
        src = c.winv.rearrange("p (a b) -> p a b", a=2)
        dst = c.winvb.rearrange("p (a b) -> p a b", a=2)
        P.add("pool", lambda e: e.dma_start(out=dst, in_=src), writes=["winvb"], dma=True)
    if part == "in":
        return
    for m in range(8):
        P.add("pool", lambda e, m=m: e.dma_start(out=c.woutb[m], in_=c.wout[m]),
              writes=[("woutb", m)], dma=True)
    for m in range(4):
        P.add("pool", lambda e, m=m: e.dma_start(out=c.wglub[m], in_=c.wglu[m]),
              writes=[("wglub", m)], dma=True)


def load_x_blocks(ph, c, K, seq, act_only=False):
    P = ph.P
    for tb in range(T // 128):
        slot = tb % 2
        xin = K.xio[slot]
        P.add("sp", lambda e, tb=tb, xin=xin: e.dma_start(out=xin[:], in_=c.x[seq, tb * 128:(tb + 1) * 128, :]),
              writes=[("xio", slot)], dma=True)
        for half in range(2):
            bank = K.psA[(2 * tb + half) % 2]
            bname = ("psA", (2 * tb + half) % 2)
            for j in range(4):
                cc = half * 4 + j
                P.add("pe", lambda e, bank=bank, j=j, cc=cc, xin=xin:
                      e.transpose(bank[:, j * 128:(j + 1) * 128], xin[:, cc * 128:(cc + 1) * 128], K.ident[:]),
                      reads=[("xio", slot), "ident"], writes=[bname])
            eng = "act" if (half == 0 or act_only) else "dve"
            dst = K.XT[:, half * 4:(half + 1) * 4, tb * 128:(tb + 1) * 128]
            srcv = bank[:].rearrange("p (j t) -> p j t", j=4)
            wr = [("XT", tb // 4, q) for q in range(half * 4, half * 4 + 4)]
            if eng == "act":
                P.add("act", lambda e, dst=dst, srcv=srcv: e.activation(dst, srcv, AF.Copy),
                      reads=[bname], writes=wr)
            else:
                P.add("dve", lambda e, dst=dst, srcv=srcv: e.tensor_copy(dst, srcv),
                      reads=[bname], writes=wr)
        yield tb


def emit_load_x(ph, c, K, seq):
    for _ in load_x_blocks(ph, c, K, seq):
        pass


def emit_store_x(ph, c, K, seq):
    P = ph.P
    for tb in range(T // 128):
        slot = tb % 2
        xo = K.xio[slot]
        for half in range(2):
            bank = K.psA[(2 * tb + half) % 2]
            bname = ("psA", (2 * tb + half) % 2)
            for j in range(4):
                cc = half * 4 + j
                P.add("pe", lambda e, bank=bank, j=j, cc=cc, tb=tb:
                      e.transpose(bank[:, j * 128:(j + 1) * 128], K.XT[:, cc, tb * 128:(tb + 1) * 128], K.ident[:]),
                      reads=[("XT", tb // 4, cc), "ident"], writes=[bname])
            dst = xo[:, half * 512:(half + 1) * 512]
            if half == 0:
                P.add("act", lambda e, dst=dst, bank=bank: e.activation(dst, bank[:], AF.Copy),
                      reads=[bname], writes=[("xio", slot)])
            else:
                P.add("dve", lambda e, dst=dst, bank=bank: e.tensor_copy(dst, bank[:]),
                      reads=[bname], writes=[("xio", slot)])
        P.add("sp", lambda e, tb=tb, xo=xo: e.dma_start(out=c.out[seq, tb * 128:(tb + 1) * 128, :], in_=xo[:]),
              reads=[("xio", slot)], writes=[("out", seq, tb)], dma=True)


def emit_rstd(P, K, ps_stat, sq_res, dim, out_rstd, out_name, nch=8):
    for cc in range(nch):
        P.add("pe", lambda e, cc=cc: e.matmul(ps_stat[:], K.ones[:], K.sq[:, cc, :], start=(cc == 0), stop=(cc == nch - 1)),
              reads=[(sq_res, cc), "ones"], writes=["ps_stat"])
    P.add("act", lambda e: e.activation(K.rt[:], ps_stat[:], AF.Sqrt, bias=K.epsb[:], scale=1.0 / dim),
          reads=["ps_stat", "epsb"], writes=["rt"])
    P.add("dve", lambda e: e.reciprocal(out_rstd[:], K.rt[:]), reads=["rt"], writes=[out_name])


import os
DBG = int(os.environ.get("KDBG", "9"))


def emit_ffn(ph, c, K, which, gpre, gpost, direct_cast=False, x_gen=None):
    P = ph.P
    st = {"wl": 0, "dl": 0}

    def prenorm(tt):
        ts = slice(tt * TT, (tt + 1) * TT)
        xn = K.xnT2[tt % 2]
        for cc in range(8):
            P.add("act", lambda e, cc=cc: e.activation(K.sq[:, cc, :], K.XT[:, cc, ts], AF.Square),
                  reads=[("XT", tt, cc)], writes=[("sq", cc)])
        emit_rstd(P, K, K.ps_stat, "sq", D, K.rstd, "rstd")
        for cc in range(8):
            P.add("dve", lambda e, cc=cc: e.scalar_tensor_tensor(
                out=xn[:, cc, :], in0=K.XT[:, cc, ts], scalar=K.gains[:, gpre, cc:cc + 1],
                in1=K.rstd[:], op0=ALU.mult, op1=ALU.mult),
                reads=[("XT", tt, cc), "rstd", "gains"], writes=[("xnT", tt % 2, cc)])

    def gateup(tt, f):
        xn = K.xnT2[tt % 2]
        slot = st["wl"] % K.NWGU
        st["wl"] += 1
        wt = K.wgu[slot]
        if direct_cast and tt == 0:
            P.add("pool", lambda e: e.dma_start(out=wt[:], in_=c.wgu[which][f]),
                  writes=[("wgu", slot)], dma=True)
            P.add("sp", lambda e: e.dma_start(out=c.wgub[which][f], in_=wt[:]),
                  reads=[("wgu", slot)], writes=[("wgub", which, f)], dma=True)
        else:
            P.add("sp", lambda e: e.dma_start(out=wt[:], in_=c.wgub[which][f]),
                  reads=[("wgub", which, f)], writes=[("wgu", slot)], dma=True)
        pg = K.psA[f % 2]
        pu = K.psB[f % 2]
        for cc in range(8):
            P.add("pe", lambda e, cc=cc: e.matmul(
                pg[:], wt[:, cc * 128:(cc + 1) * 128], xn[:, cc, :], start=(cc == 0), stop=(cc == 7)),
                reads=[("wgu", slot), ("xnT", tt % 2, cc)], writes=[("psA", f % 2)])
        for cc in range(8):
            P.add("pe", lambda e, cc=cc: e.matmul(
                pu[:], wt[:, (8 + cc) * 128:(9 + cc) * 128], xn[:, cc, :], start=(cc == 0), stop=(cc == 7)),
                reads=[("wgu", slot), ("xnT", tt % 2, cc)], writes=[("psB", f % 2)])
        sg = K.sg[f % 2]
        P.add("act", lambda e: e.activation(sg[:], pg[:], AF.Silu),
              reads=[("psA", f % 2)], writes=[("sg", f % 2)])
        P.add("dve", lambda e: e.tensor_tensor(K.hT[:, f, :], sg[:], pu[:], ALU.mult),
              reads=[("sg", f % 2), ("psB", f % 2)], writes=[("hT", f)])

    def down(tt, m):
        slot = st["dl"] % K.NWD
        st["dl"] += 1
        wt = K.wd[slot]
        if direct_cast and tt == 0:
            P.add("pool", lambda e: e.dma_start(out=wt[:].rearrange("p (a b) -> p a b", a=2),
                                                in_=c.wd[which][m].rearrange("p (a b) -> p a b", a=2)),
                  writes=[("wd", slot)], dma=True)
            P.add("sp", lambda e: e.dma_start(out=c.wdb[which][m], in_=wt[:]),
                  reads=[("wd", slot)], writes=[("wdb", which, m)], dma=True)
        else:
            P.add("sp", lambda e: e.dma_start(out=wt[:], in_=c.wdb[which][m]),
                  reads=[("wdb", which, m)], writes=[("wd", slot)], dma=True)
        py = K.psC[m % 2]
        for f in range(NF):
            P.add("pe", lambda e, f=f: e.matmul(
                py[:], wt[:, f * 128:(f + 1) * 128], K.hT[:, f, :], start=(f == 0), stop=(f == NF - 1)),
                reads=[("wd", slot), ("hT", f)], writes=[("psC", m % 2)])
        P.add("dve", lambda e: e.tensor_copy(K.yT[:, m, :], py[:]),
              reads=[("psC", m % 2)], writes=[("yT", m)])
        P.add("act", lambda e: e.activation(K.sq[:, m, :], K.yT[:, m, :], AF.Square),
              reads=[("yT", m)], writes=[("sq", m)])

    def post_piece(tt, k):
        ts = slice(tt * TT, (tt + 1) * TT)
        if k == 0:
            emit_rstd(P, K, K.ps_stat, "sq", D, K.rstd2, "rstd2")
            return
        for m in (k - 1,):
            tmp = K.tmp[m % 4]
            P.add("dve", lambda e, m=m, tmp=tmp: e.scalar_tensor_tensor(
                out=tmp[:], in0=K.yT[:, m, :], scalar=K.gains[:, gpost, m:m + 1],
                in1=K.rstd2[:], op0=ALU.mult, op1=ALU.mult),
                reads=[("yT", m), "rstd2", "gains"], writes=[("tmp", m % 4)])
            P.add("pool", lambda e, m=m, tmp=tmp: e.tensor_tensor(
                K.XT[:, m, ts], K.XT[:, m, ts], tmp[:], ALU.add),
                reads=[("tmp", m % 4), ("XT", tt, m)], writes=[("XT", tt, m)])

    prenorm(0)
    for tt in range(NTT):
        for f in range(NF):
            gateup(tt, f)
            if x_gen is not None and tt == 0:
                next(x_gen, None)
            if tt > 0 and f < 9:
                post_piece(tt - 1, f)
            if f == 9 and tt + 1 < NTT:
                prenorm(tt + 1)
        for m in range(8):
            down(tt, m)
        if direct_cast and tt == 0:
            emit_cast_mixer_weights(ph, c, part="in")
            pass
    for k in range(9):
        post_piece(NTT - 1, k)


LAM_INIT = 0.8 - 0.6 * math.exp(-0.3 * 0)
COMMON_END = 68 * 1024


def alloc_common(ph, c, K):
    K.XT = ph.sb("XT", [128, 8, T], F32)
    K.ident = ph.sb("ident", [128, 128], F32)
    K.identb = ph.sb("identb", [128, 128], BF16)
    K.maskneg = ph.sb("maskneg", [128, 128], BF16)
    K.gains = ph.sb("gains", [128, 8, 8], F32)
    K.ones = ph.sb("ones", [128, 128], BF16)
    K.epsb = ph.sb("epsb", [128, 1], F32)
    K.nlam = ph.sb("nlam", [128, 1], F32)
    K.gsubb = ph.sb("gsubb", [128, 128], F32)
    K.vecs = ph.sb("vecs", [128, 8], F32)
    K.a8 = ph.sb("a8", [128, 2, 16], F32)
    K.a8b = ph.sb("a8b", [128, 2, 16], F32)
    K.mhalf = ph.sb("mhalf", [128, 1], F32)
    assert ph.cur <= COMMON_END
    ph.cur = COMMON_END


def emit_consts(ph, c, K):
    P = ph.P
    P.add("sp", lambda e: e.dma_start(out=K.ident[:], in_=c.ident), writes=["ident"], dma=True)
    P.add("sp", lambda e: e.dma_start(out=K.gains[:], in_=c.gains), writes=["gains"], dma=True)
    P.add("sp", lambda e: e.dma_start(out=K.gsubb[:], in_=c.gsub), writes=["gsubb"], dma=True)
    P.add("sp", lambda e: e.dma_start(out=K.vecs[:], in_=c.vecs), writes=["vecs"], dma=True)
    P.add("pool", lambda e: e.dma_start(out=K.maskneg[:], in_=c.maskneg), writes=["maskneg"], dma=True)
    P.add("pool", lambda e: e.memset(K.ones[:], 1.0), writes=["ones"])
    P.add("pool", lambda e: e.memset(K.epsb[:], EPS), writes=["epsb"])
    P.add("pool", lambda e: e.memset(K.mhalf[:], -0.5), writes=["mhalf"])
    P.add("dve", lambda e: e.tensor_copy(K.identb[:], K.ident[:]), reads=["ident"], writes=["identb"])
    for g in (G_FF1_POST, G_FF2_POST):
        P.add("dve", lambda e, g=g: e.tensor_scalar(K.gains[:, g, :], K.gains[:, g, :], 0.5, None, ALU.mult),
              reads=["gains"], writes=["gains"])
    P.add("dve", lambda e: e.tensor_scalar(K.gsubb[:], K.gsubb[:], 1.0 - LAM_INIT, None, ALU.mult),
          reads=["gsubb"], writes=["gsubb"])
    lamv = ph.sb("lamv", [128, 4, 64], F32)
    junk = ph.sb("lamjunk", [128, 64], F32)
    ss = ph.sb("lamss", [128, 4], F32)
    P.add("sp", lambda e: e.dma_start(out=lamv[:], in_=c.lamv), writes=["lamv"], dma=True)
    for i in range(2):
        P.add("dve", lambda e, i=i: e.tensor_tensor(junk[:], lamv[:, 2 * i, :], lamv[:, 2 * i + 1, :], ALU.mult),
              reads=["lamv"], writes=["lamjunk"])
        P.add("dve", lambda e, i=i: e.tensor_reduce(ss[:, i:i + 1], junk[:], mybir.AxisListType.X, ALU.add),
              reads=["lamjunk"], writes=[("lamss", i)])
        P.add("act", lambda e, i=i: e.activation(ss[:, 2 + i:3 + i], ss[:, i:i + 1], AF.Exp),
              reads=[("lamss", i)], writes=[("lamss", 2 + i)])
    P.add("dve", lambda e: e.tensor_tensor(K.nlam[:], ss[:, 3:4], ss[:, 2:3], ALU.subtract),
          reads=[("lamss", 2), ("lamss", 3)], writes=["nlam"])
    P.add("dve", lambda e: e.tensor_scalar(K.nlam[:], K.nlam[:], -LAM_INIT, None, ALU.add),
          reads=["nlam"], writes=["nlam"])


def alloc_ffn(ph, K):
    K.xio = [ph.sb(f"xio{i}", [128, D], F32) for i in range(2)]
    K.sq = ph.sb("sq", [128, 8, TT], BF16)
    K.xnT2 = [ph.sb(f"xnT{i}", [128, 8, TT], BF16) for i in range(2)]
    K.hT = ph.sb("hT", [128, NF, TT], BF16)
    K.yT = ph.sb("yT", [128, 8, TT], F32)
    K.rt = ph.sb("rt", [128, TT], F32)
    K.rstd = ph.sb("rstd", [128, TT], F32)
    K.rstd2 = ph.sb("rstd2", [128, TT], F32)
    K.sg = [ph.sb(f"sg{i}", [128, TT], F32) for i in range(2)]
    K.tmp = [ph.sb(f"tmp{i}", [128, TT], F32) for i in range(4)]
    K.NWGU = 6
    K.NWD = 4
    K.wgu = [ph.sb(f"wgu{i}", [128, 2 * 8 * 128], BF16) for i in range(K.NWGU)]
    K.wd = [ph.sb(f"wd{i}", [128, NF * 128], BF16) for i in range(K.NWD)]
    K.psA = [ph.ps(f"psA{i}", [128, 512]) for i in range(2)]
    K.psB = [ph.ps(f"psB{i}", [128, 512]) for i in range(2)]
    K.psC = [ph.ps(f"psC{i}", [128, 512]) for i in range(2)]
    K.ps_stat = ph.ps("ps_stat", [128, 512])


M_UT = COMMON_END + 0
M_QT = COMMON_END + 16384
M_KT = COMMON_END + 32768
M_VA = COMMON_END + 49152
M_OTA = COMMON_END + 65792
M_TMP = COMMON_END + 82176
M_UBAR = COMMON_END + 16384
M_SBF = COMMON_END + 32768
M_GT = COMMON_END + 0


def emit_prenorm(P, K, tt, gidx):
    ts = slice(tt * TT, (tt + 1) * TT)
    XTr = lambda q: ("XT", tt, q)
    xn = K.xnT2[tt % 2]
    for cc in range(8):
        P.add("act", lambda e, cc=cc: e.activation(K.sq[:, cc, :], K.XT[:, cc, ts], AF.Square),
              reads=[XTr(cc)], writes=[("sq", cc)])
    emit_rstd(P, K, K.ps_stat, "sq", D, K.rstd, "rstd")
    for cc in range(8):
        P.add("dve", lambda e, cc=cc: e.scalar_tensor_tensor(
            out=xn[:, cc, :], in0=K.XT[:, cc, ts], scalar=K.gains[:, gidx, cc:cc + 1],
            in1=K.rstd[:], op0=ALU.mult, op1=ALU.mult),
            reads=[XTr(cc), "rstd", "gains"], writes=[("xnT", tt % 2, cc)])


def emit_proj(ph, c, K):
    P = ph.P
    K.uT = ph.sb("uT", [128, 4, T], BF16, at=M_UT)
    K.qT = ph.sb("qT", [128, 4, T], BF16, at=M_QT)
    K.kT = ph.sb("kT", [128, 4, T], BF16, at=M_KT)
    K.vA = ph.sb("vA", [128, 16, 4, 130], BF16, at=M_VA)
    ph.cur = M_TMP
    K.sq = ph.sb("sq", [128, 8, TT], BF16)
    K.xnT2 = [ph.sb(f"xnT{i}", [128, 8, TT], BF16) for i in range(2)]
    K.rt = ph.sb("rt", [128, TT], F32)
    K.rstd = ph.sb("rstd", [128, TT], F32)
    NW = 8
    wr = [ph.sb(f"wr{i}", [128, 8 * 128], BF16) for i in range(NW)]
    winv = ph.sb("winv", [128, 8 * 512], BF16)
    K.ps_stat = ph.ps("ps_stat", [128, 512])
    NPA = 4
    psA = [ph.ps(f"psA{i}", [128, 512]) for i in range(NPA)]
    psV = [ph.ps(f"psV{i}", [128, 512]) for i in range(2)]
    P.add("sp", lambda e: e.dma_start(out=winv[:], in_=c.winvb), reads=["winvb"], writes=["winv"], dma=True)
    P.add("pool", lambda e: e.memset(K.vA[:, :, :, 128:129], 1.0), writes=["vAones"])
    wl = 0
    ev = 0
    emit_prenorm(P, K, 0, G_MIX_PRE)
    for tt in range(NTT):
        ts = slice(tt * TT, (tt + 1) * TT)
        xn = K.xnT2[tt % 2]
        for j in range(12):
            if j == 6 and tt + 1 < NTT:
                emit_prenorm(P, K, tt + 1, G_MIX_PRE)
            slot = wl % NW
            wl += 1
            wt = wr[slot]
            P.add("sp", lambda e, wt=wt, j=j: e.dma_start(out=wt[:], in_=c.winfb[j]),
                  reads=[("winfb", j)], writes=[("wr", slot)], dma=True)
            pa = psA[j % NPA]
            for cc in range(8):
                P.add("pe", lambda e, wt=wt, cc=cc, pa=pa, xn=xn: e.matmul(
                    pa[:], wt[:, cc * 128:(cc + 1) * 128], xn[:, cc, :], start=(cc == 0), stop=(cc == 7)),
                    reads=[("wr", slot), ("xnT", tt % 2, cc)], writes=[("psA", j % NPA)])
            if j < 4:
                dst, res = K.qT[:, j, ts], ("qT", j, tt)
            elif j < 8:
                dst, res = K.kT[:, j - 4, ts], ("kT", j - 4, tt)
            else:
                uTp = K.uT[:].rearrange("p j (s n) -> p j s n", s=8)
                dst, res = uTp[:, j - 8, :, tt * 64:(tt + 1) * 64], ("uT", j - 8, tt)
            srcp = pa[:] if j < 8 else pa[:].rearrange("p (n s) -> p s n", s=8)
            if ev % 2 == 0:
                P.add("act", lambda e, dst=dst, srcp=srcp: e.activation(dst, srcp, AF.Copy),
                      reads=[("psA", j % NPA)], writes=[res])
            else:
                P.add("dve", lambda e, dst=dst, srcp=srcp: e.tensor_copy(dst, srcp),
                      reads=[("psA", j % NPA)], writes=[res])
            ev += 1
        for b4 in range(4):
            tb = tt * 4 + b4
            pv = psV[b4 % 2]
            for cc in range(8):
                P.add("pe", lambda e, cc=cc, pv=pv, b4=b4, xn=xn: e.matmul(
                    pv[:], xn[:, cc, b4 * 128:(b4 + 1) * 128], winv[:, cc * 512:(cc + 1) * 512],
                    start=(cc == 0), stop=(cc == 7)),
                    reads=["winv", ("xnT", tt % 2, cc)], writes=[("psV", b4 % 2)])
            dst = K.vA[:, tb, :, 0:128]
            srcv = pv[:].rearrange("p (h v) -> p h v", h=4)
            if ev % 2 == 0:
                P.add("act", lambda e, dst=dst, srcv=srcv: e.activation(dst, srcv, AF.Copy),
                      reads=[("psV", b4 % 2)], writes=[("vA", tb)])
            else:
                P.add("dve", lambda e, dst=dst, srcv=srcv: e.tensor_copy(dst, srcv),
                      reads=[("psV", b4 % 2)], writes=[("vA", tb)])
            ev += 1


def emit_attn(ph, c, K):
    P = ph.P
    K.qT = ph.sb("qT", [128, 4, T], BF16, at=M_QT)
    K.kT = ph.sb("kT", [128, 4, T], BF16, at=M_KT)
    K.vA = ph.sb("vA", [128, 16, 4, 130], BF16, at=M_VA)
    K.oTa = ph.sb("oTa", [128, 4, T], BF16, at=M_OTA)
    ph.cur = M_TMP
    NPT = 6
    pT = [ph.sb(f"pT{i}", [128, 512], BF16) for i in range(NPT)]
    o1 = [ph.sb(f"o1_{i}", [128, 4, 128], F32) for i in range(2)]
    od = [ph.sb(f"od{i}", [128, 128], F32) for i in range(8)]
    junk = [ph.sb(f"junk{i}", [128, 128], F32) for i in range(4)]
    sm = [ph.sb(f"sm{i}", [128, 8], F32) for i in range(8)]
    NPS = 4
    psS = [ph.ps(f"psS{i}", [128, 512]) for i in range(NPS)]
    acc = [[ph.ps(f"acc{r}{b}", [128, 512]) for b in range(2)] for r in range(2)]
    ob_tok = ph.sb("ob_tok", [128, 16, 4, 128], BF16)
    items = []
    si = 0
    pi = 0
    rnd = 0
    fin = 0
    for h in range(4):
        for qt in range(NTT):
            for cmap in range(2):
                r = rnd % 2
                rnd += 1
                rows = slice(cmap * 64, (cmap + 1) * 64)
                started = [False, False]
                for kb in range(4 * qt + 4):
                    j = kb - 4 * qt
                    qlo = max(j, 0) * 128
                    sb_ = psS[si % NPS]
                    sres = ("psS", si % NPS)
                    si += 1
                    pt = pT[pi % NPT]
                    pres = ("pT", pi % NPT)
                    pi += 1

                    def s_part(sb_=sb_, sres=sres, qlo=qlo, kb=kb, rows=rows, h=h, qt=qt, j=j):
                        P.add("pe", lambda e: e.matmul(
                            sb_[:, qlo:512], K.kT[rows, h, kb * 128:(kb + 1) * 128],
                            K.qT[rows, h, qt * 512 + qlo:(qt + 1) * 512], start=True, stop=(j < 0)),
                            reads=[("kT", h, kb // 4), ("qT", h, qt)], writes=[sres])
                        if j >= 0:
                            P.add("pe", lambda e: e.matmul(
                                sb_[:, qlo:qlo + 128], K.identb[:], K.maskneg[:], start=False, stop=True,
                                skip_group_check=True),
                                reads=["identb", "maskneg"], writes=[sres])

                    sts = []
                    for qb in range(max(j, 0), 4):
                        sts.append(not started[qb // 2])
                        started[qb // 2] = True

                    def r_part(sb_=sb_, sres=sres, pt=pt, pres=pres, qlo=qlo, kb=kb, h=h, qt=qt, j=j, r=r, sts=sts):
                        P.add("act", lambda e: e.activation(
                            pt[:, qlo:512], sb_[:, qlo:512], AF.Exp, scale=0.125),
                            reads=[sres], writes=[pres])
                        for ii, qb in enumerate(range(max(j, 0), 4)):
                            bank = acc[r][qb // 2]
                            col = (qb % 2) * 256
                            st = sts[ii]
                            P.add("pe", lambda e, bank=bank, col=col, qb=qb, st=st: e.matmul(
                                bank[:, col:col + 129], pt[:, qb * 128:(qb + 1) * 128], K.vA[:, kb, h, 0:129],
                                start=st, stop=(kb == 4 * qt + qb), skip_group_check=True),
                                reads=[pres, ("vA", kb), "vAones"], writes=[("acc", r, qb // 2)])

                    items.append((s_part, r_part))
                items.append((None, (lambda h=h, qt=qt, cmap=cmap, r=r, rnd=rnd: finalize(h, qt, cmap, r, rnd))))
    fin = [0]

    def finalize(h, qt, cmap, r, rnd):
        def acc_of(qb):
            return acc[r][qb // 2], (qb % 2) * 256, ("acc", r, qb // 2)
        par = fin[0] % 2
        fin[0] += 1
        smq = [sm[par * 4 + qb] for qb in range(4)]
        if cmap == 0:
            o1t = o1[(rnd // 2) % 2]
            for qb in range(4):
                bank, col, ares = acc_of(qb)
                P.add("dve", lambda e, qb=qb, bank=bank, col=col: e.reciprocal(
                    smq[qb][:, 0:1], bank[:, col + 128:col + 129]),
                    reads=[ares], writes=[("sm", par, qb, 0)])
            for qb in range(4):
                bank, col, ares = acc_of(qb)
                P.add("dve", lambda e, qb=qb, bank=bank, col=col: e.tensor_scalar(
                    o1t[:, qb, :], bank[:, col:col + 128], smq[qb][:, 0:1], None, ALU.mult),
                    reads=[ares, ("sm", par, qb, 0)], writes=[("o1", (rnd // 2) % 2, qb)])
            return
        o1t = o1[((rnd - 1) // 2) % 2]
        odq = [od[par * 4 + qb] for qb in range(4)]
        for qb in range(4):
            bank, col, ares = acc_of(qb)
            P.add("dve", lambda e, qb=qb, bank=bank, col=col: e.reciprocal(
                smq[qb][:, 1:2], bank[:, col + 128:col + 129]),
                reads=[ares], writes=[("sm", par, qb, 1)])
        for qb in range(4):
            P.add("dve", lambda e, qb=qb: e.tensor_tensor(smq[qb][:, 2:3], smq[qb][:, 1:2], K.nlam[:], ALU.mult),
                  reads=[("sm", par, qb, 1), "nlam"], writes=[("sm", par, qb, 2)])
        for qb in range(4):
            bank, col, ares = acc_of(qb)
            P.add("dve", lambda e, qb=qb, bank=bank, col=col: e.scalar_tensor_tensor(
                out=odq[qb][:], in0=bank[:, col:col + 128], scalar=smq[qb][:, 2:3], in1=o1t[:, qb, :],
                op0=ALU.mult, op1=ALU.add),
                reads=[ares, ("sm", par, qb, 2), ("o1", ((rnd - 1) // 2) % 2, qb)], writes=[("od", par, qb)])
        for qb in range(4):
            P.add("dve", lambda e, qb=qb: e.tensor_tensor(junk[qb][:], odq[qb][:], odq[qb][:], ALU.mult),
                  reads=[("od", par, qb)], writes=[("junk", qb)])
        for qb in range(4):
            P.add("dve", lambda e, qb=qb: e.tensor_reduce(
                smq[qb][:, 3:4], junk[qb][:], mybir.AxisListType.X, ALU.add),
                reads=[("junk", qb)], writes=[("sm", par, qb, 3)])
        for qb in range(4):
            P.add("dve", lambda e, qb=qb: e.tensor_scalar(
                smq[qb][:, 4:5], smq[qb][:, 3:4], 1.0 / 128, EPS, ALU.mult, ALU.add),
                reads=[("sm", par, qb, 3)], writes=[("sm", par, qb, 4)])
        for qb in range(4):
            P.add("pool", lambda e, qb=qb: e.tensor_tensor(smq[qb][:, 5:6], smq[qb][:, 4:5], K.mhalf[:], ALU.pow),
                  reads=[("sm", par, qb, 4), "mhalf"], writes=[("sm", par, qb, 5)])
        for qb in range(4):
            tb = qt * 4 + qb
            P.add("dve", lambda e, qb=qb, tb=tb: e.scalar_tensor_tensor(
                out=ob_tok[:, tb, h, :], in0=odq[qb][:], scalar=smq[qb][:, 5:6], in1=K.gsubb[:],
                op0=ALU.mult, op1=ALU.mult),
                reads=[("od", par, qb), ("sm", par, qb, 5), "gsubb"], writes=[("ob", tb, h)])

    LOOK = NPS - 1
    seq_s = [it for it in items if it[0] is not None]
    ns = 0
    nr = 0
    for it in items:
        if it[0] is None:
            it[1]()
            continue
        while ns < len(seq_s) and ns <= nr + LOOK:
            seq_s[ns][0]()
            ns += 1
        it[1]()
        nr += 1
    for tb in range(16):
        bank = psS[tb % NPS]
        bres = ("psS", tb % NPS)
        for h in range(4):
            P.add("pe", lambda e, bank=bank, tb=tb, h=h: e.matmul(
                bank[:, h * 128:(h + 1) * 128], ob_tok[:, tb, h, :], K.identb[:], start=True, stop=True,
                skip_group_check=True),
                reads=[("ob", tb, h), "identb"], writes=[bres])
        dst = K.oTa[:, :, tb * 128:(tb + 1) * 128]
        srcv = bank[:].rearrange("p (h t) -> p h t", h=4)
        if tb % 2 == 0:
            P.add("act", lambda e, dst=dst, srcv=srcv: e.activation(dst, srcv, AF.Copy),
                  writes=[bres, ("oTa", tb)])
        else:
            P.add("dve", lambda e, dst=dst, srcv=srcv: e.tensor_copy(dst, srcv),
                  writes=[bres, ("oTa", tb)])


TWO_PI = 2.0 * math.pi
MAGIC = 12582912.0


def emit_s5_setup(ph, c, K, hook=None, hook_every=8):
    class _PW:
        def __init__(self, P):
            self.P = P
            self.n = 0
            self.lim = int(os.environ.get("KSTEP", "100000"))

        def add(self, *a, **k):
            self.n += 1
            if self.n > self.lim:
                return None
            if os.environ.get("KSTEPV") and self.n == self.lim:
                import traceback
                traceback.print_stack(limit=4)
            r = self.P.add(*a, **k)
            if hook is not None and self.n % hook_every == 0:
                hook()
            return r
    P = _PW(ph.P)
    sp_ = ph.sb("s5p", [128, S5P_COLS], F32)
    cmask = ph.sb("cmask", [128, 128], F32)
    P.add("sp", lambda e: e.dma_start(out=sp_[:], in_=c.s5p), writes=["s5p"], dma=True)
    P.add("sp", lambda e: e.dma_start(out=cmask[:], in_=c.cmask), writes=["cmask"], dma=True)
    for q in range(4):
        src = c.esel[:, q * 2048:(q + 1) * 2048]
        dst = c.eb[:, q * 2048:(q + 1) * 2048]
        P.add("pool", lambda e, src=src, dst=dst: e.dma_start(out=dst, in_=src), writes=[("eb", q)], dma=True)
    are, aim, ldt = sp_[:, 0:16], sp_[:, 16:32], sp_[:, 32:48]
    bre = sp_[:, 48:304].rearrange("p (g h) -> p g h", g=16)
    bim = sp_[:, 304:560].rearrange("p (g h) -> p g h", g=16)
    cre = sp_[:, 560:816].rearrange("p (g h) -> p g h", g=16)
    cim = sp_[:, 816:1072].rearrange("p (g h) -> p g h", g=16)
    kk = sp_[:, 1072:1096].rearrange("p (w k) -> p w k", w=3)
    dD = sp_[:, 1096:1128]
    cnt = [0]

    def T_(shape):
        cnt[0] += 1
        return ph.sb(f"t{cnt[0]}", shape, F32), f"t{cnt[0]}"

    def tt(out, a, b, op, r, w, eng="dve"):
        P.add(eng, lambda e: e.tensor_tensor(out, a, b, op), reads=r, writes=w)

    def ts(out, a, s1, s2, op0, op1, r, w):
        P.add("dve", lambda e: e.tensor_scalar(out, a, s1, s2, op0, op1), reads=r, writes=w)

    def sin_of(x, xr_, n):
        t, tn = T_([128, n])
        r, rn = T_([128, n])
        ts(t[:], x, 1.0 / TWO_PI, MAGIC, ALU.mult, ALU.add, [xr_], [tn])
        ts(t[:], t[:], -MAGIC, None, ALU.add, ALU.bypass, [tn], [tn])
        P.add("dve", lambda e: e.scalar_tensor_tensor(out=r[:], in0=t[:], scalar=-TWO_PI, in1=x,
                                                       op0=ALU.mult, op1=ALU.add), reads=[tn, xr_], writes=[rn])
        ts(r[:], r[:], 3.141592, -3.141592, ALU.min, ALU.max, [rn], [rn])
        P.add("act", lambda e: e.activation(r[:], r[:], AF.Sin), reads=[rn], writes=[rn])
        return r, rn

    def cexp(xr_t, xr_n, xi_t, xi_n, n, unit=False):
        s_, sn = sin_of(xi_t, xi_n, n)
        x2, x2n = T_([128, n])
        ts(x2[:], xi_t, math.pi / 2, None, ALU.add, ALU.bypass, [xi_n], [x2n])
        c_, cn = sin_of(x2[:], x2n, n)
        if unit:
            return c_, cn, s_, sn
        e_, en = T_([128, n])
        P.add("act", lambda e: e.activation(e_[:], xr_t, AF.Exp), reads=[xr_n], writes=[en])
        tt(c_[:], c_[:], e_[:], ALU.mult, [cn, en], [cn])
        tt(s_[:], s_[:], e_[:], ALU.mult, [sn, en], [sn])
        return c_, cn, s_, sn

    dt, dtn = T_([128, 16])
    P.add("act", lambda e: e.activation(dt[:], ldt, AF.Exp), reads=["s5p"], writes=[dtn])
    xr, xrn = T_([128, 16])
    xi, xin = T_([128, 16])
    tt(xr[:], are, dt[:], ALU.mult, ["s5p", dtn], [xrn])
    tt(xi[:], aim, dt[:], ALU.mult, ["s5p", dtn], [xin])
    if int(os.environ.get('KSET', '9')) < 1:
        return
    c1, c1n, s1, s1n = cexp(xr[:], xrn, xi[:], xin, 16)
    ts(c1[:], c1[:], -1.0, None, ALU.add, ALU.bypass, [c1n], [c1n])
    t1, t1n = T_([128, 16])
    t2, t2n = T_([128, 16])
    rden, rdn = T_([128, 16])
    tt(t1[:], are, are, ALU.mult, ["s5p"], [t1n])
    tt(t2[:], aim, aim, ALU.mult, ["s5p"], [t2n])
    tt(t1[:], t1[:], t2[:], ALU.add, [t1n, t2n], [t1n])
    P.add("dve", lambda e: e.reciprocal(rden[:], t1[:]), reads=[t1n], writes=[rdn])
    cr, crn = T_([128, 16])
    ci, cin = T_([128, 16])
    tt(t1[:], c1[:], are, ALU.mult, [c1n, "s5p"], [t1n])
    tt(t2[:], s1[:], aim, ALU.mult, [s1n, "s5p"], [t2n])
    tt(t1[:], t1[:], t2[:], ALU.add, [t1n, t2n], [t1n])
    tt(cr[:], t1[:], rden[:], ALU.mult, [t1n, rdn], [crn])
    tt(t1[:], s1[:], are, ALU.mult, [s1n, "s5p"], [t1n])
    tt(t2[:], c1[:], aim, ALU.mult, [c1n, "s5p"], [t2n])
    tt(t1[:], t1[:], t2[:], ALU.subtract, [t1n, t2n], [t1n])
    tt(ci[:], t1[:], rden[:], ALU.mult, [t1n, rdn], [cin])
    if int(os.environ.get('KSET', '9')) < 2:
        return
    Br, Brn = T_([128, 16, 16])
    Bi, Bin = T_([128, 16, 16])
    u1, u1n = T_([128, 16, 16])
    crb = cr[:].rearrange("p (g o) -> p g o", o=1).broadcast_to([128, 16, 16])
    cib = ci[:].rearrange("p (g o) -> p g o", o=1).broadcast_to([128, 16, 16])
    tt(Br[:], crb, bre, ALU.mult, [crn, "s5p"], [Brn])
    tt(u1[:], cib, bim, ALU.mult, [cin, "s5p"], [u1n])
    tt(Br[:], Br[:], u1[:], ALU.subtract, [Brn, u1n], [Brn])
    tt(Bi[:], crb, bim, ALU.mult, [crn, "s5p"], [Bin])
    tt(u1[:], cib, bre, ALU.mult, [cin, "s5p"], [u1n])
    tt(Bi[:], Bi[:], u1[:], ALU.add, [Bin, u1n], [Bin])
    if int(os.environ.get('KSET', '9')) < 3:
        return
    k8r, k8rn = T_([128, 16])
    k8i, k8in = T_([128, 16])
    ts(k8r[:], xr[:], 8.0, None, ALU.mult, ALU.bypass, [xrn], [k8rn])
    ts(k8i[:], xi[:], 8.0, None, ALU.mult, ALU.bypass, [xin], [k8in])
    a8c, a8cn, a8s, a8sn = cexp(k8r[:], k8rn, k8i[:], k8in, 16)
    P.add("dve", lambda e: e.tensor_copy(K.a8[:, 0, :], a8c[:]), reads=[a8cn], writes=["a8"])
    P.add("dve", lambda e: e.tensor_copy(K.a8[:, 1, :], a8s[:]), reads=[a8sn], writes=["a8"])
    P.add("dve", lambda e: e.tensor_copy(K.a8b[:, 0, :], a8s[:]), reads=[a8sn], writes=["a8b"])
    ts(K.a8b[:, 1, :], a8s[:], -1.0, None, ALU.mult, ALU.bypass, [a8sn], ["a8b"])
    if os.environ.get('KVERB'):
        print('setup ops before powers', P.n)
    if int(os.environ.get('KSET', '9')) < 4:
        return
    outs = []
    xrb = xr[:].rearrange("p (g o) -> p g o", o=1).broadcast_to([128, 16, 8])
    xib = xi[:].rearrange("p (g o) -> p g o", o=1).broadcast_to([128, 16, 8])
    W4 = [128, 16, 8, 16]
    f1, f1n = T_(W4)
    f2, f2n = T_(W4)
    X7 = [ph.sb(f"X7{i}", W4, BF16) for i in range(2)]
    Zt = [ph.sb(f"Zt{i}", W4, BF16) for i in range(2)]
    Yp = ph.sb("Yp", [128, 2, 16, 128], BF16)
    for w in range(3):
        kb = kk[:, w:w + 1, :].broadcast_to([128, 16, 8])
        kr, krn = T_([128, 16, 8])
        ki, kin = T_([128, 16, 8])
        tt(kr[:], xrb, kb, ALU.mult, [xrn, "s5p"], [krn])
        tt(ki[:], xib, kb, ALU.mult, [xin, "s5p"], [kin])
        pr, prn, pi_, pin = cexp(kr[:].rearrange("p g k -> p (g k)"), krn,
                                 ki[:].rearrange("p g k -> p (g k)"), kin, 128)
        prb = pr[:].rearrange("p (g k o) -> p g k o", g=16, o=1).broadcast_to(W4)
        pib = pi_[:].rearrange("p (g k o) -> p g k o", g=16, o=1).broadcast_to(W4)
        if w == 0:
            mre = Br[:].rearrange("p g (o h) -> p g o h", o=1).broadcast_to(W4)
            mim = Bi[:].rearrange("p g (o h) -> p g o h", o=1).broadcast_to(W4)
            mrn, min_ = Brn, Bin
            ore, oim = X7[0][:], X7[1][:]
            orn, oin = "X7re", "X7im"
        else:
            mre = cre.rearrange("p g (o h) -> p g o h", o=1).broadcast_to(W4)
            mim = cim.rearrange("p g (o h) -> p g o h", o=1).broadcast_to(W4)
            mrn, min_ = "s5p", "s5p"
            if w == 1:
                ore, oim = Zt[0][:], Zt[1][:]
                orn, oin = "Zre", "Zim"
            else:
                ore = Yp[:, 0, :, :].rearrange("p g (k h) -> p g k h", k=8)
                oim = Yp[:, 1, :, :].rearrange("p g (k h) -> p g k h", k=8)
                orn, oin = "Ypre", "Ypim"
        tt(f1[:], prb, mre, ALU.mult, [prn, mrn], [f1n])
        tt(f2[:], pib, mim, ALU.mult, [pin, min_], [f2n])
        tt(ore, f1[:], f2[:], ALU.subtract, [f1n, f2n], [orn])
        tt(f1[:], prb, mim, ALU.mult, [prn, min_], [f1n])
        tt(f2[:], pib, mre, ALU.mult, [pin, mrn], [f2n])
        if w == 0:
            tt(oim, f1[:], f2[:], ALU.add, [f1n, f2n], [oin])
        else:
            P.add("dve", lambda e, oim=oim: e.scalar_tensor_tensor(
                out=oim, in0=f1[:], scalar=-1.0, in1=f2[:], op0=ALU.mult, op1=ALU.subtract),
                reads=[f1n, f2n], writes=[oin])
    Ypm = [ph.sb(f"Ypm{a}", [128, 2, 16, 128], BF16) for a in range(2)]
    for a in range(2):
        keep = slice(a * 64, (a + 1) * 64)
        P.add("pool", lambda e, a=a: e.memset(Ypm[a][:], 0.0), writes=[("Ypmz", a)])
        P.add("dve", lambda e, a=a, keep=keep: e.tensor_copy(Ypm[a][keep], Yp[keep]),
              reads=["Ypre", "Ypim", ("Ypmz", a)], writes=[("Ypm", a)])
        P.add("sp", lambda e, a=a: e.dma_start(out=c.ypb[a], in_=Ypm[a][:].rearrange("p r g k -> p (r g k)")),
              reads=[("Ypm", a)], writes=[("ypb", a)], dma=True)
    if os.environ.get('KVERB'):
        print('setup ops after powers', P.n)
    if int(os.environ.get('KSET', '9')) < 5:
        return
    xt7 = ph.sb("xt7", [128, 2, 32, 64], BF16)
    psX = [ph.ps(f"psX{i}", [128, 512]) for i in range(2)]
    bi = 0
    for ri in range(2):
        for gq in range(4):
            bank = psX[bi % 2]
            bres = ("psX", bi % 2)
            bi += 1
            for gi in range(4):
                gp = gq * 4 + gi
                src = X7[ri][:, gp, :, :].rearrange("p k h -> p (k h)")
                P.add("pe", lambda e, bank=bank, gi=gi, src=src: e.matmul(
                    bank[:, gi * 128:(gi + 1) * 128], src, K.identb[:], start=True, stop=True,
                    skip_group_check=True),
                    reads=["X7re" if ri == 0 else "X7im", "identb"], writes=[bres])
            dst = xt7[:, ri, gq * 8:(gq + 1) * 8, :]
            srcv = bank[:, 0:512].rearrange("p (g q) -> p g q", g=8)
            P.add("act" if gq % 2 == 0 else "dve",
                  (lambda e, dst=dst, srcv=srcv: e.activation(dst, srcv, AF.Copy)) if gq % 2 == 0 else
                  (lambda e, dst=dst, srcv=srcv: e.tensor_copy(dst, srcv)),
                  reads=[bres], writes=[bres, "xt7"])
    P.add("sp", lambda e: e.dma_start(out=c.xt7b, in_=xt7[:].rearrange("p r g q -> p (r g q)")),
          reads=["xt7"], writes=["xt7b"], dma=True)
    if int(os.environ.get('KSET', '9')) < 6:
        return
    ttl = ph.sb("ttl", [128, 32, 128], BF16)
    Zm = [[ph.sb(f"Zm{a}{b}", W4, BF16) for b in range(2)] for a in range(2)]
    for a in range(2):
        keep = slice(a * 64, (a + 1) * 64)
        for b in range(2):
            P.add("pool", lambda e, a=a, b=b: e.memset(Zm[a][b][:], 0.0), writes=[("Zmz", a, b)])
            P.add("dve", lambda e, a=a, b=b, keep=keep: e.tensor_copy(Zm[a][b][keep], Zt[b][keep]),
                  reads=["Zre", "Zim", ("Zmz", a, b)], writes=["Zm"])
    tm = [ph.sb(f"tm{i}", [128, 4, 128], F32) for i in range(2)]
    psT = [ph.ps(f"psTT{i}", [128, 512]) for i in range(2)]
    cmb = cmask[:].rearrange("p (o q) -> p o q", o=1).broadcast_to([128, 4, 128])
    for gq in range(8):
        bank = psT[gq % 2]
        bres = ("psTT", gq % 2)
        for gi in range(4):
            g = gq * 4 + gi
            gp, g2 = g // 2, g % 2
            rows = slice(g2 * 64, (g2 + 1) * 64)
            for ri in range(2):
                lh = X7[ri][:, gp, :, :].rearrange("p k h -> p (k h)")
                rh = Zm[g2][ri][:, gp, :, :].rearrange("p k h -> p (k h)")
                P.add("pe", lambda e, bank=bank, gi=gi, lh=lh, rh=rh, ri=ri, first=(gi == 0 and ri == 0): e.matmul(
                    bank[:, gi * 128:(gi + 1) * 128], lh, rh, start=first, stop=(ri == 1),
                    skip_group_check=True),
                    reads=["X7re", "X7im", "Zm"], writes=[bres])
        tmt = tm[gq % 2]
        P.add("dve", lambda e, tmt=tmt, bank=bank: e.tensor_tensor(
            tmt[:], bank[:].rearrange("p (g q) -> p g q", g=4), cmb, ALU.mult),
            reads=["cmask"], writes=[bres, ("tm", gq % 2)])
        for gi in range(4):
            g = gq * 4 + gi
            P.add("dve", lambda e, tmt=tmt, gi=gi, g=g: e.scalar_tensor_tensor(
                out=ttl[:, g, :], in0=K.ident[:], scalar=dD[:, g:g + 1], in1=tmt[:, gi, :],
                op0=ALU.mult, op1=ALU.add),
                reads=[("tm", gq % 2), "ident", "s5p"], writes=["ttl"])
    P.add("sp", lambda e: e.dma_start(out=c.ttb, in_=ttl[:].rearrange("p g q -> p (g q)")),
          reads=["ttl"], writes=["ttb"], dma=True)


def emit_s5a(ph, c, K):
    P = ph.P
    K.uT = ph.sb("uT", [128, 4, T], BF16, at=M_UT)
    K.Ubar = ph.sb("Ubar", [128, 32, 256], BF16, at=M_UBAR)
    K.Sbf = ph.sb("Sbf", [128, 2, 16, 257], BF16, at=M_SBF)
    ph.cur = M_TMP
    NCH, CL = 9, 32
    GRP = [(0, 5), (5, 9)]
    ph.cur = COMMON_END + 49280
    P1 = [[ph.sb(f"P1_{g}{i}", [128, b - a, 2, 16], F32) for i in range(2)] for g, (a, b) in enumerate(GRP)]
    P2 = [[ph.sb(f"P2_{g}{i}", [128, b - a, 2, 16], F32) for i in range(2)] for g, (a, b) in enumerate(GRP)]
    XT7_OFF = ph.cur
    xt7 = ph.sb("xt7", [128, 2, 32, 64], BF16)
    assert ph.cur <= M_OTA
    ph.cur = M_TMP
    E = ph.sb("E", [128, 64 * 128], BF16)
    L = ph.sb("L", [128, NCH * CL, 2, 16], F32)
    F1 = [ph.sb("F1", [128, 32, 16], F32, at=XT7_OFF)] * 2
    F2 = [ph.sb("F2", [128, 32, 16], F32, at=XT7_OFF + 2048)] * 2
    F3 = [ph.sb("F3", [128, 32, 16], F32, at=XT7_OFF + 4096)] * 2
    F4 = [ph.sb("F4", [128, 32, 16], F32, at=XT7_OFF + 6144)] * 2
    NPU, NPL = 4, 4
    psU = [ph.ps(f"psU{i}", [128, 512]) for i in range(NPU)]
    psL = [ph.ps(f"psL{i}", [128, 512]) for i in range(NPL)]
    P.add("sp", lambda e: e.dma_start(out=E[:], in_=c.eb), reads=[("eb", q) for q in range(4)],
          writes=["E"], dma=True)
    P.add("sp", lambda e: e.dma_start(out=xt7[:].rearrange("p r g q -> p (r g q)"), in_=c.xt7b),
          reads=["xt7b"], writes=["xt7"], dma=True)
    P.add("pool", lambda e: e.memset(K.Sbf[:, :, :, 0:1], 0.0), writes=["Sbf0"])
    uTv = K.uT[:].rearrange("p j (s n) -> p j s n", s=8)
    for g in range(32):
        j, glo = g // 8, g % 8
        bank = psU[g % NPU]
        for sg in range(8):
            idx = glo * 8 + sg
            P.add("pe", lambda e, bank=bank, idx=idx, j=j, sg=sg: e.matmul(
                bank[:, 0:256], E[:, idx * 128:(idx + 1) * 128], uTv[:, j, sg, :],
                start=(sg == 0), stop=(sg == 7)),
                reads=["E"], writes=[("psU", g % NPU)])
        if g % 2 == 0:
            P.add("act", lambda e, g=g, bank=bank: e.activation(K.Ubar[:, g, :], bank[:, 0:256], AF.Copy),
                  writes=[("psU", g % NPU), ("Ubar", g)])
        else:
            P.add("dve", lambda e, g=g, bank=bank: e.tensor_copy(K.Ubar[:, g, :], bank[:, 0:256]),
                  writes=[("psU", g % NPU), ("Ubar", g)])
    for gp in range(16):
        bank = psL[gp % NPL]
        for g2 in range(2):
            g = 2 * gp + g2
            for ri in range(2):
                P.add("pe", lambda e, bank=bank, g2=g2, g=g, ri=ri: e.matmul(
                    bank[g2 * 64:(g2 + 1) * 64, ri * 256:(ri + 1) * 256], xt7[:, ri, g, :], K.Ubar[:, g, :],
                    start=True, stop=True, skip_group_check=True),
                    reads=["xt7", ("Ubar", g)], writes=[("psL", gp % NPL)])
        dst = L[:, 0:256, :, gp].rearrange("p n r -> p r n")
        srcv = bank[:].rearrange("p (r n) -> p r n", r=2)
        if gp % 2 == 0:
            P.add("act", lambda e, dst=dst, srcv=srcv: e.activation(dst, srcv, AF.Copy),
                  writes=[("psL", gp % NPL), ("Lgp", gp)])
        else:
            P.add("dve", lambda e, dst=dst, srcv=srcv: e.tensor_copy(dst, srcv),
                  writes=[("psL", gp % NPL), ("Lgp", gp)])
    Lc = L[:].rearrange("p (c j) r g -> p c j r g", c=NCH)
    allL = [("Lgp", gp) for gp in range(16)]
    P.add("dve", lambda e: e.memset(L[:, 256:288, :, :], 0.0), reads=allL, writes=[("Lg", 1), "xt7"])
    P.add("dve", lambda e: e.tensor_copy(L[:, 256, :, :], K.a8[:]), writes=[("Lg", 1)])
    for j in range(1, CL):
        for stage in range(5):
            for g, (ca, cb) in enumerate(GRP):
                nchk = cb - ca
                p1, p2 = P1[g][j % 2], P2[g][j % 2]
                a1c = K.a8[:, 0:1, :].rearrange("p (c r) g -> p c r g", c=1).broadcast_to([128, nchk, 2, 16])
                a2c = K.a8b[:].rearrange("p (c r) g -> p c r g", c=1).broadcast_to([128, nchk, 2, 16])
                lg = ("Lg", g)
                if stage == 0:
                    P.add("dve", lambda e, p1=p1, j=j, ca=ca, cb=cb, a1c=a1c: e.tensor_tensor(
                        p1[:], Lc[:, ca:cb, j - 1, :, :], a1c, ALU.mult),
                        reads=[lg], writes=[("P1", g, j % 2)])
                elif stage == 1:
                    P.add("dve", lambda e, p2=p2, j=j, ca=ca, cb=cb, a2c=a2c: e.tensor_tensor(
                        p2[:], Lc[:, ca:cb, j - 1, :, :], a2c, ALU.mult),
                        reads=[lg], writes=[("P2", g, j % 2)])
                elif stage == 2:
                    P.add("dve", lambda e, p1=p1, j=j, ca=ca, cb=cb: e.tensor_tensor(
                        Lc[:, ca:cb, j, :, :], Lc[:, ca:cb, j, :, :], p1[:], ALU.add),
                        reads=[("P1", g, j % 2)], writes=[lg])
                elif stage == 3:
                    P.add("dve", lambda e, p2=p2, j=j, ca=ca, cb=cb: e.tensor_tensor(
                        Lc[:, ca:cb, j, 0, :], Lc[:, ca:cb, j, 0, :], p2[:, :, 1, :], ALU.add),
                        reads=[("P2", g, j % 2)], writes=[lg])
                else:
                    P.add("dve", lambda e, p2=p2, j=j, ca=ca, cb=cb: e.tensor_tensor(
                        Lc[:, ca:cb, j, 1, :], Lc[:, ca:cb, j, 1, :], p2[:, :, 0, :], ALU.add),
                        reads=[("P2", g, j % 2)], writes=[lg])
    pwr = Lc[:, 8, :, 0, :]
    pwi = Lc[:, 8, :, 1, :]
    LG = [("Lg", 0), ("Lg", 1)]
    for cch in range(1, 8):
        cr = Lc[:, cch - 1, CL - 1:CL, 0, :].broadcast_to([128, CL, 16])
        ci = Lc[:, cch - 1, CL - 1:CL, 1, :].broadcast_to([128, CL, 16])
        f = 0
        P.add("dve", lambda e, cr=cr, f=f: e.tensor_tensor(F1[f][:], pwr, cr, ALU.mult),
              reads=LG, writes=[("F1", f)])
        P.add("dve", lambda e, ci=ci, f=f: e.tensor_tensor(F2[f][:], pwi, ci, ALU.mult),
              reads=LG, writes=[("F2", f)])
        P.add("dve", lambda e, ci=ci, f=f: e.tensor_tensor(F3[f][:], pwr, ci, ALU.mult),
              reads=LG, writes=[("F3", f)])
        P.add("dve", lambda e, cr=cr, f=f: e.tensor_tensor(F4[f][:], pwi, cr, ALU.mult),
              reads=LG, writes=[("F4", f)])
        P.add("dve", lambda e, f=f: e.tensor_tensor(F1[f][:], F1[f][:], F2[f][:], ALU.subtract),
              reads=[("F2", f)], writes=[("F1", f)])
        P.add("dve", lambda e, f=f: e.tensor_tensor(F3[f][:], F3[f][:], F4[f][:], ALU.add),
              reads=[("F4", f)], writes=[("F3", f)])
        P.add("dve", lambda e, cch=cch, f=f: e.tensor_tensor(Lc[:, cch, :, 0, :], Lc[:, cch, :, 0, :], F1[f][:], ALU.add),
              reads=[("F1", f)], writes=LG)
        P.add("dve", lambda e, cch=cch, f=f: e.tensor_tensor(Lc[:, cch, :, 1, :], Lc[:, cch, :, 1, :], F3[f][:], ALU.add),
              reads=[("F3", f)], writes=LG)
    for ri in range(2):
        dst = K.Sbf[:, ri, :, 1:257]
        srcv = L[:, 0:256, ri, :].rearrange("p n g -> p g n")
        P.add("act" if ri == 0 else "dve",
              (lambda e, dst=dst, srcv=srcv: e.activation(dst, srcv, AF.Copy)) if ri == 0 else
              (lambda e, dst=dst, srcv=srcv: e.tensor_copy(dst, srcv)),
              reads=[("Lg", 0), ("Lg", 1), "Sbf0"], writes=[("Sbf", ri)])


def emit_s5b(ph, c, K):
    P = ph.P
    K.gT = ph.sb("gT", [128, 4, T], BF16, at=M_GT)
    K.Ubar = ph.sb("Ubar", [128, 32, 256], BF16, at=M_UBAR)
    K.Sbf = ph.sb("Sbf", [128, 2, 16, 257], BF16, at=M_SBF)
    ph.cur = M_TMP
    E = ph.sb("E", [128, 64 * 128], BF16)
    ttl = ph.sb("ttl", [128, 32, 128], BF16)
    yp = [ph.sb(f"yp{a}", [128, 2, 16, 128], BF16) for a in range(2)]
    gst = [ph.sb(f"gst{i}", [128, 8, 256], BF16) for i in range(2)]
    psY = [ph.ps(f"psY{i}", [128, 512]) for i in range(2)]
    psG = [ph.ps(f"psG{i}", [128, 512]) for i in range(2)]
    P.add("sp", lambda e: e.dma_start(out=ttl[:].rearrange("p g q -> p (g q)"), in_=c.ttb),
          writes=["ttl"], dma=True)
    for a in range(2):
        P.add("sp", lambda e, a=a: e.dma_start(out=yp[a][:].rearrange("p r g q -> p (r g q)"), in_=c.ypb[a]),
              writes=["yp"], dma=True)
    if os.environ.get("KRELOAD_E"):
        P.add("sp", lambda e: e.dma_start(out=E[:], in_=c.eb), writes=["E"], dma=True)
    gTv = K.gT[:].rearrange("p j (n s) -> p j s n", s=8)
    for j in range(4):
        gs = gst[j % 2]
        for glo in range(8):
            g = 8 * j + glo
            gp, g2 = g // 2, g % 2
            rows = slice(g2 * 64, (g2 + 1) * 64)
            bank = psY[g % 2]
            P.add("pe", lambda e, bank=bank, g=g: e.matmul(
                bank[:, 0:256], ttl[:, g, :], K.Ubar[:, g, :], start=True, stop=False),
                reads=["ttl"], writes=[("psY", g % 2)])
            for ri in range(2):
                P.add("pe", lambda e, bank=bank, g2=g2, ri=ri, gp=gp: e.matmul(
                    bank[:, 0:256], yp[g2][:, ri, gp, :], K.Sbf[:, ri, gp, 0:256],
                    start=False, stop=(ri == 1)),
                    reads=["yp"], writes=[("psY", g % 2)])
            P.add("act", lambda e, gs=gs, glo=glo, bank=bank: e.activation(
                gs[:, glo, :], bank[:, 0:256], AF.Gelu_apprx_tanh),
                writes=[("psY", g % 2), ("gst", j % 2, glo)])
        for tau in range(8):
            bank = psG[tau % 2]
            for glo in range(8):
                idx = tau * 8 + glo
                P.add("pe", lambda e, bank=bank, idx=idx, gs=gs, glo=glo: e.matmul(
                    bank[:, 0:256], E[:, idx * 128:(idx + 1) * 128], gs[:, glo, :],
                    start=(glo == 0), stop=(glo == 7)),
                    reads=["E", ("gst", j % 2, glo)], writes=[("psG", tau % 2)])
            dst = gTv[:, j, tau, :]
            if tau % 2 == 0:
                P.add("act", lambda e, dst=dst, bank=bank: e.activation(dst, bank[:, 0:256], AF.Copy),
                      writes=[("psG", tau % 2), ("gT", j)])
            else:
                P.add("dve", lambda e, dst=dst, bank=bank: e.tensor_copy(dst, bank[:, 0:256]),
                      writes=[("psG", tau % 2), ("gT", j)])


def emit_mixout(ph, c, K):
    P = ph.P
    K.gT = ph.sb("gT", [128, 4, T], BF16, at=M_GT)
    K.oTa = ph.sb("oTa", [128, 4, T], BF16, at=M_OTA)
    ph.cur = COMMON_END + 16384
    wgl = ph.sb("wgl", [128, 4, 4 * 128], BF16)
    NW = 8
    K.sq = ph.sb("sq", [128, 8, TT], BF16)
    osT = ph.sb("osT", [128, 4, TT], F32)
    onT = ph.sb("onT", [128, 4, TT], BF16)
    K.rt = ph.sb("rt", [128, TT], F32)
    K.rstd = ph.sb("rstd", [128, TT], F32)
    K.rstd2 = ph.sb("rstd2", [128, TT], F32)
    sig = [ph.sb(f"sig{i}", [128, TT], F32) for i in range(2)]
    sqA = ph.sb("sqA", [128, 4, TT], BF16)
    rtA = ph.sb("rtA", [128, TT], F32)
    assert ph.cur <= M_OTA
    ph.cur = M_TMP
    K.yT = ph.sb("yT", [128, 8, TT], F32)
    tmp = [ph.sb(f"tmp{i}", [128, TT], F32) for i in range(8)]
    wo = [ph.sb(f"wo{i}", [128, 8 * 128], BF16) for i in range(NW)]
    psA = [ph.ps(f"psA{i}", [128, 512]) for i in range(2)]
    psC = [ph.ps(f"psC{i}", [128, 512]) for i in range(4)]
    K.ps_stat = ph.ps("ps_stat", [128, 512])
    for m in range(4):
        P.add("sp", lambda e, m=m: e.dma_start(out=wgl[:, m, :], in_=c.wglub[m]), writes=[("wgl", m)], dma=True)
    st = {"wl": 0}

    def glu_a(tt):
        ts = slice(tt * TT, (tt + 1) * TT)
        for m in range(4):
            pa = psA[m % 2]
            for cc in range(4):
                P.add("pe", lambda e, pa=pa, m=m, cc=cc: e.matmul(
                    pa[:], wgl[:, m, cc * 128:(cc + 1) * 128], K.gT[:, cc, ts], start=(cc == 0), stop=(cc == 3)),
                    reads=[("wgl", m)], writes=[("psA", m % 2)])
            sg = sig[m % 2]
            P.add("act", lambda e, sg=sg, pa=pa, m=m: e.activation(
                sg[:], pa[:], AF.Sigmoid, bias=K.vecs[:, m:m + 1]),
                writes=[("psA", m % 2), ("sig", m % 2)])
            P.add("dve", lambda e, sg=sg, m=m: e.tensor_tensor(osT[:, m, :], K.gT[:, m, ts], sg[:], ALU.mult),
                  reads=[("sig", m % 2)], writes=[("osT", m)])
            P.add("act", lambda e, m=m: e.activation(sqA[:, m, :], osT[:, m, :], AF.Square),
                  reads=[("osT", m)], writes=[("sqA", m)])

    def glu_b(tt):
        for cc in range(4):
            P.add("pe", lambda e, cc=cc: e.matmul(K.ps_stat[:], K.ones[:], sqA[:, cc, :], start=(cc == 0), stop=(cc == 3)),
                  reads=[("sqA", cc), "ones"], writes=["ps_stat"])
        P.add("act", lambda e: e.activation(rtA[:], K.ps_stat[:], AF.Sqrt, bias=K.epsb[:], scale=1.0 / 512),
              reads=["epsb"], writes=["ps_stat", "rtA"])
        P.add("dve", lambda e: e.reciprocal(K.rstd[:], rtA[:]), reads=["rtA"], writes=["rstd"])
        for m in range(4):
            P.add("dve", lambda e, m=m: e.scalar_tensor_tensor(
                out=onT[:, m, :], in0=osT[:, m, :], scalar=K.vecs[:, 4 + m:5 + m], in1=K.rstd[:],
                op0=ALU.mult, op1=ALU.mult),
                reads=[("osT", m), "rstd"], writes=[("onT", m)])

    def outproj(tt):
        ts = slice(tt * TT, (tt + 1) * TT)
        for m in range(8):
            slot = st["wl"] % NW
            st["wl"] += 1
            wt = wo[slot]
            P.add("sp", lambda e, wt=wt, m=m: e.dma_start(out=wt[:], in_=c.woutb[m]),
                  writes=[("wo", slot)], dma=True)
            py = psC[m % 4]
            for cc in range(8):
                rhs = K.oTa[:, cc, ts] if cc < 4 else onT[:, cc - 4, :]
                P.add("pe", lambda e, wt=wt, cc=cc, py=py, rhs=rhs: e.matmul(
                    py[:], wt[:, cc * 128:(cc + 1) * 128], rhs, start=(cc == 0), stop=(cc == 7)),
                    reads=[("wo", slot)] + ([("onT", cc - 4)] if cc >= 4 else []), writes=[("psC", m % 4)])
            P.add("act", lambda e, m=m, py=py: e.activation(K.yT[:, m, :], py[:], AF.Copy),
                  reads=[("psC", m % 4)], writes=[("yT", m)])
            P.add("act", lambda e, m=m: e.activation(K.sq[:, m, :], K.yT[:, m, :], AF.Square),
                  reads=[("yT", m)], writes=[("sq", m)])

    def post_stats(tt):
        emit_rstd(P, K, K.ps_stat, "sq", D, K.rstd2, "rstd2")

    def post_apply(tt):
        ts = slice(tt * TT, (tt + 1) * TT)
        for m in range(8):
            tm_ = tmp[m]
            P.add("dve", lambda e, m=m, tm_=tm_: e.scalar_tensor_tensor(
                out=tm_[:], in0=K.yT[:, m, :], scalar=K.gains[:, G_MIX_POST, m:m + 1],
                in1=K.rstd2[:], op0=ALU.mult, op1=ALU.mult),
                reads=[("yT", m), "rstd2"], writes=[("tmp", m)])
            P.add("pool", lambda e, m=m, tm_=tm_: e.tensor_tensor(
                K.XT[:, m, ts], K.XT[:, m, ts], tm_[:], ALU.add),
                reads=[("tmp", m)], writes=[("XT", tt, m)])

    glu_a(0)
    glu_b(0)
    if NTT > 1:
        glu_a(1)
    for tt in range(NTT):
        outproj(tt)
        if tt + 1 < NTT:
            glu_b(tt + 1)
        if tt + 2 < NTT:
            glu_a(tt + 2)
        post_stats(tt)
        post_apply(tt)


def alloc_ffn_phase(ph, c, K):
    alloc_common(ph, c, K)
    alloc_ffn(ph, K)


def build_nc(stage="full"):
    nc = bass.Bass("TRN2", target_bir_lowering=False)
    debug = stage not in ("full",)
    c = declare_dram(nc, debug=debug)
    first = True
    ges = ExitStack()
    G = Globals(nc, ges)
    full_like = stage == "full"
    if full_like:
        with Phase(nc, G, "start") as ph:
            K = Ctx()
            alloc_common(ph, c, K)
            K.xio = [ph.sb(f"xio{i}", [128, D], F32) for i in range(2)]
            K.psA = [ph.ps(f"psA{i}", [128, 512]) for i in range(2)]
            emit_consts(ph, c, K)
            gen = load_x_blocks(ph, c, K, 0, act_only=True)
            next(gen, None)
            emit_s5_setup(ph, c, K, hook=lambda: next(gen, None))
            for _ in gen:
                pass
            ph.finish()
    for seq in range(NSEQ):
        with Phase(nc, G, f"f1s{seq}") as ph:
            K = Ctx()
            alloc_ffn_phase(ph, c, K)
            if first and not full_like:
                emit_consts(ph, c, K)
                if stage in ("att", "mix"):
                    emit_cast_mixer_weights(ph, c)
                elif stage != "io":
                    emit_cast_weights(ph, c, 0)
                    emit_cast_mixer_weights(ph, c)
                    emit_cast_weights(ph, c, 1)
            xg = None
            if not (full_like and seq == 0):
                if full_like:
                    xg = load_x_blocks(ph, c, K, seq)
                    for _ in range(4):
                        next(xg, None)
                else:
                    emit_load_x(ph, c, K, seq)
            if stage not in ("io", "iocast", "att", "mix"):
                emit_ffn(ph, c, K, 0, G_FF1_PRE, G_FF1_POST, direct_cast=(full_like and seq == 0), x_gen=xg)
            if stage in ("ffn1", "io", "iocast"):
                emit_store_x(ph, c, K, seq)
            ph.finish()
        first = False
        if stage in ("ffn1", "io", "iocast"):
            continue
        KMIX = int(os.environ.get("KMIX", "9"))
        if seq == 0 and stage != "att" and KMIX >= 1 and not full_like:
            with Phase(nc, G, "s5set") as ph:
                K = Ctx()
                alloc_common(ph, c, K)
                emit_s5_setup(ph, c, K)
                ph.finish()
        with Phase(nc, G, f"pjs{seq}") as ph:
            K = Ctx()
            alloc_common(ph, c, K)
            if full_like and seq == 0:
                emit_cast_mixer_weights(ph, c, part="out")
            emit_proj(ph, c, K)
            ph.finish()
        with Phase(nc, G, f"ats{seq}") as ph:
            K = Ctx()
            alloc_common(ph, c, K)
            emit_attn(ph, c, K)
            if stage == "att":
                ph.P.add("sp", lambda e, K=K, seq=seq: e.dma_start(
                    out=c.dbg[seq], in_=K.oTa[:].rearrange("p h t -> p (h t)")),
                    reads=[("oTa", tb) for tb in range(16)], writes=["dbg"], dma=True)
            ph.finish()
        if stage == "att":
            continue
        if KMIX >= 2:
          with Phase(nc, G, f"sas{seq}") as ph:
            K = Ctx()
            alloc_common(ph, c, K)
            if full_like and seq == 0:
                emit_cast_weights(ph, c, 1, part="gu")
            emit_s5a(ph, c, K)
            if os.environ.get("KDUMP") == "Ubar":
                ph.P.add("sp", lambda e, K=K, seq=seq: e.dma_start(
                    out=c.dbg[seq], in_=K.Ubar[:].rearrange("p g n -> p (g n)")),
                    reads=[("Ubar", g) for g in range(32)], writes=["dbg"], dma=True)
            if os.environ.get("KDUMP") == "Sbf":
                ph.P.add("sp", lambda e, K=K, seq=seq: e.dma_start(
                    out=c.dbg[seq].rearrange("p (a n) -> p a n", n=256),
                    in_=K.Sbf[:, :, :, 1:257].rearrange("p r g n -> p (r g) n")),
                    reads=[("Sbf", 0), ("Sbf", 1)], writes=["dbg"], dma=True)
            ph.finish()
        if KMIX >= 3:
          with Phase(nc, G, f"sbs{seq}") as ph:
            K = Ctx()
            alloc_common(ph, c, K)
            if full_like and seq == 0:
                emit_cast_weights(ph, c, 1, part="d")
            emit_s5b(ph, c, K)
            if os.environ.get("KDUMP") == "gT":
                ph.P.add("sp", lambda e, K=K, seq=seq: e.dma_start(
                    out=c.dbg[seq], in_=K.gT[:].rearrange("p j t -> p (j t)")),
                    reads=[("gT", j) for j in range(4)], writes=["dbg"], dma=True)
            ph.finish()
        if KMIX >= 4:
          with Phase(nc, G, f"mos{seq}") as ph:
            K = Ctx()
            alloc_common(ph, c, K)
            emit_mixout(ph, c, K)
            ph.finish()
        with Phase(nc, G, f"f2s{seq}") as ph:
            K = Ctx()
            alloc_ffn_phase(ph, c, K)
            if stage != "mix":
                emit_ffn(ph, c, K, 1, G_FF2_PRE, G_FF2_POST)
            emit_store_x(ph, c, K, seq)
            ph.finish()
    ges.close()
    return nc


def host_layout(inp):
    f = lambda a: np.ascontiguousarray(np.asarray(a, dtype=np.float32))
    com = {}
    com["ident"] = np.eye(128, dtype=np.float32)
    kk = np.arange(128)[:, None]
    qq = np.arange(128)[None, :]
    com["maskneg"] = np.where(kk <= qq, 0.0, -30000.0).astype(np.float32)
    gl = [inp["ff1_pre_g"], inp["ff1_post_g"], inp["mix_pre_g"], inp["mix_post_g"],
          inp["ff2_pre_g"], inp["ff2_post_g"]]
    gains = np.zeros((128, 8, 8), np.float32)
    for i, g in enumerate(gl):
        gains[:, i, :] = f(g).reshape(8, 128).T
    com["gains"] = gains
    lamv = np.stack([f(inp[k])[0] for k in ("lambda_q1", "lambda_k1", "lambda_q2", "lambda_k2")], 0)
    com["lamv"] = np.ascontiguousarray(np.broadcast_to(lamv[None], (128, 4, 64)))
    com["gsub"] = np.ascontiguousarray(np.broadcast_to(f(inp["attn_subln_g"])[0][None], (128, 128)))
    vecs = np.zeros((128, 8), np.float32)
    vecs[:, 0:4] = f(inp["ssm_b_glu"])[0].reshape(4, 128).T
    vecs[:, 4:8] = f(inp["ssm_norm_g"])[0].reshape(4, 128).T
    com["vecs"] = vecs
    for i, pre in enumerate(("ff1", "ff2")):
        wg = f(inp[pre + "_w_gate"])[0]
        wu = f(inp[pre + "_w_up"])[0]
        wd = f(inp[pre + "_w_down"])[0]
        g4 = wg.reshape(8, 128, NF, 128).transpose(2, 1, 0, 3)
        u4 = wu.reshape(8, 128, NF, 128).transpose(2, 1, 0, 3)
        com[f"wgu{i}"] = np.ascontiguousarray(np.stack([g4, u4], axis=2)).reshape(NF, 128, 2 * 8 * 128)
        d4 = wd.reshape(NF, 128, 8, 128).transpose(2, 1, 0, 3)
        com[f"wd{i}"] = np.ascontiguousarray(d4).reshape(8, 128, NF * 128)
    win = f(inp["w_in"])[0]
    w4 = win.reshape(8, 128, 16, 128).transpose(2, 1, 0, 3)
    sel = list(range(8)) + list(range(12, 16))
    com["winf"] = np.ascontiguousarray(w4[sel]).reshape(12, 128, 8 * 128)
    wv = win[:, 1024:1536].reshape(8, 128, 512).transpose(1, 0, 2)
    com["winv"] = np.ascontiguousarray(wv).reshape(128, 8 * 512)
    wo = f(inp["w_out"])[0]
    o4 = wo.reshape(8, 128, 8, 128).transpose(2, 1, 0, 3)
    com["wout"] = np.ascontiguousarray(o4).reshape(8, 128, 8 * 128)
    wgl = f(inp["ssm_w_glu"])[0]
    l4 = wgl.reshape(4, 128, 4, 128).transpose(2, 1, 0, 3)
    com["wglu"] = np.ascontiguousarray(l4).reshape(4, 128, 4 * 128)
    a_re, a_im = f(inp["ssm_a_re"])[0], f(inp["ssm_a_im"])[0]
    ldt = f(inp["ssm_log_dt"])[0]
    b_re, b_im = f(inp["ssm_b_re"])[0], f(inp["ssm_b_im"])[0]
    c_re, c_im = f(inp["ssm_c_re"])[0], f(inp["ssm_c_im"])[0]
    dsk = f(inp["ssm_d"])[0]
    s5p = np.zeros((128, S5P_COLS), np.float32)
    lay_a = lambda a: a.reshape(16, 2, 64).transpose(1, 2, 0).reshape(128, 16)
    s5p[:, 0:16] = lay_a(a_re)
    s5p[:, 16:32] = lay_a(a_im)
    s5p[:, 32:48] = np.broadcast_to(ldt.reshape(16, 2).T[:, None, :], (2, 64, 16)).reshape(128, 16)
    lay_b = lambda b: b.reshape(16, 2, 64, 16).transpose(1, 2, 0, 3).reshape(128, 256)
    lay_c = lambda cc: cc.reshape(16, 2, 16, 64).transpose(1, 3, 0, 2).reshape(128, 256)
    s5p[:, 48:304] = lay_b(b_re)
    s5p[:, 304:560] = lay_b(b_im)
    s5p[:, 560:816] = lay_c(c_re)
    s5p[:, 816:1072] = lay_c(c_im)
    kk = np.stack([7.0 - np.arange(8), np.arange(8) - 7.0, np.arange(8) + 1.0]).astype(np.float32)
    s5p[:, 1072:1096] = kk.reshape(1, 24)
    s5p[:, 1096:1128] = np.broadcast_to(dsk.reshape(32, 16).T[None], (8, 16, 32)).reshape(128, 32)
    com["s5p"] = s5p
    sg = np.arange(128) // 16
    com["cmask"] = (sg[None, :] >= sg[:, None]).astype(np.float32)
    r = np.arange(128)
    esel = np.zeros((128, 64, 128), np.float32)
    for glo in range(8):
        for sgm in range(8):
            m = ((r[:, None] // 16 == glo) & (r[:, None] % 16 == r[None, :] % 16) & (r[None, :] // 16 == sgm))
            esel[:, glo * 8 + sgm, :] = m
    com["esel"] = esel.reshape(128, 64 * 128)
    return com


_NC_CACHE = {}


def kernel(**inputs):
    x = np.ascontiguousarray(np.asarray(inputs["x"], dtype=np.float32))
    com = host_layout(inputs)
    if "full" not in _NC_CACHE:
        _NC_CACHE["full"] = build_nc("full")
    nc = _NC_CACHE["full"]
    in_maps = []
    for i in range(NCORES):
        m = dict(com)
        m["x"] = x[i * NSEQ:(i + 1) * NSEQ]
        in_maps.append(m)
    res = run_bass_kernel_spmd(nc, in_maps, core_ids=list(range(NCORES)))
    out = np.concatenate([np.asarray(r["out"]) for r in res.results], axis=0)
    return out.astype(np.float32)
```
